# Optimizing a Trainium2 kernel written in Bass

```python
import math
import jax, jax.numpy as jnp
from jax import lax
import numpy as np

D_MODEL = 1024
BATCH = 8
SEQ = 4096
DEPTH = 4

D_CONV = D_MODEL // 2
CONV_WIDTH = 3
D_SSM = D_MODEL // 2
SSM_GROUP = 16
N_SSM_GROUPS = D_SSM // SSM_GROUP
SSM_STATE = 64
D_POOL = D_MODEL // 2
POOL_WINDOWS = (2, 4, 8, 16)
POOL_GROUP = D_POOL // len(POOL_WINDOWS)
D_SGU = D_MODEL // 2
SGU_HEADS = 4
SGU_HEAD_DIM = D_SGU // SGU_HEADS
CHUNK = 128
D_FF = ((8 * D_MODEL // 3 + 127) // 128) * 128
N_EVEN = (DEPTH + 1) // 2
N_ODD = DEPTH // 2
EPS = 1e-6

kernel_name = "hybrid_conv_s5_pool_sgu_trunk"


def rmsnorm(x, g):
    xf = x.astype(jnp.float32)
    y = xf * lax.rsqrt(jnp.mean(xf * xf, axis=-1, keepdims=True) + EPS)
    return (y * g.astype(jnp.float32)).astype(x.dtype)


def causal_dwconv(x, w):
    L = x.shape[1]
    K = w.shape[0]
    xp = jnp.pad(x, ((0, 0), (K - 1, 0), (0, 0)))
    y = xp[:, 0:L] * w[0]
    for k in range(1, K):
        y = y + xp[:, k:k + L] * w[k]
    return y


def short_conv_mixer(xa, ba, ca, conv_w):
    return ba * causal_dwconv(ca * xa, conv_w)


def s5_mixer(u, log_step, a_re, a_im, b_re, b_im, c_re, c_im, d_skip, glu_w, glu_b):
    f32 = jnp.float32
    Bsz, L, _ = u.shape
    uf = u.astype(f32).reshape(Bsz, L, N_SSM_GROUPS, SSM_GROUP)
    lam = lax.complex(a_re.astype(f32), a_im.astype(f32))
    step = jnp.exp(log_step.astype(f32))[:, None]
    lam_bar = jnp.exp(lam * step)
    b_tilde = lax.complex(b_re.astype(f32), b_im.astype(f32))
    b_bar = ((lam_bar - 1.0) / lam)[..., None] * b_tilde
    bu = jnp.einsum('gph,blgh->blgp', b_bar, uf.astype(jnp.complex64))
    a_elems = jnp.broadcast_to(lam_bar, bu.shape)

    def combine(left, right):
        a_l, b_l = left
        a_r, b_r = right
        return a_r * a_l, a_r * b_l + b_r

    _, states = lax.associative_scan(combine, (a_elems, bu), axis=1)
    c_tilde = lax.complex(c_re.astype(f32), c_im.astype(f32))
    y = jnp.real(jnp.einsum('ghp,blgp->blgh', c_tilde, states))
    y = y + d_skip.astype(f32).reshape(N_SSM_GROUPS, SSM_GROUP) * uf
    y = jax.nn.gelu(y.reshape(Bsz, L, D_SSM))
    y = y * jax.nn.sigmoid(y @ glu_w.astype(f32) + glu_b.astype(f32))
    return y.astype(u.dtype)


def pool_mixer(z, pool_w, pool_scale):
    f32 = jnp.float32
    Bsz, L, _ = z.shape
    zf = z.astype(f32).reshape(Bsz, L, len(POOL_WINDOWS), POOL_GROUP)
    csum = lax.cumsum(zf, axis=1)
    count = jnp.arange(1, L + 1, dtype=f32)[None, :, None]
    outs = []
    for g, w in enumerate(POOL_WINDOWS):
        s = csum[:, :, g]
        lower = jnp.pad(s, ((0, 0), (w, 0), (0, 0)))[:, :L]
        mean = (s - lower) / jnp.minimum(count, w)
        outs.append(mean - zf[:, :, g])
    pooled = jnp.stack(outs, axis=2)
    y = jnp.einsum('blgc,gcd->blgd', pooled, pool_w.astype(f32)).reshape(Bsz, L, D_POOL)
    return (y * pool_scale.astype(f32)).astype(z.dtype)


def sgu_mixer(su, sv, norm_g, sgu_w, sgu_b):
    Bsz, L, _ = su.shape
    v = rmsnorm(sv, norm_g)
    vr = v.reshape(Bsz, L // CHUNK, CHUNK, SGU_HEADS, SGU_HEAD_DIM)
    mask = jnp.tril(jnp.ones((CHUNK, CHUNK), dtype=bool))
    w_s = jnp.where(mask, sgu_w, 0)
    mixed = jnp.einsum('hts,bnshd->bnthd', w_s, vr) + jnp.swapaxes(sgu_b, 0, 1)[:, :, None]
    return su * mixed.reshape(Bsz, L, D_SGU)


def conv_ffn(h, w_up, conv_w, conv_b, w_down):
    up = causal_dwconv(h @ w_up, conv_w) + conv_b
    g, v = jnp.split(up, 2, axis=-1)
    return (jax.nn.silu(g) * v) @ w_down


def setup_inputs(seed: int = 0) -> dict:
    key = jax.random.key(seed)
    ks = jax.random.split(key, 32)
    f32 = jnp.float32

    def nrm(k, shape, scale):
        return jax.random.normal(k, shape, f32) * scale

    G, P, Hg = N_SSM_GROUPS, SSM_STATE, SSM_GROUP
    d_even_in = 3 * D_CONV + D_SSM
    d_odd_in = D_POOL + 2 * D_SGU
    a_im_base = jnp.pi * jnp.arange(P, dtype=f32)
    return {
        "x": nrm(ks[0], (BATCH, SEQ, D_MODEL), 1.0),
        "norm_mix_g": 1.0 + nrm(ks[1], (DEPTH, D_MODEL), 0.05),
        "even_w_in": nrm(ks[2], (N_EVEN, D_MODEL, d_even_in), D_MODEL ** -0.5),
        "even_conv_w": nrm(ks[3], (N_EVEN, CONV_WIDTH, D_CONV), CONV_WIDTH ** -0.5),
        "ssm_log_step": jax.random.uniform(ks[4], (N_EVEN, G), f32, math.log(1e-3), math.log(1e-1)),
        "ssm_a_re": -0.5 * (1.0 + nrm(ks[5], (N_EVEN, G, P), 0.01)),
        "ssm_a_im": a_im_base + nrm(ks[6], (N_EVEN, G, P), 0.01),
        "ssm_b_re": nrm(ks[7], (N_EVEN, G, P, Hg), (2 * Hg) ** -0.5),
        "ssm_b_im": nrm(ks[8], (N_EVEN, G, P, Hg), (2 * Hg) ** -0.5),
        "ssm_c_re": nrm(ks[9], (N_EVEN, G, Hg, P), (2 * P) ** -0.5),
        "ssm_c_im": nrm(ks[10], (N_EVEN, G, Hg, P), (2 * P) ** -0.5),
        "ssm_d": nrm(ks[11], (N_EVEN, D_SSM), 1.0),
        "ssm_glu_w": nrm(ks[12], (N_EVEN, D_SSM, D_SSM), D_SSM ** -0.5),
        "ssm_glu_b": nrm(ks[13], (N_EVEN, D_SSM), 0.02),
        "even_w_out": nrm(ks[14], (N_EVEN, D_CONV + D_SSM, D_MODEL), (D_CONV + D_SSM) ** -0.5),
        "odd_w_in": nrm(ks[15], (N_ODD, D_MODEL, d_odd_in), D_MODEL ** -0.5),
        "pool_w": nrm(ks[16], (N_ODD, len(POOL_WINDOWS), POOL_GROUP, POOL_GROUP), POOL_GROUP ** -0.5),
        "pool_scale": 1.0 + nrm(ks[17], (N_ODD, D_POOL), 0.1),
        "sgu_norm_g": 1.0 + nrm(ks[18], (N_ODD, D_SGU), 0.05),
        "sgu_w": nrm(ks[19], (N_ODD, SGU_HEADS, CHUNK, CHUNK), CHUNK ** -0.5),
        "sgu_b": 1.0 + nrm(ks[20], (N_ODD, SGU_HEADS, CHUNK), 0.1),
        "odd_w_out": nrm(ks[21], (N_ODD, D_POOL + D_SGU, D_MODEL), (D_POOL + D_SGU) ** -0.5),
        "norm_ffn_g": 1.0 + nrm(ks[22], (DEPTH, D_MODEL), 0.05),
        "ffn_w_up": nrm(ks[23], (DEPTH, D_MODEL, 2 * D_FF), D_MODEL ** -0.5),
        "ffn_conv_w": nrm(ks[24], (DEPTH, CONV_WIDTH, 2 * D_FF), CONV_WIDTH ** -0.5),
        "ffn_conv_b": nrm(ks[25], (DEPTH, 2 * D_FF), 0.02),
        "ffn_w_down": nrm(ks[26], (DEPTH, D_FF, D_MODEL), D_FF ** -0.5),
        "norm_final_g": 1.0 + nrm(ks[27], (D_MODEL,), 0.05),
    }


def reference(x, norm_mix_g, even_w_in, even_conv_w, ssm_log_step, ssm_a_re, ssm_a_im,
              ssm_b_re, ssm_b_im, ssm_c_re, ssm_c_im, ssm_d, ssm_glu_w, ssm_glu_b,
              even_w_out, odd_w_in, pool_w, pool_scale, sgu_norm_g, sgu_w, sgu_b,
              odd_w_out, norm_ffn_g, ffn_w_up, ffn_conv_w, ffn_conv_b, ffn_w_down,
              norm_final_g):
    for i in range(DEPTH):
        h = rmsnorm(x, norm_mix_g[i])
        j = i // 2
        if i % 2 == 0:
            proj = h @ even_w_in[j]
            xa = proj[..., :D_CONV]
            ba = proj[..., D_CONV:2 * D_CONV]
            ca = proj[..., 2 * D_CONV:3 * D_CONV]
            u = proj[..., 3 * D_CONV:]
            ya = short_conv_mixer(xa, ba, ca, even_conv_w[j])
            yb = s5_mixer(u, ssm_log_step[j], ssm_a_re[j], ssm_a_im[j], ssm_b_re[j], ssm_b_im[j],
                          ssm_c_re[j], ssm_c_im[j], ssm_d[j], ssm_glu_w[j], ssm_glu_b[j])
            mix = jnp.concatenate([ya, yb], axis=-1) @ even_w_out[j]
        else:
            proj = h @ odd_w_in[j]
            z = proj[..., :D_POOL]
            uv = jax.nn.gelu(proj[..., D_POOL:])
            su = uv[..., :D_SGU]
            sv = uv[..., D_SGU:]
            yc = pool_mixer(z, pool_w[j], pool_scale[j])
            yd = sgu_mixer(su, sv, sgu_norm_g[j], sgu_w[j], sgu_b[j])
            mix = jnp.concatenate([yc, yd], axis=-1) @ odd_w_out[j]
        x = x + mix
        x = x + conv_ffn(rmsnorm(x, norm_ffn_g[i]), ffn_w_up[i], ffn_conv_w[i], ffn_conv_b[i], ffn_w_down[i])
    return rmsnorm(x, norm_final_g)
```

```python
import contextlib
import numpy as np
import concourse.bass as bass
import concourse.mybir as mybir
from concourse.bass_utils import run_bass_kernel_spmd

F32 = mybir.dt.float32
BF16 = mybir.dt.bfloat16
ALU = mybir.AluOpType
AF = mybir.ActivationFunctionType

D = 1024
SEQ = 4096
DEPTH = 4
DFF = 2816
T = 512
TS = 256
NSLOT = 4
EPS = 1e-6
ENGS = ("pe", "act", "dve", "pool", "sp")


class Op:
    __slots__ = ("eng", "fn", "deps", "signal", "sigval", "sem", "is_dma", "group")

    def __init__(self, eng, fn):
        self.eng = eng
        self.fn = fn
        self.deps = []
        self.signal = False
        self.sigval = 0
        self.sem = None
        self.is_dma = False
        self.group = None


class Prog:
    def __init__(self, nc):
        self.nc = nc
        self.ops = {e: [] for e in ENGS}
        self.state = {}
        self.stack = contextlib.ExitStack()
        self.dma_groups = {}
        self.n = 0

    def sb(self, shape, dtype=F32, name=None):
        self.n += 1
        return self.stack.enter_context(self.nc.sbuf_tensor(name or f"sb{self.n}", list(shape), dtype))

    def ps(self, shape, dtype=F32, name=None):
        self.n += 1
        return self.stack.enter_context(self.nc.psum_tensor(name or f"ps{self.n}", list(shape), dtype))

    @staticmethod
    def _norm(k):
        if isinstance(k, tuple) and len(k) == 3 and isinstance(k[0], str) and k[0].startswith("@"):
            return k[0], int(k[1]), int(k[2])
        return k, 0, 1

    def _track(self, o, reads, writes, skip_same_eng=False):
        deps = []
        rn = [self._norm(k) for k in reads]
        wn = [self._norm(k) for k in writes]
        for ns, lo, hi in rn:
            for (a, b), st in self.state.get(ns, {}).items():
                if a < hi and lo < b and st[0] is not None:
                    deps.append(st[0])
        for ns, lo, hi in wn:
            for (a, b), st in self.state.get(ns, {}).items():
                if a < hi and lo < b:
                    if st[0] is not None:
                        deps.append(st[0])
                    last = {}
                    for r in st[1]:
                        if r.is_dma:
                            deps.append(r)
                        else:
                            last[r.eng] = r
                    deps.extend(last.values())
        for ns, lo, hi in rn:
            st = self.state.setdefault(ns, {}).setdefault((lo, hi), [None, []])
            st[1].append(o)
        for ns, lo, hi in wn:
            d = self.state.setdefault(ns, {})
            for (a, b), st in d.items():
                if a < hi and lo < b:
                    st[0] = o
                    st[1] = []
            d[(lo, hi)] = [o, []]
        seen = set()
        for d in deps:
            if d is o or id(d) in seen:
                continue
            if skip_same_eng and (not d.is_dma) and d.eng == o.eng:
                continue
            seen.add(id(d))
            o.deps.append(d)

    def op(self, eng, fn, reads=(), writes=()):
        o = Op(eng, fn)
        self._track(o, reads, writes, skip_same_eng=(eng == "pe"))
        self.ops[eng].append(o)
        return o

    def dma(self, eng, out, in_, group, reads=(), writes=(), **kw):
        o = Op(eng, lambda e: e.dma_start(out=out, in_=in_, **kw))
        o.is_dma = True
        o.group = group
        self._track(o, reads, writes)
        lst = self.dma_groups.setdefault(group, [])
        if lst and not group.startswith("all:") and lst[-1] not in o.deps:
            o.deps.append(lst[-1])
        self.ops[eng].append(o)
        lst.append(o)
        return o

    def wait(self, eng, reads):
        o = Op(eng, lambda e: None)
        self._track(o, reads, ())
        self.ops[eng].append(o)
        return o

    def emit(self):
        nc = self.nc
        for e in ENGS:
            for o in self.ops[e]:
                for d in o.deps:
                    d.signal = True
        sems = {}
        for e in ENGS:
            sems[e] = self.stack.enter_context(nc.semaphore(f"s_{e}"))
            c = 0
            for o in self.ops[e]:
                if o.is_dma:
                    continue
                if o.signal:
                    c += 1
                    o.sigval = c
                    o.sem = sems[e]
        for g, lst in self.dma_groups.items():
            s = self.stack.enter_context(nc.semaphore(f"d_{len(sems)}"))
            sems["dma:" + g] = s
            if g.startswith("all:"):
                for o in lst:
                    o.sem = s
                    o.sigval = 16 * len(lst)
            else:
                for i, o in enumerate(lst):
                    o.sem = s
                    o.sigval = 16 * (i + 1)
        engmap = {"pe": "tensor", "act": "scalar", "dve": "vector", "pool": "gpsimd", "sp": "sync"}
        with nc.Block() as block:
            for e in ENGS:
                ops = self.ops[e]
                if not ops:
                    continue

                def body(eng, ops=ops):
                    waited = {}
                    for o in ops:
                        need = {}
                        for d in o.deps:
                            k = id(d.sem)
                            if d.sigval > need.get(k, (None, 0))[1]:
                                need[k] = (d.sem, d.sigval)
                        for k, (s, v) in need.items():
                            if waited.get(k, 0) >= v:
                                continue
                            eng.wait_ge(s, v)
                            waited[k] = v
                        inst = o.fn(eng)
                        if inst is None:
                            continue
                        if o.is_dma:
                            inst.then_inc(o.sem, 16)
                        elif o.signal:
                            inst.then_inc(o.sem, 1)

                getattr(block, engmap[e])(body)

    def close(self):
        self.stack.close()


def _pair(a):
    a = np.asarray(a, dtype=np.float32)
    rest = a.shape[2:]
    a = a.reshape((16, 2, 64) + rest)
    perm = (1, 2, 0) + tuple(range(3, 3 + len(rest)))
    return np.ascontiguousarray(a.transpose(perm).reshape((128, 16) + rest))


def _cols(v, n):
    return np.ascontiguousarray(np.asarray(v, dtype=np.float32).reshape(n, 128).T)


def prep_common(inp):
    f = lambda a: np.ascontiguousarray(np.asarray(a, dtype=np.float32))
    com = {}
    com["gmix"] = f(np.asarray(inp["norm_mix_g"]).reshape(4, 8, 128).transpose(2, 0, 1))
    com["gffn"] = f(np.asarray(inp["norm_ffn_g"]).reshape(4, 8, 128).transpose(2, 0, 1))
    com["gfin"] = _cols(inp["norm_final_g"], 8)
    for j in range(2):
        com[f"ewin{j}"] = f(inp["even_w_in"][j])
        com[f"econv{j}"] = f(np.asarray(inp["even_conv_w"][j]).reshape(3, 4, 128).transpose(2, 1, 0))
        ls = np.broadcast_to(np.asarray(inp["ssm_log_step"][j])[:, None], (32, 64))
        com[f"sls{j}"] = _pair(ls)
        com[f"sare{j}"] = _pair(inp["ssm_a_re"][j])
        com[f"saim{j}"] = _pair(inp["ssm_a_im"][j])
        com[f"sbre{j}"] = _pair(inp["ssm_b_re"][j])
        com[f"sbim{j}"] = _pair(inp["ssm_b_im"][j])
        com[f"scre{j}"] = _pair(np.asarray(inp["ssm_c_re"][j]).transpose(0, 2, 1))
        com[f"scim{j}"] = _pair(np.asarray(inp["ssm_c_im"][j]).transpose(0, 2, 1))
        com[f"sd{j}"] = _cols(inp["ssm_d"][j], 4)
        com[f"glw{j}"] = f(inp["ssm_glu_w"][j])
        com[f"glb{j}"] = _cols(inp["ssm_glu_b"][j], 4)
        com[f"ewout{j}"] = f(inp["even_w_out"][j])
        com[f"owin{j}"] = f(inp["odd_w_in"][j])
        com[f"poolw{j}"] = f(np.asarray(inp["pool_w"][j]).transpose(1, 0, 2))
        com[f"pools{j}"] = _cols(inp["pool_scale"][j], 4)
        com[f"sgng{j}"] = f(np.broadcast_to(np.asarray(inp["sgu_norm_g"][j])[None, :], (128, 512)))
        com[f"sgw{j}"] = f(np.asarray(inp["sgu_w"][j]).transpose(2, 0, 1))
        com[f"sgb{j}"] = f(np.broadcast_to(np.asarray(inp["sgu_b"][j])[None], (128, 4, 128)))
        com[f"owout{j}"] = f(inp["odd_w_out"][j])
    for i in range(4):
        com[f"fup{i}"] = f(inp["ffn_w_up"][i])
        com[f"fcw{i}"] = f(np.asarray(inp["ffn_conv_w"][i]).reshape(3, 44, 128).transpose(2, 1, 0))
        com[f"fcb{i}"] = _cols(inp["ffn_conv_b"][i], 44)
        com[f"fdn{i}"] = f(inp["ffn_w_down"][i])
    s = np.arange(128)
    com["trilh"] = f(0.5 * (s[:, None] <= s[None, :]))
    invc = np.zeros((128, 4, 16), np.float32)
    for g, w in enumerate((2, 4, 8, 16)):
        invc[:, g, :] = 1.0 / np.minimum(np.arange(1, 17), w)
    com["invc"] = invc
    com["ident"] = f(np.eye(128))
    return com


SHAPES = {
    "gmix": [128, 4, 8], "gffn": [128, 4, 8], "gfin": [128, 8],
    "trilh": [128, 128], "invc": [128, 4, 16], "ident": [128, 128],
}
for _j in range(2):
    SHAPES.update({
        f"ewin{_j}": [1024, 2048], f"econv{_j}": [128, 4, 3], f"sls{_j}": [128, 16],
        f"sare{_j}": [128, 16], f"saim{_j}": [128, 16], f"sbre{_j}": [128, 16, 16],
        f"sbim{_j}": [128, 16, 16], f"scre{_j}": [128, 16, 16], f"scim{_j}": [128, 16, 16],
        f"sd{_j}": [128, 4], f"glw{_j}": [512, 512], f"glb{_j}": [128, 4], f"ewout{_j}": [1024, 1024],
        f"owin{_j}": [1024, 1536], f"poolw{_j}": [128, 4, 128], f"pools{_j}": [128, 4],
        f"sgng{_j}": [128, 512], f"sgw{_j}": [128, 4, 128], f"sgb{_j}": [128, 4, 128],
        f"owout{_j}": [1024, 1024],
    })
for _i in range(4):
    SHAPES.update({f"fup{_i}": [1024, 5632], f"fcw{_i}": [128, 44, 3], f"fcb{_i}": [128, 44],
                   f"fdn{_i}": [2816, 1024]})


class V:
    __slots__ = ("ap", "reg")

    def __init__(self, ap, reg):
        self.ap = ap
        self.reg = reg


def build(layers=(0, 1, 2, 3), ntiles=8, final=True, dbg=()):
    nc = bass.Bass("TRN2", target_bir_lowering=False)
    P = Prog(nc)
    dr = {}
    dbg_out = {}

    def din(name):
        if name not in dr:
            dr[name] = nc.dram_tensor(name, SHAPES[name], F32, kind="ExternalInput").ap()
        return dr[name]

    xT = nc.dram_tensor("xT", [D, SEQ], F32, kind="ExternalInput").ap()
    outT = nc.dram_tensor("outT", [D, SEQ], F32, kind="ExternalOutput").ap()
    xTv = xT.rearrange("(kt p) t -> p kt t", p=128)
    outTv = outT.rearrange("(kt p) t -> p kt t", p=128)
    nlay = len(layers)
    scr = nc.dram_tensor("scr", [nlay * 26, 128, 4096], BF16, kind="Internal").ap()
    tabd = nc.dram_tensor("tabd", [2, 128, 2 * 16 * TS], F32, kind="Internal").ap()

    xres = P.sb([128, 8, T], F32, "xres")
    hb = P.sb([128, 8, T], BF16, "hb")
    ring = P.sb([128, NSLOT, 4096], BF16, "ring")
    stage = P.sb([128, 2, 2048], F32, "stage")
    tab = P.sb([128, 2, 16, TS], F32, "tab")
    AW = 17920
    arena = P.sb([128, AW], F32, "arena")
    ones = P.sb([128, 128], BF16, "ones")
    onesf = P.sb([128, 128], F32, "onesf")
    ident = P.sb([128, 128], F32, "ident_sb")
    cst = P.sb([128, 4], F32, "cst")
    gmix = P.sb([128, 4, 8], F32, "gmix_sb")
    gffn = P.sb([128, 4, 8], F32, "gffn_sb")
    gfin = P.sb([128, 8], F32, "gfin_sb")
    fcw = P.sb([128, 4, 44, 3], F32, "fcw_sb")
    fcb = P.sb([128, 4, 44], F32, "fcb_sb")
    fhist = P.sb([128, 4, 44, 2], F32, "fhist")
    cxhist = P.sb([128, 2, 4, 2], F32, "cxhist")
    zhist = P.sb([128, 2, 4, 16], F32, "zhist")
    qinit = P.sb([128, 2, 2, 16], F32, "qinit")
    qend = P.sb([128, 2, 16], F32, "qend")
    econv = P.sb([128, 2, 4, 3], F32, "econv_sb")
    Bl = P.sb([128, 2, 4, 2, 128], BF16, "Bl")
    Cl = P.sb([128, 2, 16, 3, 32], BF16, "Cl")
    rdec = P.sb([128, 2, 16], F32, "rdec")
    rotc = P.sb([128, 2, 2, 16], F32, "rotc")
    dq = P.sb([128, 2, 4], F32, "dq")
    gbh = P.sb([128, 2, 4], F32, "gbh")
    pools = P.sb([128, 2, 4], F32, "pools_sb")
    poolw = P.sb([128, 2, 4, 128], BF16, "poolw_sb")
    wT = P.sb([128, 2, 4, 128], BF16, "wT")
    bh = P.sb([128, 2, 4, 128], F32, "bh")
    ghbc = P.sb([128, 2, 512], F32, "ghbc")
    invc = P.sb([128, 4, 16], F32, "invc_sb")
    banks = [P.ps([128, 512], F32, f"bank{i}") for i in range(8)]

    def bankv(i, lo=0, hi=512):
        return V(banks[i][:, lo:hi], ("bank", i))

    class Arena:
        def __init__(self):
            self.p = 0

        def reset(self, p=0):
            self.p = p

        def take(self, words, dtype=F32, shape=None):
            lo = self.p
            self.p += (words + 7) // 8 * 8
            assert self.p <= AW, (self.p, AW)
            ap = arena[:, lo:lo + words]
            if dtype == BF16:
                ap = ap.bitcast(BF16)
            return V(ap, ("@ar", lo, lo + words))

    AR = Arena()

    def R(*vs):
        out = []
        for v in vs:
            if v is None:
                continue
            out.append(v.reg if isinstance(v, V) else v)
        return out

    def act(out, in_, func, reads, writes, bias=None, scale=None, accum=None):
        kw = {}
        if bias is not None:
            kw["bias"] = bias
        if scale is not None:
            kw["scale"] = scale
        if accum is not None:
            kw["accum_out"] = accum
        return P.op("act", lambda e: e.activation(out, in_, func, **kw), reads, writes)

    def tt(eng, out, a, b, op, reads, writes):
        return P.op(eng, lambda e: e.tensor_tensor(out, a, b, op), reads, writes)

    def ts(eng, out, a, s1, s2, op0, op1, reads, writes):
        if op1 is None:
            return P.op(eng, lambda e: e.tensor_scalar(out, a, s1, None, op0), reads, writes)
        return P.op(eng, lambda e: e.tensor_scalar(out, a, s1, s2, op0, op1), reads, writes)

    def stt(out, a, s, b, op0, op1, reads, writes):
        return P.op("dve", lambda e: e.scalar_tensor_tensor(out, a, s, b, op0, op1), reads, writes)

    def cp(eng, out, in_, reads, writes):
        if eng == "act":
            return P.op("act", lambda e: e.activation(out, in_, AF.Copy), reads, writes)
        return P.op(eng, lambda e: e.tensor_copy(out, in_), reads, writes)

    def mm(out, lhsT, rhs, start, stop, reads, writes, tp=None):
        if tp is None:
            return P.op("pe", lambda e: e.matmul(out, lhsT, rhs, start=start, stop=stop), reads, writes)
        return P.op("pe", lambda e: e.matmul(out, lhsT, rhs, start=start, stop=stop, tile_position=tp),
                    reads, writes)

    def dump(name, ap, shape, reads):
        if name not in dbg:
            return
        t = nc.dram_tensor("dbg_" + name, list(shape), ap.dtype, kind="ExternalOutput").ap()
        dbg_out[name] = t
        P.dma("act", t, ap, "all:dbg", reads=reads, writes=[("dbgout", name)])

    smc = {"n": 0}

    def small_dma(dst_ap, src_ap, writes):
        g = "sm%d" % (smc["n"] % 4)
        smc["n"] += 1
        P.dma("sp", dst_ap, src_ap, g, writes=writes)

    def load_small(dst_ap, name, key):
        small_dma(dst_ap, din(name), [key])

    P.op("dve", lambda e: e.memset(ones[:], 1.0), writes=["ones"])
    P.op("dve", lambda e: e.memset(onesf[:], 1.0), writes=["onesf"])
    P.op("dve", lambda e: e.memset(cst[:, 0:1], -0.5), writes=["cst"])
    P.op("dve", lambda e: e.memset(cst[:, 1:2], EPS), reads=["cst"], writes=["cst"])
    P.op("dve", lambda e: e.memset(fhist[:], 0.0), writes=["fhist_all"])
    P.op("dve", lambda e: e.memset(cxhist[:], 0.0), writes=["cxhist_all"])
    P.op("dve", lambda e: e.memset(zhist[:], 0.0), writes=["zhist_all"])
    P.op("dve", lambda e: e.memset(qinit[:], 0.0), writes=["qinit_all"])
    load_small(gmix[:], "gmix", "gmix")
    load_small(gffn[:], "gffn", "gffn")
    load_small(gfin[:], "gfin", "gfin")
    load_small(invc[:], "invc", "invc")
    for i in layers:
        load_small(fcw[:, i], f"fcw{i}", ("fcw", i))
        load_small(fcb[:, i], f"fcb{i}", ("fcb", i))

    def wview(name, K):
        return din(name).rearrange("(kt p) n -> p kt n", p=128)

    def slabs_for(i):
        j = i // 2
        L = []
        if i % 2 == 0:
            w = wview(f"ewin{j}", 1024)
            L.append((8, 512, [(w, 0, 0, 256), (w, 1024, 256, 256)]))
            L.append((8, 512, [(w, 256, 0, 256), (w, 1280, 256, 256)]))
            L.append((8, 512, [(w, 1536, 0, 512)]))
            L.append((8, 512, [(w, 512, 0, 512)]))
            L.append((4, 512, [(wview(f"glw{j}", 512), 0, 0, 512)]))
            wo = wview(f"ewout{j}", 1024)
        else:
            w = wview(f"owin{j}", 1024)
            L.append((8, 512, [(w, 0, 0, 512)]))
            L.append((8, 512, [(w, 512, 0, 512)]))
            L.append((8, 512, [(w, 1024, 0, 512)]))
            wo = wview(f"owout{j}", 1024)
        L.append((8, 512, [(wo, 0, 0, 512)]))
        L.append((8, 512, [(wo, 512, 0, 512)]))
        wu = wview(f"fup{i}", 1024)
        for k in range(11):
            L.append((8, 512, [(wu, 256 * k, 0, 256), (wu, 2816 + 256 * k, 256, 256)]))
        wd = wview(f"fdn{i}", 2816)
        for m in range(8):
            L.append((22, 128, [(wd, 128 * m, 0, 128)]))
        return L

    lay_slabs = {i: slabs_for(i) for i in layers}
    seq = []
    for t in range(ntiles):
        for li, i in enumerate(layers):
            for s in range(len(lay_slabs[i])):
                seq.append((t, li, i, s))
    stream = {"next": 0, "cur": 0, "cast": 0, "stg": 0}

    def slot_reg(slot):
        return ("@ring%d" % slot, 0, 4096)

    def make_load(n):
        t, li, i, s = seq[n]
        KT, W, pieces = lay_slabs[i][s]
        slot = n % NSLOT
        sid = li * 26 + s
        nel = KT * W
        if t == 0:
            h0 = (KT + 1) // 2
            for (k0, k1) in ((0, h0), (h0, KT)):
                if k1 <= k0:
                    continue
                sg = stream["stg"] % 2
                stream["stg"] += 1
                nk = k1 - k0
                sview = stage[:, sg, 0:nk * W].rearrange("p (k w) -> p k w", k=nk)
                for pi, (w, c0, d0, wd_) in enumerate(pieces):
                    P.dma("sp", sview[:, :, d0:d0 + wd_], w[:, k0:k1, c0:c0 + wd_], "stg%d" % sg,
                          writes=[("stage", sg, pi)])
                eng = ("pool", "act")[stream["cast"] % 2]
                stream["cast"] += 1
                dst = ring[:, slot, k0 * W:k1 * W]
                src = stage[:, sg, 0:nk * W]
                cp(eng, dst, src, [("stage", sg, pi) for pi in range(len(pieces))],
                   [("@ring%d" % slot, k0 * W, k1 * W)])
            P.dma("act", scr[sid][:, 0:nel], ring[:, slot, 0:nel], "scrst%d" % slot,
                  reads=[("@ring%d" % slot, 0, nel)], writes=[("scr", sid)])
        else:
            P.dma("sp", ring[:, slot, 0:nel], scr[sid][:, 0:nel], "ring%d" % slot,
                  reads=[("scr", sid)], writes=[("@ring%d" % slot, 0, nel)])

    def next_slab():
        n = stream["cur"]
        stream["cur"] += 1
        while stream["next"] < min(len(seq), n + NSLOT):
            make_load(stream["next"])
            stream["next"] += 1
        t, li, i, s = seq[n]
        KT, W, _ = lay_slabs[i][s]
        slot = n % NSLOT
        view = ring[:, slot, 0:KT * W].rearrange("p (k w) -> p k w", k=KT)
        return V(view, slot_reg(slot))

    rot = {"main": 0, "bu": 0, "yb": 0}
    pools_ = {"main4": [0, 1, 2, 3], "main8": [0, 1, 2, 3, 4, 5, 6, 7], "bu": [4, 5, 0, 1], "yb": [6, 7]}

    def nb(pool):
        key = "main" if pool.startswith("main") else pool
        lst = pools_[pool]
        b = lst[rot[key] % len(lst)]
        rot[key] += 1
        return b

    load_small(ident[:], "ident", "ident")
    even_js = [i // 2 for i in layers if i % 2 == 0]
    odd_js = [i // 2 for i in layers if i % 2 == 1]

    def ssm_setup(j, jj):
        AR.reset()
        sm = lambda: AR.take(16)
        ls, are, aim = sm(), sm(), sm()
        for v, nm in ((ls, "sls"), (are, "sare"), (aim, "saim")):
            small_dma(v.ap, din(f"{nm}{j}"), R(v))
        big = {}
        for nm in ("sbre", "sbim", "scre", "scim"):
            big[nm] = AR.take(256)
            small_dma(big[nm].ap.rearrange("p (k h) -> p k h", k=16), din(f"{nm}{j}"), R(big[nm]))
        small_dma(econv[:, j], din(f"econv{j}"), [("econv", j)])
        sdl, glbl = AR.take(4), AR.take(4)
        small_dma(sdl.ap, din(f"sd{j}"), R(sdl))
        small_dma(glbl.ap, din(f"glb{j}"), R(glbl))
        ts("dve", dq[:, j, :], sdl.ap, 0.25, None, ALU.mult, None, R(sdl), [("dq", j)])
        ts("dve", gbh[:, j, :], glbl.ap, 0.5, None, ALU.mult, None, R(glbl), [("gbh", j)])

        dt_, xr, th = sm(), sm(), sm()
        act(dt_.ap, ls.ap, AF.Exp, R(ls), R(dt_))
        tt("dve", xr.ap, are.ap, dt_.ap, ALU.mult, R(are, dt_), R(xr))
        tt("dve", th.ap, aim.ap, dt_.ap, ALU.mult, R(aim, dt_), R(th))
        rv = V(rdec[:, j, :], ("rdec", j))
        act(rv.ap, xr.ap, AF.Exp, R(xr), R(rv))
        al, a2, ps_, pc_ = sm(), sm(), sm(), sm()
        ts("dve", al.ap, th.ap, 1.0 / 64, None, ALU.mult, None, R(th), R(al))
        tt("dve", a2.ap, al.ap, al.ap, ALU.mult, R(al), R(a2))

        def horner(p, coefs):
            ts("dve", p.ap, a2.ap, coefs[0], coefs[1], ALU.mult, ALU.add, R(a2), R(p))
            for c in coefs[2:]:
                tt("dve", p.ap, p.ap, a2.ap, ALU.mult, R(p, a2), R(p))
                ts("dve", p.ap, p.ap, c, None, ALU.add, None, R(p), R(p))

        horner(ps_, [1.0 / 362880, -1.0 / 5040, 1.0 / 120, -1.0 / 6, 1.0])
        tt("dve", ps_.ap, ps_.ap, al.ap, ALU.mult, R(ps_, al), R(ps_))
        horner(pc_, [-1.0 / 3628800, 1.0 / 40320, -1.0 / 720, 1.0 / 24, -0.5, 1.0])
        t1, t2 = sm(), sm()
        for _ in range(6):
            tt("dve", t1.ap, pc_.ap, pc_.ap, ALU.mult, R(pc_), R(t1))
            tt("dve", t2.ap, ps_.ap, ps_.ap, ALU.mult, R(ps_), R(t2))
            tt("dve", ps_.ap, ps_.ap, pc_.ap, ALU.mult, R(ps_, pc_), R(ps_))
            ts("dve", ps_.ap, ps_.ap, 2.0, None, ALU.mult, None, R(ps_), R(ps_))
            tt("dve", pc_.ap, t1.ap, t2.ap, ALU.subtract, R(t1, t2), R(pc_))
        c1, s1 = pc_, ps_
        nre, nim, den, fre, fim = sm(), sm(), sm(), sm(), sm()
        tt("dve", nre.ap, rv.ap, c1.ap, ALU.mult, R(rv, c1), R(nre))
        ts("dve", nre.ap, nre.ap, -1.0, None, ALU.add, None, R(nre), R(nre))
        tt("dve", nim.ap, rv.ap, s1.ap, ALU.mult, R(rv, s1), R(nim))
        tt("dve", den.ap, are.ap, are.ap, ALU.mult, R(are), R(den))
        tt("dve", t1.ap, aim.ap, aim.ap, ALU.mult, R(aim), R(t1))
        tt("dve", den.ap, den.ap, t1.ap, ALU.add, R(den, t1), R(den))
        P.op("dve", lambda e: e.reciprocal(den.ap, den.ap), R(den), R(den))
        tt("dve", fre.ap, nre.ap, are.ap, ALU.mult, R(nre, are), R(fre))
        tt("dve", t1.ap, nim.ap, aim.ap, ALU.mult, R(nim, aim), R(t1))
        tt("dve", fre.ap, fre.ap, t1.ap, ALU.add, R(fre, t1), R(fre))
        tt("dve", fre.ap, fre.ap, den.ap, ALU.mult, R(fre, den), R(fre))
        tt("dve", fim.ap, nim.ap, are.ap, ALU.mult, R(nim, are), R(fim))
        tt("dve", t1.ap, nre.ap, aim.ap, ALU.mult, R(nre, aim), R(t1))
        tt("dve", fim.ap, fim.ap, t1.ap, ALU.subtract, R(fim, t1), R(fim))
        tt("dve", fim.ap, fim.ap, den.ap, ALU.mult, R(fim, den), R(fim))
        v3 = lambda v: v.ap.rearrange("p (k h) -> p k h", k=16)
        bc = lambda v: v.ap.unsqueeze(2).to_broadcast([128, 16, 16])
        Bre, Bim, tA = AR.take(256), AR.take(256), AR.take(256)
        tt("dve", v3(Bre), v3(big["sbre"]), bc(fre), ALU.mult, R(big["sbre"], fre), R(Bre))
        tt("dve", v3(tA), v3(big["sbim"]), bc(fim), ALU.mult, R(big["sbim"], fim), R(tA))
        tt("dve", v3(Bre), v3(Bre), v3(tA), ALU.subtract, R(Bre, tA), R(Bre))
        tt("dve", v3(Bim), v3(big["sbim"]), bc(fre), ALU.mult, R(big["sbim"], fre), R(Bim))
        tt("dve", v3(tA), v3(big["sbre"]), bc(fim), ALU.mult, R(big["sbre"], fim), R(tA))
        tt("dve", v3(Bim), v3(Bim), v3(tA), ALU.add, R(Bim, tA), R(Bim))
        for comp, Bv in ((0, Bre), (1, Bim)):
            M = AR.take(512)
            P.op("dve", lambda e, M=M: e.memset(M.ap, 0.0), (), R(M))
            M4 = M.ap.rearrange("p (k g h) -> p k g h", k=16, g=2)
            B3 = v3(Bv)
            cp("dve", M4[0:64, :, 0, :], B3[0:64], R(Bv, M), R(M))
            cp("dve", M4[64:128, :, 1, :], B3[64:128], R(Bv, M), R(M))
            for b in range(4):
                bk = nb("main8")
                P.op("pe", lambda e, bk=bk, M=M, b=b: e.transpose(banks[bk][:, 0:128],
                                                                  M.ap[:, b * 128:(b + 1) * 128], ident[:]),
                     R(M) + ["ident"], [("bank", bk)])
                cp("act", Bl[:, j, b, comp, :], banks[bk][:, 0:128], [("bank", bk)], [("Bl", j)])
        P.op("dve", lambda e: e.memset(Cl[:, j], 0.0), (), [("Cl", j)])
        for m, (nm, sc) in enumerate((("scre", 0.25), ("scre", -0.25), ("scim", -0.25))):
            src = v3(big[nm])
            for g2 in range(2):
                ts("dve", Cl[64 * g2:64 * g2 + 64, j, :, m, 16 * g2:16 * g2 + 16], src[64 * g2:64 * g2 + 64],
                   sc, None, ALU.mult, None, R(big[nm]) + [("Cl", j)], [("Cl", j)])
        tabv = ("tab",)
        cp("dve", tab[:, 0, :, 0:1], c1.ap.unsqueeze(2), R(c1) + ["tab"], ["tab"])
        cp("dve", tab[:, 1, :, 0:1], s1.ap.unsqueeze(2), R(s1) + ["tab"], ["tab"])
        w1, w2 = AR.take(2048), AR.take(2048)
        n = 1
        while n < TS:
            cb = tab[:, 0, :, n - 1:n].to_broadcast([128, 16, n])
            sb_ = tab[:, 1, :, n - 1:n].to_broadcast([128, 16, n])
            a1 = w1.ap[:, 0:16 * n].rearrange("p (k t) -> p k t", k=16)
            a2_ = w2.ap[:, 0:16 * n].rearrange("p (k t) -> p k t", k=16)
            tt("dve", a1, tab[:, 0, :, 0:n], cb, ALU.mult, ["tab"], R(w1))
            tt("dve", a2_, tab[:, 1, :, 0:n], sb_, ALU.mult, ["tab"], R(w2))
            tt("dve", tab[:, 0, :, n:2 * n], a1, a2_, ALU.subtract, R(w1, w2) + ["tab"], ["tab"])
            tt("dve", a1, tab[:, 1, :, 0:n], cb, ALU.mult, ["tab"], R(w1))
            tt("dve", a2_, tab[:, 0, :, 0:n], sb_, ALU.mult, ["tab"], R(w2))
            tt("dve", tab[:, 1, :, n:2 * n], a1, a2_, ALU.add, R(w1, w2) + ["tab"], ["tab"])
            n *= 2
        cp("dve", rotc[:, j, 0, :].unsqueeze(2), tab[:, 0, :, TS - 1:TS], ["tab"], [("rotc", j)])
        cp("dve", rotc[:, j, 1, :].unsqueeze(2), tab[:, 1, :, TS - 1:TS], ["tab"], [("rotc", j)])
        P.dma("act", tabd[jj], tab[:].rearrange("p c k t -> p (c k t)"), "tabst", reads=["tab"],
              writes=[("tabd", jj)])

    def odd_setup(j):
        AR.reset()
        small_dma(pools[:, j, :], din(f"pools{j}"), [("pools", j)])
        pw = AR.take(512)
        small_dma(pw.ap.rearrange("p (g d) -> p g d", g=4), din(f"poolw{j}"), R(pw))
        cp("dve", poolw[:, j].rearrange("p g d -> p (g d)"), pw.ap, R(pw), [("poolw", j)])
        sw, tr = AR.take(512), AR.take(128)
        small_dma(sw.ap.rearrange("p (g d) -> p g d", g=4), din(f"sgw{j}"), R(sw))
        small_dma(tr.ap, din("trilh"), R(tr))
        tt("dve", wT[:, j], sw.ap.rearrange("p (g d) -> p g d", g=4),
           tr.ap.unsqueeze(1).to_broadcast([128, 4, 128]), ALU.mult, R(sw, tr), [("wT", j)])
        sb2 = AR.take(512)
        small_dma(sb2.ap.rearrange("p (g d) -> p g d", g=4), din(f"sgb{j}"), R(sb2))
        ts("dve", bh[:, j].rearrange("p g d -> p (g d)"), sb2.ap, 0.5, None, ALU.mult, None, R(sb2),
           [("bh", j)])
        gg = AR.take(512)
        small_dma(gg.ap, din(f"sgng{j}"), R(gg))
        ts("dve", ghbc[:, j, :], gg.ap, 0.5, None, ALU.mult, None, R(gg), [("ghbc", j)])

    for jj, j in enumerate(even_js):
        ssm_setup(j, jj)
    for j in odd_js:
        odd_setup(j)
    tab_state = {"j": even_js[-1] if even_js else None}

    dq_ = []

    def later(delay, fn):
        dq_.append([delay, fn])

    def tick():
        due = [x for x in dq_ if x[0] <= 0]
        for x in due:
            dq_.remove(x)
        for x in dq_:
            x[0] -= 1
        for x in due:
            x[1]()

    def flush():
        while dq_:
            tick()

    def sub(v, a, b):
        return V(v.ap[:, a:b], ("@ar", v.reg[1] + a, v.reg[1] + b))

    def xr_(kt):
        return V(xres[:, kt, :], ("xres", kt))

    def hb_(kt):
        return V(hb[:, kt, :], ("hb", kt))

    def rmsnorm(gap, gkey, outs, bf=True):
        AR.reset()
        sq = [AR.take(256, BF16) for _ in range(8)]
        ms4, rs4 = AR.take(4), AR.take(4)
        dg = [AR.take(128) for _ in range(4)]
        bk = nb("main8")
        for kt in range(8):
            act(sq[kt].ap, xres[:, kt, :], AF.Square, [("xres", kt)], R(sq[kt]))
        for tb_ in range(4):
            for kt in range(8):
                mm(banks[bk][:, tb_:tb_ + 1], sq[kt].ap[:, tb_ * 128:(tb_ + 1) * 128], ones[:, 0:1],
                   kt == 0, kt == 7, ["ones"] + R(sq[kt]), [("bank", bk)])
        ts("dve", ms4.ap, banks[bk][:, 0:4], 1.0 / D, EPS, ALU.mult, ALU.add, [("bank", bk)], R(ms4))
        tt("pool", rs4.ap, ms4.ap, cst[:, 0:1].to_broadcast([128, 4]), ALU.pow, R(ms4) + ["cst"], R(rs4))
        bk2 = nb("main8")
        for tb_ in range(4):
            act(dg[tb_].ap, ident[:], AF.Identity, R(rs4) + ["ident"], R(dg[tb_]), scale=rs4.ap[:, tb_:tb_ + 1])
            mm(banks[bk2][:, tb_ * 128:(tb_ + 1) * 128], onesf[:], dg[tb_].ap, True, True,
               ["onesf"] + R(dg[tb_]), [("bank", bk2)])
        for kt in range(8):
            stt(outs[kt].ap, xres[:, kt, :], gap[:, kt:kt + 1], banks[bk2][:, :], ALU.mult, ALU.mult,
                [("xres", kt), gkey, ("bank", bk2)], R(outs[kt]))

    def proj_chunk(Wv, col, pool, rhs_list, KT=8):
        bk = nb(pool)
        for kt in range(KT):
            mm(banks[bk][:, :], Wv.ap[:, kt, col:col + 128], rhs_list[kt].ap, kt == 0, kt == KT - 1,
               [Wv.reg] + R(rhs_list[kt]), [("bank", bk)])
        return bk

    def out_proj(ycat, pool):
        for half in range(2):
            Wo = next_slab()
            for m_ in range(4):
                m = half * 4 + m_
                bk = proj_chunk(Wo, m_ * 128, pool, ycat)
                tt("dve", xres[:, m, :], xres[:, m, :], banks[bk][:, :], ALU.add,
                   [("xres", m), ("bank", bk)], [("xres", m)])

    def gelu2(bk, tb, Gout):
        cp("act", tb["pc"].ap, banks[bk][:, :], [("bank", bk)], R(tb["pc"]))
        act(tb["p2"].ap, banks[bk][:, :], AF.Square, [("bank", bk)], R(tb["p2"]))
        ts("pool", tb["ti"].ap, tb["p2"].ap, 0.044715, 1.0, ALU.mult, ALU.add, R(tb["p2"]), R(tb["ti"]))
        tt("pool", tb["ti"].ap, tb["ti"].ap, tb["pc"].ap, ALU.mult, R(tb["ti"], tb["pc"]), R(tb["ti"]))
        act(tb["th"].ap, tb["ti"].ap, AF.Tanh, R(tb["ti"]), R(tb["th"]), scale=0.7978845608028654)
        stt(Gout.ap, tb["th"].ap, 1.0, tb["pc"].ap, ALU.add, ALU.mult, R(tb["th"], tb["pc"]), R(Gout))

    def even_mixer(i, t):
        j = i // 2
        hbl = [hb_(kt) for kt in range(8)]
        AR.reset()
        xa = [AR.take(512) for _ in range(4)]
        cxb = [AR.take(514) for _ in range(4)]
        ycat = [AR.take(256, BF16) for _ in range(8)]
        u = [AR.take(512) for _ in range(4)]
        ub = [AR.take(256, BF16) for _ in range(4)]
        ssmb = [dict(bu=AR.take(512), m34=AR.take(512), q=AR.take(512),
                     X=AR.take(256, BF16), Y=AR.take(256, BF16)) for _ in range(3)]
        tmpb = [dict(yq=AR.take(256), y2=AR.take(256), ti=AR.take(256), th=AR.take(256)) for _ in range(2)]
        rt = [AR.take(16) for _ in range(4)]
        G = xa
        base = cxb[0].reg[1]
        gelu_bf = [V(arena[:, base + 256 * b:base + 256 * (b + 1)].bitcast(BF16),
                     ("@ar", base + 256 * b, base + 256 * (b + 1))) for b in range(4)]
        th2 = [V(arena[:, base + 1024 + 512 * x:base + 1024 + 512 * (x + 1)],
                 ("@ar", base + 1024 + 512 * x, base + 1024 + 512 * (x + 1))) for x in range(2)]
        if tab_state["j"] != j:
            jj = even_js.index(j)
            P.dma("sp", tab[:].rearrange("p c k t -> p (c k t)"), tabd[jj], "tabld",
                  reads=[("tabd", jj)], writes=["tab"])
            tab_state["j"] = j
        for sl in range(2):
            W = next_slab()
            for ii in range(2):
                c = 2 * sl + ii
                bx = proj_chunk(W, ii * 128, "main4", hbl)
                cp("act", xa[c].ap, banks[bx][:, :], [("bank", bx)], R(xa[c]))
                bc_ = proj_chunk(W, 256 + ii * 128, "main4", hbl)
                cxh, cxm = sub(cxb[c], 0, 2), sub(cxb[c], 2, 514)
                cp("pool", cxh.ap, cxhist[:, j, c, :], [("cxhist", j, c)], R(cxh))
                tt("dve", cxm.ap, banks[bc_][:, :], xa[c].ap, ALU.mult, [("bank", bc_)] + R(xa[c]), R(cxm))
                ek = ("econv", j)
                act(xa[c].ap, cxm.ap, AF.Identity, R(cxm) + [ek], R(xa[c]), scale=econv[:, j, c, 2:3])
                stt(xa[c].ap, cxb[c].ap[:, 1:513], econv[:, j, c, 1:2], xa[c].ap, ALU.mult, ALU.add,
                    R(cxb[c], xa[c]) + [ek], R(xa[c]))
                stt(xa[c].ap, cxb[c].ap[:, 0:512], econv[:, j, c, 0:1], xa[c].ap, ALU.mult, ALU.add,
                    R(cxb[c], xa[c]) + [ek], R(xa[c]))
                cp("pool", cxhist[:, j, c, :], cxb[c].ap[:, 512:514], R(cxm), [("cxhist", j, c)])
        W = next_slab()
        for b in range(4):
            bk = proj_chunk(W, b * 128, "main4", hbl)
            cp("act", u[b].ap, banks[bk][:, :], [("bank", bk)], R(u[b]))
            cp("pool", ub[b].ap, u[b].ap, R(u[b]), R(ub[b]))
        W = next_slab()
        for c in range(4):
            bk = proj_chunk(W, c * 128, "main4", hbl)
            tt("dve", ycat[c].ap, banks[bk][:, :], xa[c].ap, ALU.mult, [("bank", bk)] + R(xa[c]), R(ycat[c]))
        items = [(sb_i, b, q) for sb_i in range(T // TS) for b in range(4) for q in range(4)]
        v3 = lambda v: v.ap.rearrange("p (c t) -> p c t", c=2)
        stA = {}

        def stageA(n):
            sb_i, b, q = items[n]
            c0 = sb_i * TS
            S = ssmb[n % 3]
            bb = nb("bu")
            for comp in range(2):
                mm(banks[bb][:, comp * TS:(comp + 1) * TS], Bl[32 * q:32 * q + 32, j, b, comp, :],
                   ub[b].ap[32 * q:32 * q + 32, c0:c0 + TS], True, True,
                   [("Bl", j)] + R(ub[b]), [("bank", bb)], tp=(32 * q, 0))
            cp("act", S["bu"].ap, banks[bb][:, :], [("bank", bb)], R(S["bu"]))

        def stageB(n, yk):
            sb_i, b, q = items[n]
            k = 4 * b + q
            S = ssmb[n % 3]
            cbc = tab[:, 0, k, :].unsqueeze(1).to_broadcast([128, 2, TS])
            sbc = tab[:, 1, k, :].unsqueeze(1).to_broadcast([128, 2, TS])
            tt("pool", v3(S["m34"]), v3(S["bu"]), sbc, ALU.mult, R(S["bu"]) + ["tab"], R(S["m34"]))
            tt("pool", v3(S["bu"]), v3(S["bu"]), cbc, ALU.mult, R(S["bu"]) + ["tab"], R(S["bu"]))
            mre, mim = sub(S["bu"], 0, TS), sub(S["bu"], TS, 2 * TS)
            tt("dve", mre.ap, mre.ap, S["m34"].ap[:, TS:2 * TS], ALU.add, R(mre, S["m34"]), R(mre))
            tt("dve", mim.ap, mim.ap, S["m34"].ap[:, 0:TS], ALU.subtract, R(mim, S["m34"]), R(mim))
            rb = rdec[:, j, k:k + 1].to_broadcast([128, TS])
            for comp, mv in ((0, mre), (1, mim)):
                qv = sub(S["q"], comp * TS, (comp + 1) * TS)
                P.op("dve", lambda e, qv=qv, rb=rb, mv=mv, comp=comp, k=k: e.tensor_tensor_scan(
                    qv.ap, rb, mv.ap, qinit[:, j, comp, k:k + 1], ALU.mult, ALU.add),
                    R(mv) + [("rdec", j), ("qinit", j)], R(qv))
            Xv = S["X"].ap.rearrange("p (c t) -> p c t", c=2)
            Yv = S["Y"].ap.rearrange("p (c t) -> p c t", c=2)
            tt("dve", Xv, v3(S["q"]), cbc, ALU.mult, R(S["q"]) + ["tab"], R(S["X"]))
            tt("dve", Yv, v3(S["q"]), sbc, ALU.mult, R(S["q"]) + ["tab"], R(S["Y"]))
            cp("pool", qend[:, :, k:k + 1], v3(S["q"])[:, :, TS - 1:TS], R(S["q"]), [("qend", k)])
            yo = banks[yk][32 * q:32 * q + 32, 0:TS]
            ops_ = ((0, S["X"], 0), (1, S["Y"], 1), (2, S["Y"], 0), (2, S["X"], 1))
            for n_, (m_, src, half) in enumerate(ops_):
                mm(yo, Cl[:, j, k, m_, :], src.ap[:, half * TS:(half + 1) * TS], n_ == 0, n_ == 3,
                   [("Cl", j)] + R(src), [("bank", yk)], tp=(0, 32 * q))

        def block_epilogue(sb_i, b, yk, ib):
            c0 = sb_i * TS
            tb = tmpb[ib % 2]
            Gs = sub(G[b], c0, c0 + TS)
            gb = V(gelu_bf[b].ap[:, c0:c0 + TS], ("@ar", gelu_bf[b].reg[1] + c0 // 2,
                                                 gelu_bf[b].reg[1] + (c0 + TS) // 2))

            def e1():
                stt(tb["yq"].ap, u[b].ap[:, c0:c0 + TS], dq[:, j, b:b + 1], banks[yk][:, 0:TS], ALU.mult, ALU.add,
                    R(u[b]) + [("dq", j), ("bank", yk)], R(tb["yq"]))
                act(tb["y2"].ap, tb["yq"].ap, AF.Square, R(tb["yq"]), R(tb["y2"]), scale=4.0)

            def e2():
                ts("pool", tb["ti"].ap, tb["y2"].ap, 4 * 0.044715, 4.0, ALU.mult, ALU.add, R(tb["y2"]), R(tb["ti"]))
                tt("pool", tb["ti"].ap, tb["ti"].ap, tb["yq"].ap, ALU.mult, R(tb["ti"], tb["yq"]), R(tb["ti"]))

            def e3():
                act(tb["th"].ap, tb["ti"].ap, AF.Tanh, R(tb["ti"]), R(tb["th"]), scale=0.7978845608028654)

            def e4():
                stt(Gs.ap, tb["th"].ap, 1.0, tb["yq"].ap, ALU.add, ALU.mult, R(tb["th"], tb["yq"]), R(Gs))
                act(gb.ap, Gs.ap, AF.Copy, R(Gs), R(gb), scale=2.0)

            later(0, e1)
            later(1, e2)
            later(2, e3)
            later(3, e4)

        def carry():
            qk = [("qend", k) for k in range(16)]
            cT, sT = rotc[:, j, 0, :], rotc[:, j, 1, :]
            rk = [("rotc", j)]
            tt("dve", rt[0].ap, cT, qend[:, 0, :], ALU.mult, qk + rk, R(rt[0]))
            tt("dve", rt[1].ap, sT, qend[:, 1, :], ALU.mult, qk + rk, R(rt[1]))
            tt("dve", rt[2].ap, sT, qend[:, 0, :], ALU.mult, qk + rk, R(rt[2]))
            tt("dve", rt[3].ap, cT, qend[:, 1, :], ALU.mult, qk + rk, R(rt[3]))
            tt("dve", qinit[:, j, 0, :], rt[0].ap, rt[1].ap, ALU.subtract, R(rt[0], rt[1]), [("qinit", j)])
            tt("dve", qinit[:, j, 1, :], rt[2].ap, rt[3].ap, ALU.add, R(rt[2], rt[3]), [("qinit", j)])

        NI = len(items)
        stageA(0)
        stageA(1)
        yk = None
        ib = 0
        for n in range(NI):
            sb_i, b, q = items[n]
            if n + 2 < NI:
                stageA(n + 2)
            if q == 0:
                yk = nb("yb")
            stageB(n, yk)
            tick()
            if q == 3:
                block_epilogue(sb_i, b, yk, ib)
                ib += 1
                if b == 3:
                    carry()
        flush()
        Wg = next_slab()
        for c in range(4):
            bk = proj_chunk(Wg, c * 128, "main4", gelu_bf, KT=4)
            tv = th2[c % 2]
            act(tv.ap, banks[bk][:, :], AF.Tanh, [("bank", bk), ("gbh", j)], R(tv), bias=gbh[:, j, c:c + 1], scale=0.5)
            stt(ycat[4 + c].ap, tv.ap, 1.0, G[c].ap, ALU.add, ALU.mult, R(tv, G[c]), R(ycat[4 + c]))
        out_proj(ycat, "main4")

    def ffn(i, t):
        hbl = [hb_(kt) for kt in range(8)]
        AR.reset(3072)
        hid = [AR.take(256, BF16) for _ in range(22)]
        acc = [AR.take(512) for _ in range(4)]
        sg = [AR.take(512) for _ in range(2)]
        for k in range(11):
            W = next_slab()
            for jj in range(2):
                jch = 2 * k + jj
                chs = [jch, jch + 22]
                bks = [proj_chunk(W, gv * 256 + jj * 128, "main8", hbl) for gv in range(2)]
                accs = [acc[2 * jj + gv] for gv in range(2)]
                wk = [("fcw", i)]
                w_ = lambda ch, kk: fcw[:, i, ch, kk:kk + 1]
                for gv in range(2):
                    act(accs[gv].ap, banks[bks[gv]][:, :], AF.Identity, [("bank", bks[gv]), ("fcb", i)] + wk,
                        R(accs[gv]), bias=fcb[:, i, chs[gv]:chs[gv] + 1], scale=w_(chs[gv], 2))
                for gv in range(2):
                    a, bk, ch = accs[gv], bks[gv], chs[gv]
                    stt(a.ap[:, 1:T], banks[bk][:, 0:T - 1], w_(ch, 1), a.ap[:, 1:T], ALU.mult, ALU.add,
                        [("bank", bk)] + wk + R(a), R(a))
                for gv in range(2):
                    a, bk, ch = accs[gv], bks[gv], chs[gv]
                    stt(a.ap[:, 0:1], fhist[:, i, ch, 1:2], w_(ch, 1), a.ap[:, 0:1], ALU.mult, ALU.add,
                        [("fhist", i, ch)] + wk + R(a), R(a))
                for gv in range(2):
                    a, bk, ch = accs[gv], bks[gv], chs[gv]
                    stt(a.ap[:, 2:T], banks[bk][:, 0:T - 2], w_(ch, 0), a.ap[:, 2:T], ALU.mult, ALU.add,
                        [("bank", bk)] + wk + R(a), R(a))
                for gv in range(2):
                    a, bk, ch = accs[gv], bks[gv], chs[gv]
                    stt(a.ap[:, 0:2], fhist[:, i, ch, 0:2], w_(ch, 0), a.ap[:, 0:2], ALU.mult, ALU.add,
                        [("fhist", i, ch)] + wk + R(a), R(a))
                def fin(chs=chs, bks=bks, accs=accs, s_=sg[jj], jch=jch):
                    for gv in range(2):
                        cp("act", fhist[:, i, chs[gv], :], banks[bks[gv]][:, T - 2:T], [("bank", bks[gv])],
                           [("fhist", i, chs[gv])])
                    act(s_.ap, accs[0].ap, AF.Silu, R(accs[0]), R(s_))
                    tt("pool", hid[jch].ap, s_.ap, accs[1].ap, ALU.mult, R(s_, accs[1]), R(hid[jch]))

                tick()
                later(0, fin)
        flush()
        for m in range(8):
            Wd = next_slab()
            bk = nb("main8")
            for jch in range(22):
                mm(banks[bk][:, :], Wd.ap[:, jch, :], hid[jch].ap, jch == 0, jch == 21,
                   [Wd.reg] + R(hid[jch]), [("bank", bk)])
            tt("dve", xres[:, m, :], xres[:, m, :], banks[bk][:, :], ALU.add,
               [("xres", m), ("bank", bk)], [("xres", m)])

    def odd_mixer(i, t):
        j = i // 2
        hbl = [hb_(kt) for kt in range(8)]
        AR.reset()
        zt = [AR.take(528) for _ in range(4)]
        sA, sB = AR.take(528), AR.take(528)
        pooled = [AR.take(256, BF16) for _ in range(4)]
        Gu = [AR.take(512) for _ in range(4)]
        tmpb = [dict(pc=AR.take(512), p2=AR.take(512), ti=AR.take(512), th=AR.take(512)) for _ in range(2)]
        Gv = [AR.take(512) for _ in range(2)]
        vtok = [AR.take(256, BF16) for _ in range(4)]
        ycat = [AR.take(256, BF16) for _ in range(8)]
        sgt = [AR.take(512) for _ in range(2)]
        sml = [dict(ss=AR.take(1), ms=AR.take(1), rs=AR.take(1)) for _ in range(2)]
        t15 = AR.take(16)
        W = next_slab()
        for g, w in enumerate((2, 4, 8, 16)):
            bk = proj_chunk(W, g * 128, "main8", hbl)
            zh, zm = sub(zt[g], 0, 16), sub(zt[g], 16, 528)
            cp("pool", zh.ap, zhist[:, j, g, :], [("zhist", j, g)], R(zh))
            cp("act", zm.ap, banks[bk][:, :], [("bank", bk)], R(zm))
            z = zt[g]
            tt("pool", sA.ap[:, 1:528], z.ap[:, 1:528], z.ap[:, 0:527], ALU.add, R(z), R(sA))
            Sv = sA
            if g >= 1:
                tt("pool", sB.ap[:, 3:528], sA.ap[:, 3:528], sA.ap[:, 1:526], ALU.add, R(sA), R(sB))
                Sv = sB
            if g >= 2:
                tt("pool", sA.ap[:, 7:528], sB.ap[:, 7:528], sB.ap[:, 3:524], ALU.add, R(sB), R(sA))
                Sv = sA
            if g >= 3:
                tt("pool", sB.ap[:, 15:528], sA.ap[:, 15:528], sA.ap[:, 7:520], ALU.add, R(sA), R(sB))
                Sv = sB
            stt(pooled[g].ap, Sv.ap[:, 16:528], 1.0 / w, zm.ap, ALU.mult, ALU.subtract, R(Sv, zm), R(pooled[g]))
            if t == 0:
                tt("dve", t15.ap[:, 0:15], Sv.ap[:, 16:31], invc[:, g, 0:15], ALU.mult, R(Sv) + ["invc"], R(t15))
                tt("dve", pooled[g].ap[:, 0:15], t15.ap[:, 0:15], zt[g].ap[:, 16:31], ALU.subtract,
                   R(t15, zm, pooled[g]), R(pooled[g]))
            cp("pool", zhist[:, j, g, :], zt[g].ap[:, 512:528], R(zm), [("zhist", j, g)])
            bk2 = nb("main8")
            mm(banks[bk2][:, :], poolw[:, j, g, :], pooled[g].ap, True, True, [("poolw", j)] + R(pooled[g]),
               [("bank", bk2)])
            act(ycat[g].ap, banks[bk2][:, :], AF.Identity, [("bank", bk2), ("pools", j)], R(ycat[g]),
                scale=pools[:, j, g:g + 1])
        W = next_slab()
        for c in range(4):
            bk = proj_chunk(W, c * 128, "main8", hbl)
            gelu2(bk, tmpb[c % 2], Gu[c])
        W = next_slab()
        for tb_ in range(4):
            bk = nb("main8")
            for kt in range(8):
                mm(banks[bk][:, :], hb[:, kt, tb_ * 128:(tb_ + 1) * 128], W.ap[:, kt, :], kt == 0, kt == 7,
                   [W.reg, ("hb", kt)], [("bank", bk)])
            tb = tmpb[tb_ % 2]
            gv = Gv[tb_ % 2]
            sm_ = sml[tb_ % 2]
            gelu2(bk, tb, gv)
            act(tb["p2"].ap, gv.ap, AF.Square, R(gv), R(tb["p2"], sm_["ss"]), scale=0.5, accum=sm_["ss"].ap)
            ts("dve", sm_["ms"].ap, sm_["ss"].ap, 1.0 / 512, EPS, ALU.mult, ALU.add, R(sm_["ss"]), R(sm_["ms"]))
            tt("pool", sm_["rs"].ap, sm_["ms"].ap, cst[:, 0:1], ALU.pow, R(sm_["ms"]) + ["cst"], R(sm_["rs"]))
            stt(vtok[tb_].ap, gv.ap, sm_["rs"].ap, ghbc[:, j, :], ALU.mult, ALU.mult,
                R(gv, sm_["rs"]) + [("ghbc", j)], R(vtok[tb_]))
        for hd in range(4):
            bk = nb("main8")
            for tb_ in range(4):
                mm(banks[bk][:, tb_ * 128:(tb_ + 1) * 128], vtok[tb_].ap[:, hd * 128:(hd + 1) * 128],
                   wT[:, j, hd, :], True, True, [("wT", j)] + R(vtok[tb_]), [("bank", bk)])
            sv_ = sgt[hd % 2]
            tt("dve", sv_.ap.rearrange("p (a b) -> p a b", a=4),
               banks[bk][:, :].rearrange("p (a b) -> p a b", a=4),
               bh[:, j, hd, :].unsqueeze(1).to_broadcast([128, 4, 128]), ALU.add,
               [("bank", bk), ("bh", j)], R(sv_))
            tt("dve", ycat[4 + hd].ap, sv_.ap, Gu[hd].ap, ALU.mult, R(sv_, Gu[hd]), R(ycat[4 + hd]))
        out_proj(ycat, "main8")

    for t in range(ntiles):
        t0 = t * T
        P.dma("sp", xres[:, :, :], xTv[:, :, t0:t0 + T], "xload", writes=[("xres", kt) for kt in range(8)])
        for i in layers:
            j = i // 2
            rmsnorm(gmix[:, i, :], "gmix", [hb_(kt) for kt in range(8)])
            if i % 2 == 0:
                even_mixer(i, t)
            else:
                odd_mixer(i, t)
            rmsnorm(gffn[:, i, :], "gffn", [hb_(kt) for kt in range(8)])
            ffn(i, t)
        if final:
            AR.reset(3072)
            ost = [AR.take(512) for _ in range(8)]
            p_after = AR.p
            rmsnorm(gfin[:, :], "gfin", ost)
            AR.reset(p_after)
            ov = arena[:, ost[0].reg[1]:ost[0].reg[1] + 4096].rearrange("p (k t) -> p k t", k=8)
            P.dma("act", outTv[:, :, t0:t0 + T], ov, "ostore", reads=R(*ost), writes=[("outT", t)])
        else:
            P.dma("act", outTv[:, :, t0:t0 + T], xres[:, :, :], "ostore",
                  reads=[("xres", kt) for kt in range(8)], writes=[("outT", t)])
    P.wait("act", [("outT", t) for t in range(ntiles)])
    P.emit()
    P.close()
    names = ["xT"] + list(dr.keys())
    return nc, names, dbg_out


_CACHE = {}


def _get(layers, ntiles, final):
    key = (tuple(layers), ntiles, final)
    if key not in _CACHE:
        _CACHE[key] = build(layers, ntiles, final)
    return _CACHE[key]


def kernel(**inputs):
    com = prep_common(inputs)
    x = np.asarray(inputs["x"], dtype=np.float32)
    nb_ = x.shape[0]
    xTs = [np.ascontiguousarray(x[b].T) for b in range(nb_)]
    nc, names, _ = _get((0, 1, 2, 3), SEQ // T, True)
    in_maps = []
    for b in range(nb_):
        m = {"xT": xTs[b]}
        for n in names:
            if n != "xT":
                m[n] = com[n]
        in_maps.append(m)
    res = run_bass_kernel_spmd(nc, in_maps, core_ids=list(range(nb_)))
    out = np.stack([np.ascontiguousarray(res.results[b]["outT"].T) for b in range(nb_)], axis=0)
    return out.astype(np.float32)
```

```python
import contextlib
import numpy as np
import concourse.bass as bass
import concourse.mybir as mybir
from concourse.bass_utils import run_bass_kernel_spmd

F32 = mybir.dt.float32
BF16 = mybir.dt.bfloat16
ALU = mybir.AluOpType
AF = mybir.ActivationFunctionType

D = 1024
SEQ = 4096
DEPTH = 4
DFF = 2816
T = 512
TS = 256
NSLOT = 4
EPS = 1e-6
ENGS = ("pe", "act", "dve", "pool", "sp")


class Op:
    __slots__ = ("eng", "fn", "deps", "signal", "sigval", "sem", "is_dma", "group")

    def __init__(self, eng, fn):
        self.eng = eng
        self.fn = fn
        self.deps = []
        self.signal = False
        self.sigval = 0
        self.sem = None
        self.is_dma = False
        self.group = None


class Prog:
    def __init__(self, nc):
        self.nc = nc
        self.ops = {e: [] for e in ENGS}
        self.state = {}
        self.stack = contextlib.ExitStack()
        self.dma_groups = {}
        self.n = 0

    def sb(self, shape, dtype=F32, name=None):
        self.n += 1
        return self.stack.enter_context(self.nc.sbuf_tensor(name or f"sb{self.n}", list(shape), dtype))

    def ps(self, shape, dtype=F32, name=None):
        self.n += 1
        return self.stack.enter_context(self.nc.psum_tensor(name or f"ps{self.n}", list(shape), dtype))

    @staticmethod
    def _norm(k):
        if isinstance(k, tuple) and len(k) == 3 and isinstance(k[0], str) and k[0].startswith("@"):
            return k[0], int(k[1]), int(k[2])
        return k, 0, 1

    def _segs(self, ns, lo, hi):
        L = self.state.setdefault(ns, [])
        out = []
        newL = []
        cur = lo
        for sg in L:
            a, b, w, r = sg
            if b <= lo or a >= hi:
                newL.append(sg)
                continue
            if a < lo:
                newL.append([a, lo, w, list(r)])
                a = lo
            if b > hi:
                newL.append([hi, b, w, list(r)])
                b = hi
            mid = [a, b, w, r]
            newL.append(mid)
            out.append(mid)
        out.sort(key=lambda x: x[0])
        filled = []
        for sg in out:
            if sg[0] > cur:
                g = [cur, sg[0], None, []]
                newL.append(g)
                filled.append(g)
            filled.append(sg)
            cur = sg[1]
        if cur < hi:
            g = [cur, hi, None, []]
            newL.append(g)
            filled.append(g)
        self.state[ns] = newL
        return filled

    def _track(self, o, reads, writes, skip_same_eng=False):
        deps = []
        rn = [self._norm(k) for k in reads]
        wn = [self._norm(k) for k in writes]
        for ns, lo, hi in rn:
            for sg in self._segs(ns, lo, hi):
                if sg[2] is not None:
                    deps.append(sg[2])
        for ns, lo, hi in wn:
            for sg in self._segs(ns, lo, hi):
                if sg[2] is not None:
                    deps.append(sg[2])
                last = {}
                for r in sg[3]:
                    if r.is_dma:
                        deps.append(r)
                    else:
                        last[r.eng] = r
                deps.extend(last.values())
        for ns, lo, hi in rn:
            for sg in self._segs(ns, lo, hi):
                sg[3].append(o)
        for ns, lo, hi in wn:
            segs = self._segs(ns, lo, hi)
            L = self.state[ns]
            for sg in segs:
                L.remove(sg)
            L.append([lo, hi, o, []])
        seen = set()
        for d in deps:
            if d is o or id(d) in seen:
                continue
            if skip_same_eng and (not d.is_dma) and d.eng == o.eng:
                continue
            seen.add(id(d))
            o.deps.append(d)

    def op(self, eng, fn, reads=(), writes=()):
        o = Op(eng, fn)
        self._track(o, reads, writes, skip_same_eng=(eng == "pe"))
        self.ops[eng].append(o)
        return o

    def dma(self, eng, out, in_, group, reads=(), writes=(), **kw):
        o = Op(eng, lambda e: e.dma_start(out=out, in_=in_, **kw))
        o.is_dma = True
        o.group = group
        self._track(o, reads, writes)
        lst = self.dma_groups.setdefault(group, [])
        if lst and not group.startswith("all:") and lst[-1] not in o.deps:
            o.deps.append(lst[-1])
        self.ops[eng].append(o)
        lst.append(o)
        return o

    def wait(self, eng, reads):
        o = Op(eng, lambda e: None)
        self._track(o, reads, ())
        self.ops[eng].append(o)
        return o

    def emit(self):
        nc = self.nc
        for e in ENGS:
            for o in self.ops[e]:
                for d in o.deps:
                    d.signal = True
        sems = {}
        for e in ENGS:
            sems[e] = self.stack.enter_context(nc.semaphore(f"s_{e}"))
            c = 0
            for o in self.ops[e]:
                if o.is_dma:
                    continue
                if o.signal:
                    c += 1
                    o.sigval = c
                    o.sem = sems[e]
        for g, lst in self.dma_groups.items():
            s = self.stack.enter_context(nc.semaphore(f"d_{len(sems)}"))
            sems["dma:" + g] = s
            if g.startswith("all:"):
                for o in lst:
                    o.sem = s
                    o.sigval = 16 * len(lst)
            else:
                for i, o in enumerate(lst):
                    o.sem = s
                    o.sigval = 16 * (i + 1)
        engmap = {"pe": "tensor", "act": "scalar", "dve": "vector", "pool": "gpsimd", "sp": "sync"}
        with nc.Block() as block:
            for e in ENGS:
                ops = self.ops[e]
                if not ops:
                    continue

                def body(eng, ops=ops):
                    waited = {}
                    for o in ops:
                        need = {}
                        for d in o.deps:
                            k = id(d.sem)
                            if d.sigval > need.get(k, (None, 0))[1]:
                                need[k] = (d.sem, d.sigval)
                        for k, (s, v) in need.items():
                            if waited.get(k, 0) >= v:
                                continue
                            eng.wait_ge(s, v)
                            waited[k] = v
                        inst = o.fn(eng)
                        if inst is None:
                            continue
                        if o.is_dma:
                            inst.then_inc(o.sem, 16)
                        elif o.signal:
                            inst.then_inc(o.sem, 1)

                getattr(block, engmap[e])(body)

    def close(self):
        self.stack.close()


def _pair(a):
    a = np.asarray(a, dtype=np.float32)
    rest = a.shape[2:]
    a = a.reshape((16, 2, 64) + rest)
    perm = (1, 2, 0) + tuple(range(3, 3 + len(rest)))
    return np.ascontiguousarray(a.transpose(perm).reshape((128, 16) + rest))


def _cols(v, n):
    return np.ascontiguousarray(np.asarray(v, dtype=np.float32).reshape(n, 128).T)


def prep_common(inp):
    f = lambda a: np.ascontiguousarray(np.asarray(a, dtype=np.float32))
    com = {}
    com["gmix"] = f(np.asarray(inp["norm_mix_g"]).reshape(4, 8, 128).transpose(2, 0, 1))
    com["gffn"] = f(np.asarray(inp["norm_ffn_g"]).reshape(4, 8, 128).transpose(2, 0, 1))
    com["gfin"] = _cols(inp["norm_final_g"], 8)
    for j in range(2):
        com[f"ewin{j}"] = f(inp["even_w_in"][j])
        com[f"econv{j}"] = f(np.asarray(inp["even_conv_w"][j]).reshape(3, 4, 128).transpose(2, 1, 0))
        ls = np.broadcast_to(np.asarray(inp["ssm_log_step"][j])[:, None], (32, 64))
        com[f"sls{j}"] = _pair(ls)
        com[f"sare{j}"] = _pair(inp["ssm_a_re"][j])
        com[f"saim{j}"] = _pair(inp["ssm_a_im"][j])
        com[f"sbre{j}"] = _pair(inp["ssm_b_re"][j])
        com[f"sbim{j}"] = _pair(inp["ssm_b_im"][j])
        com[f"scre{j}"] = _pair(np.asarray(inp["ssm_c_re"][j]).transpose(0, 2, 1))
        com[f"scim{j}"] = _pair(np.asarray(inp["ssm_c_im"][j]).transpose(0, 2, 1))
        com[f"sd{j}"] = _cols(inp["ssm_d"][j], 4)
        com[f"glw{j}"] = f(inp["ssm_glu_w"][j])
        com[f"glb{j}"] = _cols(inp["ssm_glu_b"][j], 4)
        com[f"ewout{j}"] = f(inp["even_w_out"][j])
        com[f"owin{j}"] = f(inp["odd_w_in"][j])
        com[f"poolw{j}"] = f(np.asarray(inp["pool_w"][j]).transpose(1, 0, 2))
        com[f"pools{j}"] = _cols(inp["pool_scale"][j], 4)
        com[f"sgng{j}"] = f(np.broadcast_to(np.asarray(inp["sgu_norm_g"][j])[None, :], (128, 512)))
        com[f"sgw{j}"] = f(np.asarray(inp["sgu_w"][j]).transpose(2, 0, 1))
        com[f"sgb{j}"] = f(np.broadcast_to(np.asarray(inp["sgu_b"][j])[None], (128, 4, 128)))
        com[f"owout{j}"] = f(inp["odd_w_out"][j])
    for i in range(4):
        com[f"fup{i}"] = f(inp["ffn_w_up"][i])
        com[f"fcw{i}"] = f(np.asarray(inp["ffn_conv_w"][i]).reshape(3, 44, 128).transpose(2, 1, 0))
        com[f"fcb{i}"] = _cols(inp["ffn_conv_b"][i], 44)
        com[f"fdn{i}"] = f(inp["ffn_w_down"][i])
    s = np.arange(128)
    com["trilh"] = f(0.5 * (s[:, None] <= s[None, :]))
    invc = np.zeros((128, 4, 16), np.float32)
    for g, w in enumerate((2, 4, 8, 16)):
        invc[:, g, :] = 1.0 / np.minimum(np.arange(1, 17), w)
    com["invc"] = invc
    com["ident"] = f(np.eye(128))
    return com


SHAPES = {
    "gmix": [128, 4, 8], "gffn": [128, 4, 8], "gfin": [128, 8],
    "trilh": [128, 128], "invc": [128, 4, 16], "ident": [128, 128],
}
for _j in range(2):
    SHAPES.update({
        f"ewin{_j}": [1024, 2048], f"econv{_j}": [128, 4, 3], f"sls{_j}": [128, 16],
        f"sare{_j}": [128, 16], f"saim{_j}": [128, 16], f"sbre{_j}": [128, 16, 16],
        f"sbim{_j}": [128, 16, 16], f"scre{_j}": [128, 16, 16], f"scim{_j}": [128, 16, 16],
        f"sd{_j}": [128, 4], f"glw{_j}": [512, 512], f"glb{_j}": [128, 4], f"ewout{_j}": [1024, 1024],
        f"owin{_j}": [1024, 1536], f"poolw{_j}": [128, 4, 128], f"pools{_j}": [128, 4],
        f"sgng{_j}": [128, 512], f"sgw{_j}": [128, 4, 128], f"sgb{_j}": [128, 4, 128],
        f"owout{_j}": [1024, 1024],
    })
for _i in range(4):
    SHAPES.update({f"fup{_i}": [1024, 5632], f"fcw{_i}": [128, 44, 3], f"fcb{_i}": [128, 44],
                   f"fdn{_i}": [2816, 1024]})


class V:
    __slots__ = ("ap", "reg")

    def __init__(self, ap, reg):
        self.ap = ap
        self.reg = reg


def build(layers=(0, 1, 2, 3), ntiles=8, final=True, dbg=()):
    nc = bass.Bass("TRN2", target_bir_lowering=False)
    P = Prog(nc)
    dr = {}
    dbg_out = {}

    def din(name):
        if name not in dr:
            dr[name] = nc.dram_tensor(name, SHAPES[name], F32, kind="ExternalInput").ap()
        return dr[name]

    xT = nc.dram_tensor("xT", [D, SEQ], F32, kind="ExternalInput").ap()
    outT = nc.dram_tensor("outT", [D, SEQ], F32, kind="ExternalOutput").ap()
    xTv = xT.rearrange("(kt p) t -> p kt t", p=128)
    outTv = outT.rearrange("(kt p) t -> p kt t", p=128)
    nlay = len(layers)
    scr = nc.dram_tensor("scr", [nlay * 26, 128, 4096], BF16, kind="Internal").ap()
    tabd = nc.dram_tensor("tabd", [2, 128, 2 * 16 * TS], F32, kind="Internal").ap()

    xres = P.sb([128, 8, T], F32, "xres")
    hb = P.sb([128, 8, T], BF16, "hb")
    ring = P.sb([128, NSLOT, 4096], BF16, "ring")
    stage = P.sb([128, 2, 2048], F32, "stage")
    tab = P.sb([128, 2, 16, TS], F32, "tab")
    AW = 17920
    arena = P.sb([128, AW], F32, "arena")
    ones = P.sb([128, 128], BF16, "ones")
    onesf = P.sb([128, 128], F32, "onesf")
    ident = P.sb([128, 128], F32, "ident_sb")
    cst = P.sb([128, 4], F32, "cst")
    gmix = P.sb([128, 4, 8], F32, "gmix_sb")
    gffn = P.sb([128, 4, 8], F32, "gffn_sb")
    gfin = P.sb([128, 8], F32, "gfin_sb")
    fcw = P.sb([128, 4, 44, 3], F32, "fcw_sb")
    fcb = P.sb([128, 4, 44], F32, "fcb_sb")
    fhist = P.sb([128, 4, 44, 2], F32, "fhist")
    cxhist = P.sb([128, 2, 4, 2], F32, "cxhist")
    zhist = P.sb([128, 2, 4, 16], F32, "zhist")
    qinit = P.sb([128, 2, 2, 16], F32, "qinit")
    qend = P.sb([128, 2, 16], F32, "qend")
    econv = P.sb([128, 2, 4, 3], F32, "econv_sb")
    Bl = P.sb([128, 2, 4, 2, 128], BF16, "Bl")
    Cl = P.sb([128, 2, 16, 3, 32], BF16, "Cl")
    rdec = P.sb([128, 2, 16], F32, "rdec")
    rotc = P.sb([128, 2, 2, 16], F32, "rotc")
    dq = P.sb([128, 2, 4], F32, "dq")
    gbh = P.sb([128, 2, 4], F32, "gbh")
    pools = P.sb([128, 2, 4], F32, "pools_sb")
    poolw = P.sb([128, 2, 4, 128], BF16, "poolw_sb")
    wT = P.sb([128, 2, 4, 128], BF16, "wT")
    bh = P.sb([128, 2, 4, 128], F32, "bh")
    ghbc = P.sb([128, 2, 512], F32, "ghbc")
    invc = P.sb([128, 4, 16], F32, "invc_sb")
    banks = [P.ps([128, 512], F32, f"bank{i}") for i in range(8)]

    def bankv(i, lo=0, hi=512):
        return V(banks[i][:, lo:hi], ("bank", i))

    class Arena:
        def __init__(self):
            self.p = 0

        def reset(self, p=0):
            self.p = p

        def take(self, words, dtype=F32, shape=None):
            lo = self.p
            self.p += (words + 7) // 8 * 8
            assert self.p <= AW, (self.p, AW)
            ap = arena[:, lo:lo + words]
            if dtype == BF16:
                ap = ap.bitcast(BF16)
            return V(ap, ("@ar", lo, lo + words))

    AR = Arena()

    def R(*vs):
        out = []
        for v in vs:
            if v is None:
                continue
            out.append(v.reg if isinstance(v, V) else v)
        return out

    def act(out, in_, func, reads, writes, bias=None, scale=None, accum=None):
        kw = {}
        if bias is not None:
            kw["bias"] = bias
        if scale is not None:
            kw["scale"] = scale
        if accum is not None:
            kw["accum_out"] = accum
        return P.op("act", lambda e: e.activation(out, in_, func, **kw), reads, writes)

    def tt(eng, out, a, b, op, reads, writes):
        return P.op(eng, lambda e: e.tensor_tensor(out, a, b, op), reads, writes)

    def ts(eng, out, a, s1, s2, op0, op1, reads, writes):
        if op1 is None:
            return P.op(eng, lambda e: e.tensor_scalar(out, a, s1, None, op0), reads, writes)
        return P.op(eng, lambda e: e.tensor_scalar(out, a, s1, s2, op0, op1), reads, writes)

    def stt(out, a, s, b, op0, op1, reads, writes):
        return P.op("dve", lambda e: e.scalar_tensor_tensor(out, a, s, b, op0, op1), reads, writes)

    def cp(eng, out, in_, reads, writes):
        if eng == "act":
            return P.op("act", lambda e: e.activation(out, in_, AF.Copy), reads, writes)
        return P.op(eng, lambda e: e.tensor_copy(out, in_), reads, writes)

    def mm(out, lhsT, rhs, start, stop, reads, writes, tp=None):
        if tp is None:
            return P.op("pe", lambda e: e.matmul(out, lhsT, rhs, start=start, stop=stop), reads, writes)
        return P.op("pe", lambda e: e.matmul(out, lhsT, rhs, start=start, stop=stop, tile_position=tp),
                    reads, writes)

    def dump(name, ap, shape, reads):
        if name not in dbg:
            return
        t = nc.dram_tensor("dbg_" + name, list(shape), ap.dtype, kind="ExternalOutput").ap()
        dbg_out[name] = t
        P.dma("act", t, ap, "all:dbg", reads=reads, writes=[("dbgout", name)])

    smc = {"n": 0}

    def small_dma(dst_ap, src_ap, writes):
        g = "sm%d" % (smc["n"] % 4)
        smc["n"] += 1
        P.dma("sp", dst_ap, src_ap, g, writes=writes)

    def load_small(dst_ap, name, key):
        small_dma(dst_ap, din(name), [key])

    P.op("dve", lambda e: e.memset(ones[:], 1.0), writes=["ones"])
    P.op("dve", lambda e: e.memset(onesf[:], 1.0), writes=["onesf"])
    P.op("dve", lambda e: e.memset(cst[:, 0:1], -0.5), writes=["cst"])
    P.op("dve", lambda e: e.memset(cst[:, 1:2], EPS), reads=["cst"], writes=["cst"])
    P.op("dve", lambda e: e.memset(fhist[:], 0.0), writes=["fhist_all"])
    P.op("dve", lambda e: e.memset(cxhist[:], 0.0), writes=["cxhist_all"])
    P.op("dve", lambda e: e.memset(zhist[:], 0.0), writes=["zhist_all"])
    P.op("dve", lambda e: e.memset(qinit[:], 0.0), writes=["qinit_all"])
    load_small(gmix[:], "gmix", "gmix")
    load_small(gffn[:], "gffn", "gffn")
    load_small(gfin[:], "gfin", "gfin")
    load_small(invc[:], "invc", "invc")
    for i in layers:
        load_small(fcw[:, i], f"fcw{i}", ("fcw", i))
        load_small(fcb[:, i], f"fcb{i}", ("fcb", i))

    def wview(name, K):
        return din(name).rearrange("(kt p) n -> p kt n", p=128)

    def slabs_for(i):
        j = i // 2
        L = []
        if i % 2 == 0:
            w = wview(f"ewin{j}", 1024)
            L.append((8, 512, [(w, 0, 0, 256), (w, 1024, 256, 256)]))
            L.append((8, 512, [(w, 256, 0, 256), (w, 1280, 256, 256)]))
            L.append((8, 512, [(w, 1536, 0, 512)]))
            L.append((8, 512, [(w, 512, 0, 512)]))
            L.append((4, 512, [(wview(f"glw{j}", 512), 0, 0, 512)]))
            wo = wview(f"ewout{j}", 1024)
        else:
            w = wview(f"owin{j}", 1024)
            L.append((8, 512, [(w, 0, 0, 512)]))
            L.append((8, 512, [(w, 512, 0, 512)]))
            L.append((8, 512, [(w, 1024, 0, 512)]))
            wo = wview(f"owout{j}", 1024)
        L.append((8, 512, [(wo, 0, 0, 512)]))
        L.append((8, 512, [(wo, 512, 0, 512)]))
        wu = wview(f"fup{i}", 1024)
        for k in range(11):
            L.append((8, 512, [(wu, 256 * k, 0, 256), (wu, 2816 + 256 * k, 256, 256)]))
        wd = wview(f"fdn{i}", 2816)
        for m in range(8):
            L.append((22, 128, [(wd, 128 * m, 0, 128)]))
        return L

    lay_slabs = {i: slabs_for(i) for i in layers}
    seq = []
    for t in range(ntiles):
        for li, i in enumerate(layers):
            for s in range(len(lay_slabs[i])):
                seq.append((t, li, i, s))
    stream = {"next": 0, "cur": 0, "cast": 0, "stg": 0}

    def slot_reg(slot):
        return ("@ring%d" % slot, 0, 4096)

    def make_load(n):
        t, li, i, s = seq[n]
        KT, W, pieces = lay_slabs[i][s]
        slot = n % NSLOT
        sid = li * 26 + s
        nel = KT * W
        if t == 0:
            h0 = (KT + 1) // 2
            for (k0, k1) in ((0, h0), (h0, KT)):
                if k1 <= k0:
                    continue
                sg = stream["stg"] % 2
                stream["stg"] += 1
                nk = k1 - k0
                sview = stage[:, sg, 0:nk * W].rearrange("p (k w) -> p k w", k=nk)
                for pi, (w, c0, d0, wd_) in enumerate(pieces):
                    P.dma("sp", sview[:, :, d0:d0 + wd_], w[:, k0:k1, c0:c0 + wd_], "stg%d" % sg,
                          writes=[("stage", sg, pi)])
                eng = ("pool", "act")[stream["cast"] % 2]
                stream["cast"] += 1
                dst = ring[:, slot, k0 * W:k1 * W]
                src = stage[:, sg, 0:nk * W]
                cp(eng, dst, src, [("stage", sg, pi) for pi in range(len(pieces))],
                   [("@ring%d" % slot, k0 * W, k1 * W)])
            P.dma("act", scr[sid][:, 0:nel], ring[:, slot, 0:nel], "scrst%d" % slot,
                  reads=[("@ring%d" % slot, 0, nel)], writes=[("scr", sid)])
        else:
            P.dma("sp", ring[:, slot, 0:nel], scr[sid][:, 0:nel], "ring%d" % slot,
                  reads=[("scr", sid)], writes=[("@ring%d" % slot, 0, nel)])

    def next_slab():
        n = stream["cur"]
        stream["cur"] += 1
        while stream["next"] < min(len(seq), n + NSLOT):
            make_load(stream["next"])
            stream["next"] += 1
        t, li, i, s = seq[n]
        KT, W, _ = lay_slabs[i][s]
        slot = n % NSLOT
        view = ring[:, slot, 0:KT * W].rearrange("p (k w) -> p k w", k=KT)
        return V(view, slot_reg(slot))

    rot = {"main": 0, "bu": 0, "yb": 0}
    pools_ = {"main4": [0, 1, 2, 3], "main8": [0, 1, 2, 3, 4, 5, 6, 7], "bu": [4, 5, 0, 1], "yb": [6, 7]}

    def nb(pool):
        key = "main" if pool.startswith("main") else pool
        lst = pools_[pool]
        b = lst[rot[key] % len(lst)]
        rot[key] += 1
        return b

    load_small(ident[:], "ident", "ident")
    even_js = [i // 2 for i in layers if i % 2 == 0]
    odd_js = [i // 2 for i in layers if i % 2 == 1]

    def ssm_setup(j, jj):
        AR.reset()
        sm = lambda: AR.take(16)
        ls, are, aim = sm(), sm(), sm()
        for v, nm in ((ls, "sls"), (are, "sare"), (aim, "saim")):
            small_dma(v.ap, din(f"{nm}{j}"), R(v))
        big = {}
        for nm in ("sbre", "sbim", "scre", "scim"):
            big[nm] = AR.take(256)
            small_dma(big[nm].ap.rearrange("p (k h) -> p k h", k=16), din(f"{nm}{j}"), R(big[nm]))
        small_dma(econv[:, j], din(f"econv{j}"), [("econv", j)])
        sdl, glbl = AR.take(4), AR.take(4)
        small_dma(sdl.ap, din(f"sd{j}"), R(sdl))
        small_dma(glbl.ap, din(f"glb{j}"), R(glbl))
        ts("dve", dq[:, j, :], sdl.ap, 0.25, None, ALU.mult, None, R(sdl), [("dq", j)])
        ts("dve", gbh[:, j, :], glbl.ap, 0.5, None, ALU.mult, None, R(glbl), [("gbh", j)])

        dt_, xr, th = sm(), sm(), sm()
        act(dt_.ap, ls.ap, AF.Exp, R(ls), R(dt_))
        tt("dve", xr.ap, are.ap, dt_.ap, ALU.mult, R(are, dt_), R(xr))
        tt("dve", th.ap, aim.ap, dt_.ap, ALU.mult, R(aim, dt_), R(th))
        rv = V(rdec[:, j, :], ("rdec", j))
        act(rv.ap, xr.ap, AF.Exp, R(xr), R(rv))
        al, a2, ps_, pc_ = sm(), sm(), sm(), sm()
        ts("dve", al.ap, th.ap, 1.0 / 64, None, ALU.mult, None, R(th), R(al))
        tt("dve", a2.ap, al.ap, al.ap, ALU.mult, R(al), R(a2))

        def horner(p, coefs):
            ts("dve", p.ap, a2.ap, coefs[0], coefs[1], ALU.mult, ALU.add, R(a2), R(p))
            for c in coefs[2:]:
                tt("dve", p.ap, p.ap, a2.ap, ALU.mult, R(p, a2), R(p))
                ts("dve", p.ap, p.ap, c, None, ALU.add, None, R(p), R(p))

        horner(ps_, [1.0 / 362880, -1.0 / 5040, 1.0 / 120, -1.0 / 6, 1.0])
        tt("dve", ps_.ap, ps_.ap, al.ap, ALU.mult, R(ps_, al), R(ps_))
        horner(pc_, [-1.0 / 3628800, 1.0 / 40320, -1.0 / 720, 1.0 / 24, -0.5, 1.0])
        t1, t2 = sm(), sm()
        for _ in range(6):
            tt("dve", t1.ap, pc_.ap, pc_.ap, ALU.mult, R(pc_), R(t1))
            tt("dve", t2.ap, ps_.ap, ps_.ap, ALU.mult, R(ps_), R(t2))
            tt("dve", ps_.ap, ps_.ap, pc_.ap, ALU.mult, R(ps_, pc_), R(ps_))
            ts("dve", ps_.ap, ps_.ap, 2.0, None, ALU.mult, None, R(ps_), R(ps_))
            tt("dve", pc_.ap, t1.ap, t2.ap, ALU.subtract, R(t1, t2), R(pc_))
        c1, s1 = pc_, ps_
        nre, nim, den, fre, fim = sm(), sm(), sm(), sm(), sm()
        tt("dve", nre.ap, rv.ap, c1.ap, ALU.mult, R(rv, c1), R(nre))
        ts("dve", nre.ap, nre.ap, -1.0, None, ALU.add, None, R(nre), R(nre))
        tt("dve", nim.ap, rv.ap, s1.ap, ALU.mult, R(rv, s1), R(nim))
        tt("dve", den.ap, are.ap, are.ap, ALU.mult, R(are), R(den))
        tt("dve", t1.ap, aim.ap, aim.ap, ALU.mult, R(aim), R(t1))
        tt("dve", den.ap, den.ap, t1.ap, ALU.add, R(den, t1), R(den))
        P.op("dve", lambda e: e.reciprocal(den.ap, den.ap), R(den), R(den))
        tt("dve", fre.ap, nre.ap, are.ap, ALU.mult, R(nre, are), R(fre))
        tt("dve", t1.ap, nim.ap, aim.ap, ALU.mult, R(nim, aim), R(t1))
        tt("dve", fre.ap, fre.ap, t1.ap, ALU.add, R(fre, t1), R(fre))
        tt("dve", fre.ap, fre.ap, den.ap, ALU.mult, R(fre, den), R(fre))
        tt("dve", fim.ap, nim.ap, are.ap, ALU.mult, R(nim, are), R(fim))
        tt("dve", t1.ap, nre.ap, aim.ap, ALU.mult, R(nre, aim), R(t1))
        tt("dve", fim.ap, fim.ap, t1.ap, ALU.subtract, R(fim, t1), R(fim))
        tt("dve", fim.ap, fim.ap, den.ap, ALU.mult, R(fim, den), R(fim))
        v3 = lambda v: v.ap.rearrange("p (k h) -> p k h", k=16)
        bc = lambda v: v.ap.unsqueeze(2).to_broadcast([128, 16, 16])
        Bre, Bim, tA = AR.take(256), AR.take(256), AR.take(256)
        tt("dve", v3(Bre), v3(big["sbre"]), bc(fre), ALU.mult, R(big["sbre"], fre), R(Bre))
        tt("dve", v3(tA), v3(big["sbim"]), bc(fim), ALU.mult, R(big["sbim"], fim), R(tA))
        tt("dve", v3(Bre), v3(Bre), v3(tA), ALU.subtract, R(Bre, tA), R(Bre))
        tt("dve", v3(Bim), v3(big["sbim"]), bc(fre), ALU.mult, R(big["sbim"], fre), R(Bim))
        tt("dve", v3(tA), v3(big["sbre"]), bc(fim), ALU.mult, R(big["sbre"], fim), R(tA))
        tt("dve", v3(Bim), v3(Bim), v3(tA), ALU.add, R(Bim, tA), R(Bim))
        for comp, Bv in ((0, Bre), (1, Bim)):
            M = AR.take(512)
            P.op("dve", lambda e, M=M: e.memset(M.ap, 0.0), (), R(M))
            M4 = M.ap.rearrange("p (k g h) -> p k g h", k=16, g=2)
            B3 = v3(Bv)
            cp("dve", M4[0:64, :, 0, :], B3[0:64], R(Bv, M), R(M))
            cp("dve", M4[64:128, :, 1, :], B3[64:128], R(Bv, M), R(M))
            for b in range(4):
                bk = nb("main8")
                P.op("pe", lambda e, bk=bk, M=M, b=b: e.transpose(banks[bk][:, 0:128],
                                                                  M.ap[:, b * 128:(b + 1) * 128], ident[:]),
                     R(M) + ["ident"], [("bank", bk)])
                cp("act", Bl[:, j, b, comp, :], banks[bk][:, 0:128], [("bank", bk)], [("Bl", j)])
        P.op("dve", lambda e: e.memset(Cl[:, j], 0.0), (), [("Cl", j)])
        for m, (nm, sc) in enumerate((("scre", 0.25), ("scre", -0.25), ("scim", -0.25))):
            src = v3(big[nm])
            for g2 in range(2):
                ts("dve", Cl[64 * g2:64 * g2 + 64, j, :, m, 16 * g2:16 * g2 + 16], src[64 * g2:64 * g2 + 64],
                   sc, None, ALU.mult, None, R(big[nm]) + [("Cl", j)], [("Cl", j)])
        tabv = ("tab",)
        cp("dve", tab[:, 0, :, 0:1], c1.ap.unsqueeze(2), R(c1) + ["tab"], ["tab"])
        cp("dve", tab[:, 1, :, 0:1], s1.ap.unsqueeze(2), R(s1) + ["tab"], ["tab"])
        w1, w2 = AR.take(2048), AR.take(2048)
        n = 1
        while n < TS:
            cb = tab[:, 0, :, n - 1:n].to_broadcast([128, 16, n])
            sb_ = tab[:, 1, :, n - 1:n].to_broadcast([128, 16, n])
            a1 = w1.ap[:, 0:16 * n].rearrange("p (k t) -> p k t", k=16)
            a2_ = w2.ap[:, 0:16 * n].rearrange("p (k t) -> p k t", k=16)
            tt("dve", a1, tab[:, 0, :, 0:n], cb, ALU.mult, ["tab"], R(w1))
            tt("dve", a2_, tab[:, 1, :, 0:n], sb_, ALU.mult, ["tab"], R(w2))
            tt("dve", tab[:, 0, :, n:2 * n], a1, a2_, ALU.subtract, R(w1, w2) + ["tab"], ["tab"])
            tt("dve", a1, tab[:, 1, :, 0:n], cb, ALU.mult, ["tab"], R(w1))
            tt("dve", a2_, tab[:, 0, :, 0:n], sb_, ALU.mult, ["tab"], R(w2))
            tt("dve", tab[:, 1, :, n:2 * n], a1, a2_, ALU.add, R(w1, w2) + ["tab"], ["tab"])
            n *= 2
        cp("dve", rotc[:, j, 0, :].unsqueeze(2), tab[:, 0, :, TS - 1:TS], ["tab"], [("rotc", j)])
        cp("dve", rotc[:, j, 1, :].unsqueeze(2), tab[:, 1, :, TS - 1:TS], ["tab"], [("rotc", j)])
        P.dma("act", tabd[jj], tab[:].rearrange("p c k t -> p (c k t)"), "tabst", reads=["tab"],
              writes=[("tabd", jj)])

    def odd_setup(j):
        AR.reset()
        small_dma(pools[:, j, :], din(f"pools{j}"), [("pools", j)])
        pw = AR.take(512)
        small_dma(pw.ap.rearrange("p (g d) -> p g d", g=4), din(f"poolw{j}"), R(pw))
        cp("dve", poolw[:, j].rearrange("p g d -> p (g d)"), pw.ap, R(pw), [("poolw", j)])
        sw, tr = AR.take(512), AR.take(128)
        small_dma(sw.ap.rearrange("p (g d) -> p g d", g=4), din(f"sgw{j}"), R(sw))
        small_dma(tr.ap, din("trilh"), R(tr))
        tt("dve", wT[:, j], sw.ap.rearrange("p (g d) -> p g d", g=4),
           tr.ap.unsqueeze(1).to_broadcast([128, 4, 128]), ALU.mult, R(sw, tr), [("wT", j)])
        sb2 = AR.take(512)
        small_dma(sb2.ap.rearrange("p (g d) -> p g d", g=4), din(f"sgb{j}"), R(sb2))
        ts("dve", bh[:, j].rearrange("p g d -> p (g d)"), sb2.ap, 0.5, None, ALU.mult, None, R(sb2),
           [("bh", j)])
        gg = AR.take(512)
        small_dma(gg.ap, din(f"sgng{j}"), R(gg))
        ts("dve", ghbc[:, j, :], gg.ap, 0.5, None, ALU.mult, None, R(gg), [("ghbc", j)])

    for jj, j in enumerate(even_js):
        ssm_setup(j, jj)
    for j in odd_js:
        odd_setup(j)
    tab_state = {"j": even_js[-1] if even_js else None}

    dq_ = []

    def later(delay, fn):
        dq_.append([delay, fn])

    def tick():
        due = [x for x in dq_ if x[0] <= 0]
        for x in due:
            dq_.remove(x)
        for x in dq_:
            x[0] -= 1
        for x in due:
            x[1]()

    def flush():
        while dq_:
            tick()

    def sub(v, a, b):
        return V(v.ap[:, a:b], ("@ar", v.reg[1] + a, v.reg[1] + b))

    def xr_(kt):
        return V(xres[:, kt, :], ("xres", kt))

    def hb_(kt):
        return V(hb[:, kt, :], ("hb", kt))

    def rmsnorm(gap, gkey, outs, bf=True):
        AR.reset()
        sq = [AR.take(256, BF16) for _ in range(8)]
        ms4, rs4 = AR.take(4), AR.take(4)
        dg = [AR.take(128) for _ in range(4)]
        bk = nb("main8")
        for kt in range(8):
            act(sq[kt].ap, xres[:, kt, :], AF.Square, [("xres", kt)], R(sq[kt]))
        for tb_ in range(4):
            for kt in range(8):
                mm(banks[bk][:, tb_:tb_ + 1], sq[kt].ap[:, tb_ * 128:(tb_ + 1) * 128], ones[:, 0:1],
                   kt == 0, kt == 7, ["ones"] + R(sq[kt]), [("bank", bk)])
        ts("dve", ms4.ap, banks[bk][:, 0:4], 1.0 / D, EPS, ALU.mult, ALU.add, [("bank", bk)], R(ms4))
        tt("pool", rs4.ap, ms4.ap, cst[:, 0:1].to_broadcast([128, 4]), ALU.pow, R(ms4) + ["cst"], R(rs4))
        bk2 = nb("main8")
        for tb_ in range(4):
            act(dg[tb_].ap, ident[:], AF.Identity, R(rs4) + ["ident"], R(dg[tb_]), scale=rs4.ap[:, tb_:tb_ + 1])
            mm(banks[bk2][:, tb_ * 128:(tb_ + 1) * 128], onesf[:], dg[tb_].ap, True, True,
               ["onesf"] + R(dg[tb_]), [("bank", bk2)])
        for kt in range(8):
            stt(outs[kt].ap, xres[:, kt, :], gap[:, kt:kt + 1], banks[bk2][:, :], ALU.mult, ALU.mult,
                [("xres", kt), gkey, ("bank", bk2)], R(outs[kt]))

    def proj_chunk(Wv, col, pool, rhs_list, KT=8):
        bk = nb(pool)
        for kt in range(KT):
            mm(banks[bk][:, :], Wv.ap[:, kt, col:col + 128], rhs_list[kt].ap, kt == 0, kt == KT - 1,
               [Wv.reg] + R(rhs_list[kt]), [("bank", bk)])
        return bk

    def out_proj(ycat, pool):
        for half in range(2):
            Wo = next_slab()
            for m_ in range(4):
                m = half * 4 + m_
                bk = proj_chunk(Wo, m_ * 128, pool, ycat)
                tt("dve", xres[:, m, :], xres[:, m, :], banks[bk][:, :], ALU.add,
                   [("xres", m), ("bank", bk)], [("xres", m)])

    def gelu2(bk, tb, Gout):
        cp("act", tb["pc"].ap, banks[bk][:, :], [("bank", bk)], R(tb["pc"]))
        act(tb["p2"].ap, banks[bk][:, :], AF.Square, [("bank", bk)], R(tb["p2"]))
        ts("pool", tb["ti"].ap, tb["p2"].ap, 0.044715, 1.0, ALU.mult, ALU.add, R(tb["p2"]), R(tb["ti"]))
        tt("pool", tb["ti"].ap, tb["ti"].ap, tb["pc"].ap, ALU.mult, R(tb["ti"], tb["pc"]), R(tb["ti"]))
        act(tb["th"].ap, tb["ti"].ap, AF.Tanh, R(tb["ti"]), R(tb["th"]), scale=0.7978845608028654)
        stt(Gout.ap, tb["th"].ap, 1.0, tb["pc"].ap, ALU.add, ALU.mult, R(tb["th"], tb["pc"]), R(Gout))

    def even_mixer(i, t):
        j = i // 2
        hbl = [hb_(kt) for kt in range(8)]
        AR.reset()
        xa = [AR.take(512) for _ in range(4)]
        cxb = [AR.take(514) for _ in range(4)]
        ycat = [AR.take(256, BF16) for _ in range(8)]
        u = [AR.take(512) for _ in range(4)]
        ub = [AR.take(256, BF16) for _ in range(4)]
        ssmb = [dict(bu=AR.take(512), m34=AR.take(512), q=AR.take(512),
                     X=AR.take(256, BF16), Y=AR.take(256, BF16)) for _ in range(3)]
        tmpb = [dict(yq=AR.take(256), y2=AR.take(256), ti=AR.take(256), th=AR.take(256)) for _ in range(2)]
        rt = [AR.take(16) for _ in range(4)]
        G = xa
        base = cxb[0].reg[1]
        gelu_bf = [V(arena[:, base + 256 * b:base + 256 * (b + 1)].bitcast(BF16),
                     ("@ar", base + 256 * b, base + 256 * (b + 1))) for b in range(4)]
        th2 = [V(arena[:, base + 1024 + 512 * x:base + 1024 + 512 * (x + 1)],
                 ("@ar", base + 1024 + 512 * x, base + 1024 + 512 * (x + 1))) for x in range(2)]
        if tab_state["j"] != j:
            jj = even_js.index(j)
            P.dma("sp", tab[:].rearrange("p c k t -> p (c k t)"), tabd[jj], "tabld",
                  reads=[("tabd", jj)], writes=["tab"])
            tab_state["j"] = j
        for sl in range(2):
            W = next_slab()
            for ii in range(2):
                c = 2 * sl + ii
                bx = proj_chunk(W, ii * 128, "main4", hbl)
                cp("act", xa[c].ap, banks[bx][:, :], [("bank", bx)], R(xa[c]))
                bc_ = proj_chunk(W, 256 + ii * 128, "main4", hbl)
                cxh, cxm = sub(cxb[c], 0, 2), sub(cxb[c], 2, 514)
                cp("pool", cxh.ap, cxhist[:, j, c, :], [("cxhist", j, c)], R(cxh))
                tt("dve", cxm.ap, banks[bc_][:, :], xa[c].ap, ALU.mult, [("bank", bc_)] + R(xa[c]), R(cxm))
                ek = ("econv", j)
                act(xa[c].ap, cxm.ap, AF.Identity, R(cxm) + [ek], R(xa[c]), scale=econv[:, j, c, 2:3])
                stt(xa[c].ap, cxb[c].ap[:, 1:513], econv[:, j, c, 1:2], xa[c].ap, ALU.mult, ALU.add,
                    R(cxb[c], xa[c]) + [ek], R(xa[c]))
                stt(xa[c].ap, cxb[c].ap[:, 0:512], econv[:, j, c, 0:1], xa[c].ap, ALU.mult, ALU.add,
                    R(cxb[c], xa[c]) + [ek], R(xa[c]))
                cp("pool", cxhist[:, j, c, :], cxb[c].ap[:, 512:514], R(cxm), [("cxhist", j, c)])
        W = next_slab()
        for b in range(4):
            bk = proj_chunk(W, b * 128, "main4", hbl)
            cp("act", u[b].ap, banks[bk][:, :], [("bank", bk)], R(u[b]))
            cp("pool", ub[b].ap, u[b].ap, R(u[b]), R(ub[b]))
        W = next_slab()
        for c in range(4):
            bk = proj_chunk(W, c * 128, "main4", hbl)
            tt("dve", ycat[c].ap, banks[bk][:, :], xa[c].ap, ALU.mult, [("bank", bk)] + R(xa[c]), R(ycat[c]))
        items = [(sb_i, b, q) for sb_i in range(T // TS) for b in range(4) for q in range(4)]
        v3 = lambda v: v.ap.rearrange("p (c t) -> p c t", c=2)
        stA = {}

        def stageA(n):
            sb_i, b, q = items[n]
            c0 = sb_i * TS
            S = ssmb[n % 3]
            bb = nb("bu")
            for comp in range(2):
                mm(banks[bb][:, comp * TS:(comp + 1) * TS], Bl[32 * q:32 * q + 32, j, b, comp, :],
                   ub[b].ap[32 * q:32 * q + 32, c0:c0 + TS], True, True,
                   [("Bl", j)] + R(ub[b]), [("bank", bb)], tp=(32 * q, 0))
            cp("act", S["bu"].ap, banks[bb][:, :], [("bank", bb)], R(S["bu"]))

        def stageB(n, yk):
            sb_i, b, q = items[n]
            k = 4 * b + q
            S = ssmb[n % 3]
            cbc = tab[:, 0, k, :].unsqueeze(1).to_broadcast([128, 2, TS])
            sbc = tab[:, 1, k, :].unsqueeze(1).to_broadcast([128, 2, TS])
            tt("pool", v3(S["m34"]), v3(S["bu"]), sbc, ALU.mult, R(S["bu"]) + ["tab"], R(S["m34"]))
            tt("pool", v3(S["bu"]), v3(S["bu"]), cbc, ALU.mult, R(S["bu"]) + ["tab"], R(S["bu"]))
            mre, mim = sub(S["bu"], 0, TS), sub(S["bu"], TS, 2 * TS)
            tt("dve", mre.ap, mre.ap, S["m34"].ap[:, TS:2 * TS], ALU.add, R(mre, S["m34"]), R(mre))
            tt("dve", mim.ap, mim.ap, S["m34"].ap[:, 0:TS], ALU.subtract, R(mim, S["m34"]), R(mim))
            rb = rdec[:, j, k:k + 1].to_broadcast([128, TS])
            for comp, mv in ((0, mre), (1, mim)):
                qv = sub(S["q"], comp * TS, (comp + 1) * TS)
                P.op("dve", lambda e, qv=qv, rb=rb, mv=mv, comp=comp, k=k: e.tensor_tensor_scan(
                    qv.ap, rb, mv.ap, qinit[:, j, comp, k:k + 1], ALU.mult, ALU.add),
                    R(mv) + [("rdec", j), ("qinit", j)], R(qv))
            Xv = S["X"].ap.rearrange("p (c t) -> p c t", c=2)
            Yv = S["Y"].ap.rearrange("p (c t) -> p c t", c=2)
            tt("dve", Xv, v3(S["q"]), cbc, ALU.mult, R(S["q"]) + ["tab"], R(S["X"]))
            tt("dve", Yv, v3(S["q"]), sbc, ALU.mult, R(S["q"]) + ["tab"], R(S["Y"]))
            cp("pool", qend[:, :, k:k + 1], v3(S["q"])[:, :, TS - 1:TS], R(S["q"]), [("qend", k)])
            yo = banks[yk][32 * q:32 * q + 32, 0:TS]
            ops_ = ((0, S["X"], 0), (1, S["Y"], 1), (2, S["Y"], 0), (2, S["X"], 1))
            for n_, (m_, src, half) in enumerate(ops_):
                mm(yo, Cl[:, j, k, m_, :], src.ap[:, half * TS:(half + 1) * TS], n_ == 0, n_ == 3,
                   [("Cl", j)] + R(src), [("bank", yk)], tp=(0, 32 * q))

        def block_epilogue(sb_i, b, yk, ib):
            c0 = sb_i * TS
            tb = tmpb[ib % 2]
            Gs = sub(G[b], c0, c0 + TS)
            gb = V(gelu_bf[b].ap[:, c0:c0 + TS], ("@ar", gelu_bf[b].reg[1] + c0 // 2,
                                                 gelu_bf[b].reg[1] + (c0 + TS) // 2))

            def e1():
                stt(tb["yq"].ap, u[b].ap[:, c0:c0 + TS], dq[:, j, b:b + 1], banks[yk][:, 0:TS], ALU.mult, ALU.add,
                    R(u[b]) + [("dq", j), ("bank", yk)], R(tb["yq"]))
                act(tb["y2"].ap, tb["yq"].ap, AF.Square, R(tb["yq"]), R(tb["y2"]), scale=4.0)

            def e2():
                ts("pool", tb["ti"].ap, tb["y2"].ap, 4 * 0.044715, 4.0, ALU.mult, ALU.add, R(tb["y2"]), R(tb["ti"]))
                tt("pool", tb["ti"].ap, tb["ti"].ap, tb["yq"].ap, ALU.mult, R(tb["ti"], tb["yq"]), R(tb["ti"]))

            def e3():
                act(tb["th"].ap, tb["ti"].ap, AF.Tanh, R(tb["ti"]), R(tb["th"]), scale=0.7978845608028654)

            def e4():
                stt(Gs.ap, tb["th"].ap, 1.0, tb["yq"].ap, ALU.add, ALU.mult, R(tb["th"], tb["yq"]), R(Gs))
                act(gb.ap, Gs.ap, AF.Copy, R(Gs), R(gb), scale=2.0)

            later(0, e1)
            later(1, e2)
            later(2, e3)
            later(3, e4)

        def carry():
            qk = [("qend", k) for k in range(16)]
            cT, sT = rotc[:, j, 0, :], rotc[:, j, 1, :]
            rk = [("rotc", j)]
            tt("dve", rt[0].ap, cT, qend[:, 0, :], ALU.mult, qk + rk, R(rt[0]))
            tt("dve", rt[1].ap, sT, qend[:, 1, :], ALU.mult, qk + rk, R(rt[1]))
            tt("dve", rt[2].ap, sT, qend[:, 0, :], ALU.mult, qk + rk, R(rt[2]))
            tt("dve", rt[3].ap, cT, qend[:, 1, :], ALU.mult, qk + rk, R(rt[3]))
            tt("dve", qinit[:, j, 0, :], rt[0].ap, rt[1].ap, ALU.subtract, R(rt[0], rt[1]), [("qinit", j)])
            tt("dve", qinit[:, j, 1, :], rt[2].ap, rt[3].ap, ALU.add, R(rt[2], rt[3]), [("qinit", j)])

        NI = len(items)
        stageA(0)
        stageA(1)
        yk = None
        ib = 0
        for n in range(NI):
            sb_i, b, q = items[n]
            if n + 2 < NI:
                stageA(n + 2)
            if q == 0:
                yk = nb("yb")
            stageB(n, yk)
            tick()
            if q == 3:
                block_epilogue(sb_i, b, yk, ib)
                ib += 1
                if b == 3:
                    carry()
        flush()
        Wg = next_slab()
        for c in range(4):
            bk = proj_chunk(Wg, c * 128, "main4", gelu_bf, KT=4)
            tv = th2[c % 2]
            act(tv.ap, banks[bk][:, :], AF.Tanh, [("bank", bk), ("gbh", j)], R(tv), bias=gbh[:, j, c:c + 1], scale=0.5)
            stt(ycat[4 + c].ap, tv.ap, 1.0, G[c].ap, ALU.add, ALU.mult, R(tv, G[c]), R(ycat[4 + c]))
        out_proj(ycat, "main4")

    def ffn(i, t):
        hbl = [hb_(kt) for kt in range(8)]
        AR.reset(3072)
        hid = [AR.take(256, BF16) for _ in range(22)]
        acc = [AR.take(512) for _ in range(8)]
        sg = [AR.take(512) for _ in range(4)]
        for k in range(11):
            W = next_slab()
            for jj in range(2):
                jch = 2 * k + jj
                chs = [jch, jch + 22]
                bks = [proj_chunk(W, gv * 256 + jj * 128, "main8", hbl) for gv in range(2)]
                accs = [acc[2 * (jch % 4) + gv] for gv in range(2)]
                wk = [("fcw", i)]
                w_ = lambda ch, kk: fcw[:, i, ch, kk:kk + 1]
                for gv in range(2):
                    act(accs[gv].ap, banks[bks[gv]][:, :], AF.Identity, [("bank", bks[gv]), ("fcb", i)] + wk,
                        R(accs[gv]), bias=fcb[:, i, chs[gv]:chs[gv] + 1], scale=w_(chs[gv], 2))
                for gv in range(2):
                    a, bk, ch = accs[gv], bks[gv], chs[gv]
                    stt(a.ap[:, 1:T], banks[bk][:, 0:T - 1], w_(ch, 1), a.ap[:, 1:T], ALU.mult, ALU.add,
                        [("bank", bk)] + wk + R(a), R(a))
                for gv in range(2):
                    a, bk, ch = accs[gv], bks[gv], chs[gv]
                    stt(a.ap[:, 0:1], fhist[:, i, ch, 1:2], w_(ch, 1), a.ap[:, 0:1], ALU.mult, ALU.add,
                        [("fhist", i, ch)] + wk + R(a), R(a))
                for gv in range(2):
                    a, bk, ch = accs[gv], bks[gv], chs[gv]
                    stt(a.ap[:, 2:T], banks[bk][:, 0:T - 2], w_(ch, 0), a.ap[:, 2:T], ALU.mult, ALU.add,
                        [("bank", bk)] + wk + R(a), R(a))
                for gv in range(2):
                    a, bk, ch = accs[gv], bks[gv], chs[gv]
                    stt(a.ap[:, 0:2], fhist[:, i, ch, 0:2], w_(ch, 0), a.ap[:, 0:2], ALU.mult, ALU.add,
                        [("fhist", i, ch)] + wk + R(a), R(a))
                def fin(chs=chs, bks=bks, accs=accs, s_=sg[jch % 4], jch=jch):
                    for gv in range(2):
                        cp("act", fhist[:, i, chs[gv], :], banks[bks[gv]][:, T - 2:T], [("bank", bks[gv])],
                           [("fhist", i, chs[gv])])
                    act(s_.ap, accs[0].ap, AF.Silu, R(accs[0]), R(s_))
                    tt("pool", hid[jch].ap, s_.ap, accs[1].ap, ALU.mult, R(s_, accs[1]), R(hid[jch]))

                tick()
                later(0, fin)
        flush()
        for m in range(8):
            Wd = next_slab()
            bk = nb("main8")
            for jch in range(22):
                mm(banks[bk][:, :], Wd.ap[:, jch, :], hid[jch].ap, jch == 0, jch == 21,
                   [Wd.reg] + R(hid[jch]), [("bank", bk)])
            tt("dve", xres[:, m, :], xres[:, m, :], banks[bk][:, :], ALU.add,
               [("xres", m), ("bank", bk)], [("xres", m)])

    def odd_mixer(i, t):
        j = i // 2
        hbl = [hb_(kt) for kt in range(8)]
        AR.reset()
        zt = [AR.take(528) for _ in range(4)]
        sA, sB = AR.take(528), AR.take(528)
        pooled = [AR.take(256, BF16) for _ in range(4)]
        Gu = [AR.take(512) for _ in range(4)]
        tmpb = [dict(pc=AR.take(512), p2=AR.take(512), ti=AR.take(512), th=AR.take(512)) for _ in range(2)]
        Gv = [AR.take(512) for _ in range(2)]
        vtok = [AR.take(256, BF16) for _ in range(4)]
        ycat = [AR.take(256, BF16) for _ in range(8)]
        sgt = [AR.take(512) for _ in range(2)]
        sml = [dict(ss=AR.take(1), ms=AR.take(1), rs=AR.take(1)) for _ in range(2)]
        t15 = AR.take(16)
        W = next_slab()
        for g, w in enumerate((2, 4, 8, 16)):
            bk = proj_chunk(W, g * 128, "main8", hbl)
            zh, zm = sub(zt[g], 0, 16), sub(zt[g], 16, 528)
            cp("pool", zh.ap, zhist[:, j, g, :], [("zhist", j, g)], R(zh))
            cp("act", zm.ap, banks[bk][:, :], [("bank", bk)], R(zm))
            z = zt[g]
            tt("pool", sA.ap[:, 1:528], z.ap[:, 1:528], z.ap[:, 0:527], ALU.add, R(z), R(sA))
            Sv = sA
            if g >= 1:
                tt("pool", sB.ap[:, 3:528], sA.ap[:, 3:528], sA.ap[:, 1:526], ALU.add, R(sA), R(sB))
                Sv = sB
            if g >= 2:
                tt("pool", sA.ap[:, 7:528], sB.ap[:, 7:528], sB.ap[:, 3:524], ALU.add, R(sB), R(sA))
                Sv = sA
            if g >= 3:
                tt("pool", sB.ap[:, 15:528], sA.ap[:, 15:528], sA.ap[:, 7:520], ALU.add, R(sA), R(sB))
                Sv = sB
            stt(pooled[g].ap, Sv.ap[:, 16:528], 1.0 / w, zm.ap, ALU.mult, ALU.subtract, R(Sv, zm), R(pooled[g]))
            if t == 0:
                tt("dve", t15.ap[:, 0:15], Sv.ap[:, 16:31], invc[:, g, 0:15], ALU.mult, R(Sv) + ["invc"], R(t15))
                tt("dve", pooled[g].ap[:, 0:15], t15.ap[:, 0:15], zt[g].ap[:, 16:31], ALU.subtract,
                   R(t15, zm, pooled[g]), R(pooled[g]))
            cp("pool", zhist[:, j, g, :], zt[g].ap[:, 512:528], R(zm), [("zhist", j, g)])
            bk2 = nb("main8")
            mm(banks[bk2][:, :], poolw[:, j, g, :], pooled[g].ap, True, True, [("poolw", j)] + R(pooled[g]),
               [("bank", bk2)])
            act(ycat[g].ap, banks[bk2][:, :], AF.Identity, [("bank", bk2), ("pools", j)], R(ycat[g]),
                scale=pools[:, j, g:g + 1])
        W = next_slab()
        for c in range(4):
            bk = proj_chunk(W, c * 128, "main8", hbl)
            gelu2(bk, tmpb[c % 2], Gu[c])
        W = next_slab()
        for tb_ in range(4):
            bk = nb("main8")
            for kt in range(8):
                mm(banks[bk][:, :], hb[:, kt, tb_ * 128:(tb_ + 1) * 128], W.ap[:, kt, :], kt == 0, kt == 7,
                   [W.reg, ("hb", kt)], [("bank", bk)])
            tb = tmpb[tb_ % 2]
            gv = Gv[tb_ % 2]
            sm_ = sml[tb_ % 2]
            gelu2(bk, tb, gv)
            act(tb["p2"].ap, gv.ap, AF.Square, R(gv), R(tb["p2"], sm_["ss"]), scale=0.5, accum=sm_["ss"].ap)
            ts("dve", sm_["ms"].ap, sm_["ss"].ap, 1.0 / 512, EPS, ALU.mult, ALU.add, R(sm_["ss"]), R(sm_["ms"]))
            tt("pool", sm_["rs"].ap, sm_["ms"].ap, cst[:, 0:1], ALU.pow, R(sm_["ms"]) + ["cst"], R(sm_["rs"]))
            stt(vtok[tb_].ap, gv.ap, sm_["rs"].ap, ghbc[:, j, :], ALU.mult, ALU.mult,
                R(gv, sm_["rs"]) + [("ghbc", j)], R(vtok[tb_]))
        for hd in range(4):
            bk = nb("main8")
            for tb_ in range(4):
                mm(banks[bk][:, tb_ * 128:(tb_ + 1) * 128], vtok[tb_].ap[:, hd * 128:(hd + 1) * 128],
                   wT[:, j, hd, :], True, True, [("wT", j)] + R(vtok[tb_]), [("bank", bk)])
            sv_ = sgt[hd % 2]
            tt("dve", sv_.ap.rearrange("p (a b) -> p a b", a=4),
               banks[bk][:, :].rearrange("p (a b) -> p a b", a=4),
               bh[:, j, hd, :].unsqueeze(1).to_broadcast([128, 4, 128]), ALU.add,
               [("bank", bk), ("bh", j)], R(sv_))
            tt("dve", ycat[4 + hd].ap, sv_.ap, Gu[hd].ap, ALU.mult, R(sv_, Gu[hd]), R(ycat[4 + hd]))
        out_proj(ycat, "main8")

    for t in range(ntiles):
        t0 = t * T
        P.dma("sp", xres[:, :, :], xTv[:, :, t0:t0 + T], "xload", writes=[("xres", kt) for kt in range(8)])
        for i in layers:
            j = i // 2
            rmsnorm(gmix[:, i, :], "gmix", [hb_(kt) for kt in range(8)])
            if i % 2 == 0:
                even_mixer(i, t)
            else:
                odd_mixer(i, t)
            rmsnorm(gffn[:, i, :], "gffn", [hb_(kt) for kt in range(8)])
            ffn(i, t)
        if final:
            AR.reset(3072)
            ost = [AR.take(512) for _ in range(8)]
            p_after = AR.p
            rmsnorm(gfin[:, :], "gfin", ost)
            AR.reset(p_after)
            ov = arena[:, ost[0].reg[1]:ost[0].reg[1] + 4096].rearrange("p (k t) -> p k t", k=8)
            P.dma("act", outTv[:, :, t0:t0 + T], ov, "ostore", reads=R(*ost), writes=[("outT", t)])
        else:
            P.dma("act", outTv[:, :, t0:t0 + T], xres[:, :, :], "ostore",
                  reads=[("xres", kt) for kt in range(8)], writes=[("outT", t)])
    P.wait("act", [("outT", t) for t in range(ntiles)])
    P.emit()
    P.close()
    names = ["xT"] + list(dr.keys())
    return nc, names, dbg_out


_CACHE = {}


def _get(layers, ntiles, final):
    key = (tuple(layers), ntiles, final)
    if key not in _CACHE:
        _CACHE[key] = build(layers, ntiles, final)
    return _CACHE[key]


def kernel(**inputs):
    com = prep_common(inputs)
    x = np.asarray(inputs["x"], dtype=np.float32)
    nb_ = x.shape[0]
    xTs = [np.ascontiguousarray(x[b].T) for b in range(nb_)]
    nc, names, _ = _get((0, 1, 2, 3), SEQ // T, True)
    in_maps = []
    for b in range(nb_):
        m = {"xT": xTs[b]}
        for n in names:
            if n != "xT":
                m[n] = com[n]
        in_maps.append(m)
    res = run_bass_kernel_spmd(nc, in_maps, core_ids=list(range(nb_)))
    out = np.stack([np.ascontiguousarray(res.results[b]["outT"].T) for b in range(nb_)], axis=0)
    return out.astype(np.float32)
```

```python
import contextlib
import numpy as np
import concourse.bass as bass
import concourse.mybir as mybir
from concourse.bass_utils import run_bass_kernel_spmd

F32 = mybir.dt.float32
BF16 = mybir.dt.bfloat16
ALU = mybir.AluOpType
AF = mybir.ActivationFunctionType

D = 1024
SEQ = 4096
DEPTH = 4
DFF = 2816
T = 512
TS = 256
NSLOT = 4
EPS = 1e-6
ENGS = ("pe", "act", "dve", "pool", "sp")


class Op:
    __slots__ = ("eng", "fn", "deps", "signal", "sigval", "sem", "is_dma", "group")

    def __init__(self, eng, fn):
        self.eng = eng
        self.fn = fn
        self.deps = []
        self.signal = False
        self.sigval = 0
        self.sem = None
        self.is_dma = False
        self.group = None


class Prog:
    def __init__(self, nc):
        self.nc = nc
        self.ops = {e: [] for e in ENGS}
        self.state = {}
        self.stack = contextlib.ExitStack()
        self.dma_groups = {}
        self.n = 0

    def sb(self, shape, dtype=F32, name=None):
        self.n += 1
        return self.stack.enter_context(self.nc.sbuf_tensor(name or f"sb{self.n}", list(shape), dtype))

    def ps(self, shape, dtype=F32, name=None):
        self.n += 1
        return self.stack.enter_context(self.nc.psum_tensor(name or f"ps{self.n}", list(shape), dtype))

    @staticmethod
    def _norm(k):
        if isinstance(k, tuple) and len(k) == 3 and isinstance(k[0], str) and k[0].startswith("@"):
            return k[0], int(k[1]), int(k[2])
        return k, 0, 1

    def _segs(self, ns, lo, hi):
        L = self.state.setdefault(ns, [])
        out = []
        newL = []
        cur = lo
        for sg in L:
            a, b, w, r = sg
            if b <= lo or a >= hi:
                newL.append(sg)
                continue
            if a < lo:
                newL.append([a, lo, w, list(r)])
                a = lo
            if b > hi:
                newL.append([hi, b, w, list(r)])
                b = hi
            mid = [a, b, w, r]
            newL.append(mid)
            out.append(mid)
        out.sort(key=lambda x: x[0])
        filled = []
        for sg in out:
            if sg[0] > cur:
                g = [cur, sg[0], None, []]
                newL.append(g)
                filled.append(g)
            filled.append(sg)
            cur = sg[1]
        if cur < hi:
            g = [cur, hi, None, []]
            newL.append(g)
            filled.append(g)
        self.state[ns] = newL
        return filled

    def _track(self, o, reads, writes, skip_same_eng=False):
        deps = []
        rn = [self._norm(k) for k in reads]
        wn = [self._norm(k) for k in writes]
        for ns, lo, hi in rn:
            for sg in self._segs(ns, lo, hi):
                if sg[2] is not None:
                    deps.append(sg[2])
        for ns, lo, hi in wn:
            for sg in self._segs(ns, lo, hi):
                if sg[2] is not None:
                    deps.append(sg[2])
                last = {}
                for r in sg[3]:
                    if r.is_dma:
                        deps.append(r)
                    else:
                        last[r.eng] = r
                deps.extend(last.values())
        for ns, lo, hi in rn:
            for sg in self._segs(ns, lo, hi):
                sg[3].append(o)
        for ns, lo, hi in wn:
            segs = self._segs(ns, lo, hi)
            L = self.state[ns]
            for sg in segs:
                L.remove(sg)
            L.append([lo, hi, o, []])
        seen = set()
        for d in deps:
            if d is o or id(d) in seen:
                continue
            if skip_same_eng and (not d.is_dma) and d.eng == o.eng:
                continue
            seen.add(id(d))
            o.deps.append(d)

    def op(self, eng, fn, reads=(), writes=()):
        o = Op(eng, fn)
        self._track(o, reads, writes, skip_same_eng=(eng == "pe"))
        self.ops[eng].append(o)
        return o

    def dma(self, eng, out, in_, group, reads=(), writes=(), **kw):
        o = Op(eng, lambda e: e.dma_start(out=out, in_=in_, **kw))
        o.is_dma = True
        o.group = group
        self._track(o, reads, writes)
        lst = self.dma_groups.setdefault(group, [])
        if lst and not group.startswith("all:") and lst[-1] not in o.deps:
            o.deps.append(lst[-1])
        self.ops[eng].append(o)
        lst.append(o)
        return o

    def wait(self, eng, reads):
        o = Op(eng, lambda e: None)
        self._track(o, reads, ())
        self.ops[eng].append(o)
        return o

    def emit(self):
        nc = self.nc
        for e in ENGS:
            for o in self.ops[e]:
                for d in o.deps:
                    d.signal = True
        sems = {}
        for e in ENGS:
            sems[e] = self.stack.enter_context(nc.semaphore(f"s_{e}"))
            c = 0
            for o in self.ops[e]:
                if o.is_dma:
                    continue
                if o.signal:
                    c += 1
                    o.sigval = c
                    o.sem = sems[e]
        for g, lst in self.dma_groups.items():
            s = self.stack.enter_context(nc.semaphore(f"d_{len(sems)}"))
            sems["dma:" + g] = s
            if g.startswith("all:"):
                for o in lst:
                    o.sem = s
                    o.sigval = 16 * len(lst)
            else:
                for i, o in enumerate(lst):
                    o.sem = s
                    o.sigval = 16 * (i + 1)
        engmap = {"pe": "tensor", "act": "scalar", "dve": "vector", "pool": "gpsimd", "sp": "sync"}
        with nc.Block() as block:
            for e in ENGS:
                ops = self.ops[e]
                if not ops:
                    continue

                def body(eng, ops=ops):
                    waited = {}
                    for o in ops:
                        need = {}
                        for d in o.deps:
                            k = id(d.sem)
                            if d.sigval > need.get(k, (None, 0))[1]:
                                need[k] = (d.sem, d.sigval)
                        for k, (s, v) in need.items():
                            if waited.get(k, 0) >= v:
                                continue
                            eng.wait_ge(s, v)
                            waited[k] = v
                        inst = o.fn(eng)
                        if inst is None:
                            continue
                        if o.is_dma:
                            inst.then_inc(o.sem, 16)
                        elif o.signal:
                            inst.then_inc(o.sem, 1)

                getattr(block, engmap[e])(body)

    def close(self):
        self.stack.close()


def _pair(a):
    a = np.asarray(a, dtype=np.float32)
    rest = a.shape[2:]
    a = a.reshape((16, 2, 64) + rest)
    perm = (1, 2, 0) + tuple(range(3, 3 + len(rest)))
    return np.ascontiguousarray(a.transpose(perm).reshape((128, 16) + rest))


def _cols(v, n):
    return np.ascontiguousarray(np.asarray(v, dtype=np.float32).reshape(n, 128).T)


def prep_common(inp):
    f = lambda a: np.ascontiguousarray(np.asarray(a, dtype=np.float32))
    com = {}
    com["gmix"] = f(np.asarray(inp["norm_mix_g"]).reshape(4, 8, 128).transpose(2, 0, 1))
    com["gffn"] = f(np.asarray(inp["norm_ffn_g"]).reshape(4, 8, 128).transpose(2, 0, 1))
    com["gfin"] = _cols(inp["norm_final_g"], 8)
    for j in range(2):
        com[f"ewin{j}"] = f(inp["even_w_in"][j])
        com[f"econv{j}"] = f(np.asarray(inp["even_conv_w"][j]).reshape(3, 4, 128).transpose(2, 1, 0))
        ls = np.broadcast_to(np.asarray(inp["ssm_log_step"][j])[:, None], (32, 64))
        com[f"sls{j}"] = _pair(ls)
        com[f"sare{j}"] = _pair(inp["ssm_a_re"][j])
        com[f"saim{j}"] = _pair(inp["ssm_a_im"][j])
        com[f"sbre{j}"] = _pair(inp["ssm_b_re"][j])
        com[f"sbim{j}"] = _pair(inp["ssm_b_im"][j])
        com[f"scre{j}"] = _pair(np.asarray(inp["ssm_c_re"][j]).transpose(0, 2, 1))
        com[f"scim{j}"] = _pair(np.asarray(inp["ssm_c_im"][j]).transpose(0, 2, 1))
        com[f"sd{j}"] = _cols(inp["ssm_d"][j], 4)
        com[f"glw{j}"] = f(inp["ssm_glu_w"][j])
        com[f"glb{j}"] = _cols(inp["ssm_glu_b"][j], 4)
        com[f"ewout{j}"] = f(inp["even_w_out"][j])
        com[f"owin{j}"] = f(inp["odd_w_in"][j])
        com[f"poolw{j}"] = f(np.asarray(inp["pool_w"][j]).transpose(1, 0, 2))
        com[f"pools{j}"] = _cols(inp["pool_scale"][j], 4)
        com[f"sgng{j}"] = f(np.broadcast_to(np.asarray(inp["sgu_norm_g"][j])[None, :], (128, 512)))
        com[f"sgw{j}"] = f(np.asarray(inp["sgu_w"][j]).transpose(2, 0, 1))
        com[f"sgb{j}"] = f(np.broadcast_to(np.asarray(inp["sgu_b"][j])[None], (128, 4, 128)))
        com[f"owout{j}"] = f(inp["odd_w_out"][j])
    for i in range(4):
        com[f"fup{i}"] = f(inp["ffn_w_up"][i])
        com[f"fcw{i}"] = f(np.asarray(inp["ffn_conv_w"][i]).reshape(3, 44, 128).transpose(2, 1, 0))
        com[f"fcb{i}"] = _cols(inp["ffn_conv_b"][i], 44)
        com[f"fdn{i}"] = f(inp["ffn_w_down"][i])
    s = np.arange(128)
    com["trilh"] = f(0.5 * (s[:, None] <= s[None, :]))
    invc = np.zeros((128, 4, 16), np.float32)
    for g, w in enumerate((2, 4, 8, 16)):
        invc[:, g, :] = 1.0 / np.minimum(np.arange(1, 17), w)
    com["invc"] = invc
    com["ident"] = f(np.eye(128))
    return com


SHAPES = {
    "gmix": [128, 4, 8], "gffn": [128, 4, 8], "gfin": [128, 8],
    "trilh": [128, 128], "invc": [128, 4, 16], "ident": [128, 128],
}
for _j in range(2):
    SHAPES.update({
        f"ewin{_j}": [1024, 2048], f"econv{_j}": [128, 4, 3], f"sls{_j}": [128, 16],
        f"sare{_j}": [128, 16], f"saim{_j}": [128, 16], f"sbre{_j}": [128, 16, 16],
        f"sbim{_j}": [128, 16, 16], f"scre{_j}": [128, 16, 16], f"scim{_j}": [128, 16, 16],
        f"sd{_j}": [128, 4], f"glw{_j}": [512, 512], f"glb{_j}": [128, 4], f"ewout{_j}": [1024, 1024],
        f"owin{_j}": [1024, 1536], f"poolw{_j}": [128, 4, 128], f"pools{_j}": [128, 4],
        f"sgng{_j}": [128, 512], f"sgw{_j}": [128, 4, 128], f"sgb{_j}": [128, 4, 128],
        f"owout{_j}": [1024, 1024],
    })
for _i in range(4):
    SHAPES.update({f"fup{_i}": [1024, 5632], f"fcw{_i}": [128, 44, 3], f"fcb{_i}": [128, 44],
                   f"fdn{_i}": [2816, 1024]})


class V:
    __slots__ = ("ap", "reg")

    def __init__(self, ap, reg):
        self.ap = ap
        self.reg = reg


def build(layers=(0, 1, 2, 3), ntiles=8, final=True, dbg=()):
    nc = bass.Bass("TRN2", target_bir_lowering=False)
    P = Prog(nc)
    dr = {}
    dbg_out = {}

    def din(name):
        if name not in dr:
            dr[name] = nc.dram_tensor(name, SHAPES[name], F32, kind="ExternalInput").ap()
        return dr[name]

    xT = nc.dram_tensor("xT", [D, SEQ], F32, kind="ExternalInput").ap()
    outT = nc.dram_tensor("outT", [D, SEQ], F32, kind="ExternalOutput").ap()
    xTv = xT.rearrange("(kt p) t -> p kt t", p=128)
    outTv = outT.rearrange("(kt p) t -> p kt t", p=128)
    nlay = len(layers)
    scr = nc.dram_tensor("scr", [nlay * 26, 128, 4096], BF16, kind="Internal").ap()
    tabd = nc.dram_tensor("tabd", [2, 128, 2 * 16 * TS], F32, kind="Internal").ap()

    xres = P.sb([128, 8, T], F32, "xres")
    hb = P.sb([128, 8, T], BF16, "hb")
    ring = P.sb([128, NSLOT, 4096], BF16, "ring")
    stage = P.sb([128, 2, 2048], F32, "stage")
    tab = P.sb([128, 2, 16, TS], F32, "tab")
    AW = 17920
    arena = P.sb([128, AW], F32, "arena")
    ones = P.sb([128, 128], BF16, "ones")
    onesf = P.sb([128, 128], F32, "onesf")
    ident = P.sb([128, 128], F32, "ident_sb")
    cst = P.sb([128, 4], F32, "cst")
    gmix = P.sb([128, 4, 8], F32, "gmix_sb")
    gffn = P.sb([128, 4, 8], F32, "gffn_sb")
    gfin = P.sb([128, 8], F32, "gfin_sb")
    fcw = P.sb([128, 4, 44, 3], F32, "fcw_sb")
    fcb = P.sb([128, 4, 44], F32, "fcb_sb")
    fhist = P.sb([128, 4, 44, 2], F32, "fhist")
    cxhist = P.sb([128, 2, 4, 2], F32, "cxhist")
    zhist = P.sb([128, 2, 4, 16], F32, "zhist")
    qinit = P.sb([128, 2, 2, 16], F32, "qinit")
    qend = P.sb([128, 2, 16], F32, "qend")
    econv = P.sb([128, 2, 4, 3], F32, "econv_sb")
    Bl = P.sb([128, 2, 4, 2, 128], BF16, "Bl")
    Cl = P.sb([128, 2, 16, 3, 32], BF16, "Cl")
    rdec = P.sb([128, 2, 16], F32, "rdec")
    rotc = P.sb([128, 2, 2, 16], F32, "rotc")
    dq = P.sb([128, 2, 4], F32, "dq")
    gbh = P.sb([128, 2, 4], F32, "gbh")
    pools = P.sb([128, 2, 4], F32, "pools_sb")
    poolw = P.sb([128, 2, 4, 128], BF16, "poolw_sb")
    wT = P.sb([128, 2, 4, 128], BF16, "wT")
    bh = P.sb([128, 2, 4, 128], F32, "bh")
    ghbc = P.sb([128, 2, 512], F32, "ghbc")
    invc = P.sb([128, 4, 16], F32, "invc_sb")
    banks = [P.ps([128, 512], F32, f"bank{i}") for i in range(8)]

    def bankv(i, lo=0, hi=512):
        return V(banks[i][:, lo:hi], ("bank", i))

    class Arena:
        def __init__(self):
            self.p = 0

        def reset(self, p=0):
            self.p = p

        def take(self, words, dtype=F32, shape=None):
            lo = self.p
            self.p += (words + 7) // 8 * 8
            assert self.p <= AW, (self.p, AW)
            ap = arena[:, lo:lo + words]
            if dtype == BF16:
                ap = ap.bitcast(BF16)
            return V(ap, ("@ar", lo, lo + words))

    AR = Arena()

    def R(*vs):
        out = []
        for v in vs:
            if v is None:
                continue
            out.append(v.reg if isinstance(v, V) else v)
        return out

    def act(out, in_, func, reads, writes, bias=None, scale=None, accum=None):
        kw = {}
        if bias is not None:
            kw["bias"] = bias
        if scale is not None:
            kw["scale"] = scale
        if accum is not None:
            kw["accum_out"] = accum
        return P.op("act", lambda e: e.activation(out, in_, func, **kw), reads, writes)

    def tt(eng, out, a, b, op, reads, writes):
        return P.op(eng, lambda e: e.tensor_tensor(out, a, b, op), reads, writes)

    def ts(eng, out, a, s1, s2, op0, op1, reads, writes):
        if op1 is None:
            return P.op(eng, lambda e: e.tensor_scalar(out, a, s1, None, op0), reads, writes)
        return P.op(eng, lambda e: e.tensor_scalar(out, a, s1, s2, op0, op1), reads, writes)

    def stt(out, a, s, b, op0, op1, reads, writes):
        return P.op("dve", lambda e: e.scalar_tensor_tensor(out, a, s, b, op0, op1), reads, writes)

    def cp(eng, out, in_, reads, writes):
        if eng == "act":
            return P.op("act", lambda e: e.activation(out, in_, AF.Copy), reads, writes)
        return P.op(eng, lambda e: e.tensor_copy(out, in_), reads, writes)

    def mm(out, lhsT, rhs, start, stop, reads, writes, tp=None):
        if tp is None:
            return P.op("pe", lambda e: e.matmul(out, lhsT, rhs, start=start, stop=stop), reads, writes)
        return P.op("pe", lambda e: e.matmul(out, lhsT, rhs, start=start, stop=stop, tile_position=tp),
                    reads, writes)

    def dump(name, ap, shape, reads):
        if name not in dbg:
            return
        t = nc.dram_tensor("dbg_" + name, list(shape), ap.dtype, kind="ExternalOutput").ap()
        dbg_out[name] = t
        P.dma("act", t, ap, "all:dbg", reads=reads, writes=[("dbgout", name)])

    smc = {"n": 0}

    def small_dma(dst_ap, src_ap, writes):
        g = "sm%d" % (smc["n"] % 4)
        smc["n"] += 1
        P.dma("sp", dst_ap, src_ap, g, writes=writes)

    def load_small(dst_ap, name, key):
        small_dma(dst_ap, din(name), [key])

    P.op("dve", lambda e: e.memset(ones[:], 1.0), writes=["ones"])
    P.op("dve", lambda e: e.memset(onesf[:], 1.0), writes=["onesf"])
    P.op("dve", lambda e: e.memset(cst[:, 0:1], -0.5), writes=["cst"])
    P.op("dve", lambda e: e.memset(cst[:, 1:2], EPS), reads=["cst"], writes=["cst"])
    P.op("dve", lambda e: e.memset(fhist[:], 0.0), writes=["fhist_all"])
    P.op("dve", lambda e: e.memset(cxhist[:], 0.0), writes=["cxhist_all"])
    P.op("dve", lambda e: e.memset(zhist[:], 0.0), writes=["zhist_all"])
    P.op("dve", lambda e: e.memset(qinit[:], 0.0), writes=["qinit_all"])
    load_small(gmix[:], "gmix", "gmix")
    load_small(gffn[:], "gffn", "gffn")
    load_small(gfin[:], "gfin", "gfin")
    load_small(invc[:], "invc", "invc")
    for i in layers:
        load_small(fcw[:, i], f"fcw{i}", ("fcw", i))
        load_small(fcb[:, i], f"fcb{i}", ("fcb", i))

    def wview(name, K):
        return din(name).rearrange("(kt p) n -> p kt n", p=128)

    def slabs_for(i):
        j = i // 2
        L = []
        if i % 2 == 0:
            w = wview(f"ewin{j}", 1024)
            L.append((8, 512, [(w, 0, 0, 256), (w, 1024, 256, 256)]))
            L.append((8, 512, [(w, 256, 0, 256), (w, 1280, 256, 256)]))
            L.append((8, 512, [(w, 1536, 0, 512)]))
            L.append((8, 512, [(w, 512, 0, 512)]))
            L.append((4, 512, [(wview(f"glw{j}", 512), 0, 0, 512)]))
            wo = wview(f"ewout{j}", 1024)
        else:
            w = wview(f"owin{j}", 1024)
            L.append((8, 512, [(w, 0, 0, 512)]))
            L.append((8, 512, [(w, 512, 0, 512)]))
            L.append((8, 512, [(w, 1024, 0, 512)]))
            wo = wview(f"owout{j}", 1024)
        L.append((8, 512, [(wo, 0, 0, 512)]))
        L.append((8, 512, [(wo, 512, 0, 512)]))
        wu = wview(f"fup{i}", 1024)
        for k in range(11):
            L.append((8, 512, [(wu, 256 * k, 0, 256), (wu, 2816 + 256 * k, 256, 256)]))
        wd = wview(f"fdn{i}", 2816)
        for m in range(8):
            L.append((22, 128, [(wd, 128 * m, 0, 128)]))
        return L

    lay_slabs = {i: slabs_for(i) for i in layers}
    seq = []
    for t in range(ntiles):
        for li, i in enumerate(layers):
            for s in range(len(lay_slabs[i])):
                seq.append((t, li, i, s))
    stream = {"next": 0, "cur": 0, "cast": 0, "stg": 0}

    def slot_reg(slot):
        return ("@ring%d" % slot, 0, 4096)

    def make_load(n):
        t, li, i, s = seq[n]
        KT, W, pieces = lay_slabs[i][s]
        slot = n % NSLOT
        sid = li * 26 + s
        nel = KT * W
        if t == 0:
            h0 = (KT + 1) // 2
            for (k0, k1) in ((0, h0), (h0, KT)):
                if k1 <= k0:
                    continue
                sg = stream["stg"] % 2
                stream["stg"] += 1
                nk = k1 - k0
                sview = stage[:, sg, 0:nk * W].rearrange("p (k w) -> p k w", k=nk)
                for pi, (w, c0, d0, wd_) in enumerate(pieces):
                    P.dma("sp", sview[:, :, d0:d0 + wd_], w[:, k0:k1, c0:c0 + wd_], "stg%d" % sg,
                          writes=[("stage", sg, pi)])
                eng = "act"
                stream["cast"] += 1
                dst = ring[:, slot, k0 * W:k1 * W]
                src = stage[:, sg, 0:nk * W]
                cp(eng, dst, src, [("stage", sg, pi) for pi in range(len(pieces))],
                   [("@ring%d" % slot, k0 * W, k1 * W)])
            P.dma("act", scr[sid][:, 0:nel], ring[:, slot, 0:nel], "scrst%d" % slot,
                  reads=[("@ring%d" % slot, 0, nel)], writes=[("scr", sid)])
        else:
            P.dma("sp", ring[:, slot, 0:nel], scr[sid][:, 0:nel], "ring%d" % slot,
                  reads=[("scr", sid)], writes=[("@ring%d" % slot, 0, nel)])

    def next_slab():
        n = stream["cur"]
        stream["cur"] += 1
        while stream["next"] < min(len(seq), n + NSLOT):
            make_load(stream["next"])
            stream["next"] += 1
        t, li, i, s = seq[n]
        KT, W, _ = lay_slabs[i][s]
        slot = n % NSLOT
        view = ring[:, slot, 0:KT * W].rearrange("p (k w) -> p k w", k=KT)
        return V(view, slot_reg(slot))

    rot = {"main": 0, "bu": 0, "yb": 0}
    pools_ = {"main4": [0, 1, 2, 3], "main8": [0, 1, 2, 3, 4, 5, 6, 7], "bu": [4, 5, 0, 1], "yb": [6, 7]}

    def nb(pool):
        key = "main" if pool.startswith("main") else pool
        lst = pools_[pool]
        b = lst[rot[key] % len(lst)]
        rot[key] += 1
        return b

    load_small(ident[:], "ident", "ident")
    even_js = [i // 2 for i in layers if i % 2 == 0]
    odd_js = [i // 2 for i in layers if i % 2 == 1]

    def ssm_setup(j, jj):
        AR.reset()
        sm = lambda: AR.take(16)
        ls, are, aim = sm(), sm(), sm()
        for v, nm in ((ls, "sls"), (are, "sare"), (aim, "saim")):
            small_dma(v.ap, din(f"{nm}{j}"), R(v))
        big = {}
        for nm in ("sbre", "sbim", "scre", "scim"):
            big[nm] = AR.take(256)
            small_dma(big[nm].ap.rearrange("p (k h) -> p k h", k=16), din(f"{nm}{j}"), R(big[nm]))
        small_dma(econv[:, j], din(f"econv{j}"), [("econv", j)])
        sdl, glbl = AR.take(4), AR.take(4)
        small_dma(sdl.ap, din(f"sd{j}"), R(sdl))
        small_dma(glbl.ap, din(f"glb{j}"), R(glbl))
        ts("dve", dq[:, j, :], sdl.ap, 0.25, None, ALU.mult, None, R(sdl), [("dq", j)])
        ts("dve", gbh[:, j, :], glbl.ap, 0.5, None, ALU.mult, None, R(glbl), [("gbh", j)])

        dt_, xr, th = sm(), sm(), sm()
        act(dt_.ap, ls.ap, AF.Exp, R(ls), R(dt_))
        tt("dve", xr.ap, are.ap, dt_.ap, ALU.mult, R(are, dt_), R(xr))
        tt("dve", th.ap, aim.ap, dt_.ap, ALU.mult, R(aim, dt_), R(th))
        rv = V(rdec[:, j, :], ("rdec", j))
        act(rv.ap, xr.ap, AF.Exp, R(xr), R(rv))
        al, a2, ps_, pc_ = sm(), sm(), sm(), sm()
        ts("dve", al.ap, th.ap, 1.0 / 64, None, ALU.mult, None, R(th), R(al))
        tt("dve", a2.ap, al.ap, al.ap, ALU.mult, R(al), R(a2))

        def horner(p, coefs):
            ts("dve", p.ap, a2.ap, coefs[0], coefs[1], ALU.mult, ALU.add, R(a2), R(p))
            for c in coefs[2:]:
                tt("dve", p.ap, p.ap, a2.ap, ALU.mult, R(p, a2), R(p))
                ts("dve", p.ap, p.ap, c, None, ALU.add, None, R(p), R(p))

        horner(ps_, [1.0 / 362880, -1.0 / 5040, 1.0 / 120, -1.0 / 6, 1.0])
        tt("dve", ps_.ap, ps_.ap, al.ap, ALU.mult, R(ps_, al), R(ps_))
        horner(pc_, [-1.0 / 3628800, 1.0 / 40320, -1.0 / 720, 1.0 / 24, -0.5, 1.0])
        t1, t2 = sm(), sm()
        for _ in range(6):
            tt("dve", t1.ap, pc_.ap, pc_.ap, ALU.mult, R(pc_), R(t1))
            tt("dve", t2.ap, ps_.ap, ps_.ap, ALU.mult, R(ps_), R(t2))
            tt("dve", ps_.ap, ps_.ap, pc_.ap, ALU.mult, R(ps_, pc_), R(ps_))
            ts("dve", ps_.ap, ps_.ap, 2.0, None, ALU.mult, None, R(ps_), R(ps_))
            tt("dve", pc_.ap, t1.ap, t2.ap, ALU.subtract, R(t1, t2), R(pc_))
        c1, s1 = pc_, ps_
        nre, nim, den, fre, fim = sm(), sm(), sm(), sm(), sm()
        tt("dve", nre.ap, rv.ap, c1.ap, ALU.mult, R(rv, c1), R(nre))
        ts("dve", nre.ap, nre.ap, -1.0, None, ALU.add, None, R(nre), R(nre))
        tt("dve", nim.ap, rv.ap, s1.ap, ALU.mult, R(rv, s1), R(nim))
        tt("dve", den.ap, are.ap, are.ap, ALU.mult, R(are), R(den))
        tt("dve", t1.ap, aim.ap, aim.ap, ALU.mult, R(aim), R(t1))
        tt("dve", den.ap, den.ap, t1.ap, ALU.add, R(den, t1), R(den))
        P.op("dve", lambda e: e.reciprocal(den.ap, den.ap), R(den), R(den))
        tt("dve", fre.ap, nre.ap, are.ap, ALU.mult, R(nre, are), R(fre))
        tt("dve", t1.ap, nim.ap, aim.ap, ALU.mult, R(nim, aim), R(t1))
        tt("dve", fre.ap, fre.ap, t1.ap, ALU.add, R(fre, t1), R(fre))
        tt("dve", fre.ap, fre.ap, den.ap, ALU.mult, R(fre, den), R(fre))
        tt("dve", fim.ap, nim.ap, are.ap, ALU.mult, R(nim, are), R(fim))
        tt("dve", t1.ap, nre.ap, aim.ap, ALU.mult, R(nre, aim), R(t1))
        tt("dve", fim.ap, fim.ap, t1.ap, ALU.subtract, R(fim, t1), R(fim))
        tt("dve", fim.ap, fim.ap, den.ap, ALU.mult, R(fim, den), R(fim))
        v3 = lambda v: v.ap.rearrange("p (k h) -> p k h", k=16)
        bc = lambda v: v.ap.unsqueeze(2).to_broadcast([128, 16, 16])
        Bre, Bim, tA = AR.take(256), AR.take(256), AR.take(256)
        tt("dve", v3(Bre), v3(big["sbre"]), bc(fre), ALU.mult, R(big["sbre"], fre), R(Bre))
        tt("dve", v3(tA), v3(big["sbim"]), bc(fim), ALU.mult, R(big["sbim"], fim), R(tA))
        tt("dve", v3(Bre), v3(Bre), v3(tA), ALU.subtract, R(Bre, tA), R(Bre))
        tt("dve", v3(Bim), v3(big["sbim"]), bc(fre), ALU.mult, R(big["sbim"], fre), R(Bim))
        tt("dve", v3(tA), v3(big["sbre"]), bc(fim), ALU.mult, R(big["sbre"], fim), R(tA))
        tt("dve", v3(Bim), v3(Bim), v3(tA), ALU.add, R(Bim, tA), R(Bim))
        for comp, Bv in ((0, Bre), (1, Bim)):
            M = AR.take(512)
            P.op("dve", lambda e, M=M: e.memset(M.ap, 0.0), (), R(M))
            M4 = M.ap.rearrange("p (k g h) -> p k g h", k=16, g=2)
            B3 = v3(Bv)
            cp("dve", M4[0:64, :, 0, :], B3[0:64], R(Bv, M), R(M))
            cp("dve", M4[64:128, :, 1, :], B3[64:128], R(Bv, M), R(M))
            for b in range(4):
                bk = nb("main8")
                P.op("pe", lambda e, bk=bk, M=M, b=b: e.transpose(banks[bk][:, 0:128],
                                                                  M.ap[:, b * 128:(b + 1) * 128], ident[:]),
                     R(M) + ["ident"], [("bank", bk)])
                cp("act", Bl[:, j, b, comp, :], banks[bk][:, 0:128], [("bank", bk)], [("Bl", j)])
        P.op("dve", lambda e: e.memset(Cl[:, j], 0.0), (), [("Cl", j)])
        for m, (nm, sc) in enumerate((("scre", 0.25), ("scre", -0.25), ("scim", -0.25))):
            src = v3(big[nm])
            for g2 in range(2):
                ts("dve", Cl[64 * g2:64 * g2 + 64, j, :, m, 16 * g2:16 * g2 + 16], src[64 * g2:64 * g2 + 64],
                   sc, None, ALU.mult, None, R(big[nm]) + [("Cl", j)], [("Cl", j)])
        tabv = ("tab",)
        cp("dve", tab[:, 0, :, 0:1], c1.ap.unsqueeze(2), R(c1) + ["tab"], ["tab"])
        cp("dve", tab[:, 1, :, 0:1], s1.ap.unsqueeze(2), R(s1) + ["tab"], ["tab"])
        w1, w2 = AR.take(2048), AR.take(2048)
        n = 1
        while n < TS:
            cb = tab[:, 0, :, n - 1:n].to_broadcast([128, 16, n])
            sb_ = tab[:, 1, :, n - 1:n].to_broadcast([128, 16, n])
            a1 = w1.ap[:, 0:16 * n].rearrange("p (k t) -> p k t", k=16)
            a2_ = w2.ap[:, 0:16 * n].rearrange("p (k t) -> p k t", k=16)
            tt("dve", a1, tab[:, 0, :, 0:n], cb, ALU.mult, ["tab"], R(w1))
            tt("dve", a2_, tab[:, 1, :, 0:n], sb_, ALU.mult, ["tab"], R(w2))
            tt("dve", tab[:, 0, :, n:2 * n], a1, a2_, ALU.subtract, R(w1, w2) + ["tab"], ["tab"])
            tt("dve", a1, tab[:, 1, :, 0:n], cb, ALU.mult, ["tab"], R(w1))
            tt("dve", a2_, tab[:, 0, :, 0:n], sb_, ALU.mult, ["tab"], R(w2))
            tt("dve", tab[:, 1, :, n:2 * n], a1, a2_, ALU.add, R(w1, w2) + ["tab"], ["tab"])
            n *= 2
        cp("dve", rotc[:, j, 0, :].unsqueeze(2), tab[:, 0, :, TS - 1:TS], ["tab"], [("rotc", j)])
        cp("dve", rotc[:, j, 1, :].unsqueeze(2), tab[:, 1, :, TS - 1:TS], ["tab"], [("rotc", j)])
        P.dma("act", tabd[jj], tab[:].rearrange("p c k t -> p (c k t)"), "tabst", reads=["tab"],
              writes=[("tabd", jj)])

    def odd_setup(j):
        AR.reset()
        small_dma(pools[:, j, :], din(f"pools{j}"), [("pools", j)])
        pw = AR.take(512)
        small_dma(pw.ap.rearrange("p (g d) -> p g d", g=4), din(f"poolw{j}"), R(pw))
        cp("dve", poolw[:, j].rearrange("p g d -> p (g d)"), pw.ap, R(pw), [("poolw", j)])
        sw, tr = AR.take(512), AR.take(128)
        small_dma(sw.ap.rearrange("p (g d) -> p g d", g=4), din(f"sgw{j}"), R(sw))
        small_dma(tr.ap, din("trilh"), R(tr))
        tt("dve", wT[:, j], sw.ap.rearrange("p (g d) -> p g d", g=4),
           tr.ap.unsqueeze(1).to_broadcast([128, 4, 128]), ALU.mult, R(sw, tr), [("wT", j)])
        sb2 = AR.take(512)
        small_dma(sb2.ap.rearrange("p (g d) -> p g d", g=4), din(f"sgb{j}"), R(sb2))
        ts("dve", bh[:, j].rearrange("p g d -> p (g d)"), sb2.ap, 0.5, None, ALU.mult, None, R(sb2),
           [("bh", j)])
        gg = AR.take(512)
        small_dma(gg.ap, din(f"sgng{j}"), R(gg))
        ts("dve", ghbc[:, j, :], gg.ap, 0.5, None, ALU.mult, None, R(gg), [("ghbc", j)])

    for jj, j in enumerate(even_js):
        ssm_setup(j, jj)
    for j in odd_js:
        odd_setup(j)
    tab_state = {"j": even_js[-1] if even_js else None}

    dq_ = []

    def later(delay, fn):
        dq_.append([delay, fn])

    def tick():
        due = [x for x in dq_ if x[0] <= 0]
        for x in due:
            dq_.remove(x)
        for x in dq_:
            x[0] -= 1
        for x in due:
            x[1]()

    def flush():
        while dq_:
            tick()

    def sub(v, a, b):
        return V(v.ap[:, a:b], ("@ar", v.reg[1] + a, v.reg[1] + b))

    def xr_(kt):
        return V(xres[:, kt, :], ("xres", kt))

    def hb_(kt):
        return V(hb[:, kt, :], ("hb", kt))

    def rmsnorm(gap, gkey, outs, bf=True):
        AR.reset()
        sq = [AR.take(256, BF16) for _ in range(8)]
        ms4, rs4 = AR.take(4), AR.take(4)
        dg = [AR.take(128) for _ in range(4)]
        bk = nb("main8")
        for kt in range(8):
            act(sq[kt].ap, xres[:, kt, :], AF.Square, [("xres", kt)], R(sq[kt]))
        for tb_ in range(4):
            for kt in range(8):
                mm(banks[bk][:, tb_:tb_ + 1], sq[kt].ap[:, tb_ * 128:(tb_ + 1) * 128], ones[:, 0:1],
                   kt == 0, kt == 7, ["ones"] + R(sq[kt]), [("bank", bk)])
        ts("dve", ms4.ap, banks[bk][:, 0:4], 1.0 / D, EPS, ALU.mult, ALU.add, [("bank", bk)], R(ms4))
        tt("pool", rs4.ap, ms4.ap, cst[:, 0:1].to_broadcast([128, 4]), ALU.pow, R(ms4) + ["cst"], R(rs4))
        bk2 = nb("main8")
        for tb_ in range(4):
            act(dg[tb_].ap, ident[:], AF.Identity, R(rs4) + ["ident"], R(dg[tb_]), scale=rs4.ap[:, tb_:tb_ + 1])
            mm(banks[bk2][:, tb_ * 128:(tb_ + 1) * 128], onesf[:], dg[tb_].ap, True, True,
               ["onesf"] + R(dg[tb_]), [("bank", bk2)])
        for kt in range(8):
            stt(outs[kt].ap, xres[:, kt, :], gap[:, kt:kt + 1], banks[bk2][:, :], ALU.mult, ALU.mult,
                [("xres", kt), gkey, ("bank", bk2)], R(outs[kt]))

    def proj_chunk(Wv, col, pool, rhs_list, KT=8):
        bk = nb(pool)
        for kt in range(KT):
            mm(banks[bk][:, :], Wv.ap[:, kt, col:col + 128], rhs_list[kt].ap, kt == 0, kt == KT - 1,
               [Wv.reg] + R(rhs_list[kt]), [("bank", bk)])
        return bk

    def out_proj(ycat, pool):
        for half in range(2):
            Wo = next_slab()
            for m_ in range(4):
                m = half * 4 + m_
                bk = proj_chunk(Wo, m_ * 128, pool, ycat)
                tt("dve", xres[:, m, :], xres[:, m, :], banks[bk][:, :], ALU.add,
                   [("xres", m), ("bank", bk)], [("xres", m)])

    def gelu2(bk, tb, Gout):
        cp("act", tb["pc"].ap, banks[bk][:, :], [("bank", bk)], R(tb["pc"]))
        act(tb["p2"].ap, banks[bk][:, :], AF.Square, [("bank", bk)], R(tb["p2"]))
        ts("pool", tb["ti"].ap, tb["p2"].ap, 0.044715, 1.0, ALU.mult, ALU.add, R(tb["p2"]), R(tb["ti"]))
        tt("pool", tb["ti"].ap, tb["ti"].ap, tb["pc"].ap, ALU.mult, R(tb["ti"], tb["pc"]), R(tb["ti"]))
        act(tb["th"].ap, tb["ti"].ap, AF.Tanh, R(tb["ti"]), R(tb["th"]), scale=0.7978845608028654)
        stt(Gout.ap, tb["th"].ap, 1.0, tb["pc"].ap, ALU.add, ALU.mult, R(tb["th"], tb["pc"]), R(Gout))

    def even_mixer(i, t):
        j = i // 2
        hbl = [hb_(kt) for kt in range(8)]
        AR.reset()
        xa = [AR.take(512) for _ in range(4)]
        cxb = [AR.take(514) for _ in range(4)]
        ycat = [AR.take(256, BF16) for _ in range(8)]
        u = [AR.take(512) for _ in range(4)]
        ub = [AR.take(256, BF16) for _ in range(4)]
        ssmb = [dict(bu=AR.take(512), m34=AR.take(512), q=AR.take(512),
                     X=AR.take(256, BF16), Y=AR.take(256, BF16)) for _ in range(3)]
        tmpb = [dict(yq=AR.take(256), y2=AR.take(256), ti=AR.take(256), th=AR.take(256)) for _ in range(2)]
        rt = [AR.take(16) for _ in range(4)]
        G = xa
        base = cxb[0].reg[1]
        gelu_bf = [V(arena[:, base + 256 * b:base + 256 * (b + 1)].bitcast(BF16),
                     ("@ar", base + 256 * b, base + 256 * (b + 1))) for b in range(4)]
        th2 = [V(arena[:, base + 1024 + 512 * x:base + 1024 + 512 * (x + 1)],
                 ("@ar", base + 1024 + 512 * x, base + 1024 + 512 * (x + 1))) for x in range(2)]
        if tab_state["j"] != j:
            jj = even_js.index(j)
            P.dma("sp", tab[:].rearrange("p c k t -> p (c k t)"), tabd[jj], "tabld",
                  reads=[("tabd", jj)], writes=["tab"])
            tab_state["j"] = j
        for sl in range(2):
            W = next_slab()
            for ii in range(2):
                c = 2 * sl + ii
                bx = proj_chunk(W, ii * 128, "main4", hbl)
                cp("act", xa[c].ap, banks[bx][:, :], [("bank", bx)], R(xa[c]))
                bc_ = proj_chunk(W, 256 + ii * 128, "main4", hbl)
                cxh, cxm = sub(cxb[c], 0, 2), sub(cxb[c], 2, 514)
                cp("pool", cxh.ap, cxhist[:, j, c, :], [("cxhist", j, c)], R(cxh))
                tt("dve", cxm.ap, banks[bc_][:, :], xa[c].ap, ALU.mult, [("bank", bc_)] + R(xa[c]), R(cxm))
                ek = ("econv", j)
                act(xa[c].ap, cxm.ap, AF.Identity, R(cxm) + [ek], R(xa[c]), scale=econv[:, j, c, 2:3])
                stt(xa[c].ap, cxb[c].ap[:, 1:513], econv[:, j, c, 1:2], xa[c].ap, ALU.mult, ALU.add,
                    R(cxb[c], xa[c]) + [ek], R(xa[c]))
                stt(xa[c].ap, cxb[c].ap[:, 0:512], econv[:, j, c, 0:1], xa[c].ap, ALU.mult, ALU.add,
                    R(cxb[c], xa[c]) + [ek], R(xa[c]))
                cp("pool", cxhist[:, j, c, :], cxb[c].ap[:, 512:514], R(cxm), [("cxhist", j, c)])
        W = next_slab()
        for b in range(4):
            bk = proj_chunk(W, b * 128, "main4", hbl)
            cp("act", u[b].ap, banks[bk][:, :], [("bank", bk)], R(u[b]))
            cp("pool", ub[b].ap, u[b].ap, R(u[b]), R(ub[b]))
        W = next_slab()
        for c in range(4):
            bk = proj_chunk(W, c * 128, "main4", hbl)
            tt("dve", ycat[c].ap, banks[bk][:, :], xa[c].ap, ALU.mult, [("bank", bk)] + R(xa[c]), R(ycat[c]))
        items = [(sb_i, b, q) for sb_i in range(T // TS) for b in range(4) for q in range(4)]
        v3 = lambda v: v.ap.rearrange("p (c t) -> p c t", c=2)
        stA = {}

        def stageA(n):
            sb_i, b, q = items[n]
            c0 = sb_i * TS
            S = ssmb[n % 3]
            bb = nb("bu")
            for comp in range(2):
                mm(banks[bb][:, comp * TS:(comp + 1) * TS], Bl[32 * q:32 * q + 32, j, b, comp, :],
                   ub[b].ap[32 * q:32 * q + 32, c0:c0 + TS], True, True,
                   [("Bl", j)] + R(ub[b]), [("bank", bb)], tp=(32 * q, 0))
            stA[n] = bb

        def stageB(n, yk):
            sb_i, b, q = items[n]
            k = 4 * b + q
            S = ssmb[n % 3]
            cbc = tab[:, 0, k, :].unsqueeze(1).to_broadcast([128, 2, TS])
            sbc = tab[:, 1, k, :].unsqueeze(1).to_broadcast([128, 2, TS])
            bb = stA.pop(n)
            bk3 = banks[bb][:, :].rearrange("p (c t) -> p c t", c=2)
            tt("dve", v3(S["m34"]), bk3, sbc, ALU.mult, [("bank", bb), "tab"], R(S["m34"]))
            tt("dve", v3(S["bu"]), bk3, cbc, ALU.mult, [("bank", bb), "tab"], R(S["bu"]))
            mre, mim = sub(S["bu"], 0, TS), sub(S["bu"], TS, 2 * TS)
            tt("dve", mre.ap, mre.ap, S["m34"].ap[:, TS:2 * TS], ALU.add, R(mre, S["m34"]), R(mre))
            tt("dve", mim.ap, mim.ap, S["m34"].ap[:, 0:TS], ALU.subtract, R(mim, S["m34"]), R(mim))
            rb = rdec[:, j, k:k + 1].to_broadcast([128, TS])
            for comp, mv in ((0, mre), (1, mim)):
                qv = sub(S["q"], comp * TS, (comp + 1) * TS)
                P.op("dve", lambda e, qv=qv, rb=rb, mv=mv, comp=comp, k=k: e.tensor_tensor_scan(
                    qv.ap, rb, mv.ap, qinit[:, j, comp, k:k + 1], ALU.mult, ALU.add),
                    R(mv) + [("rdec", j), ("qinit", j)], R(qv))
            Xv = S["X"].ap.rearrange("p (c t) -> p c t", c=2)
            Yv = S["Y"].ap.rearrange("p (c t) -> p c t", c=2)
            tt("dve", Xv, v3(S["q"]), cbc, ALU.mult, R(S["q"]) + ["tab"], R(S["X"]))
            tt("dve", Yv, v3(S["q"]), sbc, ALU.mult, R(S["q"]) + ["tab"], R(S["Y"]))
            cp("pool", qend[:, :, k:k + 1], v3(S["q"])[:, :, TS - 1:TS], R(S["q"]), [("qend", k)])
            yo = banks[yk][32 * q:32 * q + 32, 0:TS]
            ops_ = ((0, S["X"], 0), (1, S["Y"], 1), (2, S["Y"], 0), (2, S["X"], 1))
            for n_, (m_, src, half) in enumerate(ops_):
                mm(yo, Cl[:, j, k, m_, :], src.ap[:, half * TS:(half + 1) * TS], n_ == 0, n_ == 3,
                   [("Cl", j)] + R(src), [("bank", yk)], tp=(0, 32 * q))

        def block_epilogue(sb_i, b, yk, ib):
            c0 = sb_i * TS
            tb = tmpb[ib % 2]
            Gs = sub(G[b], c0, c0 + TS)
            gb = V(gelu_bf[b].ap[:, c0:c0 + TS], ("@ar", gelu_bf[b].reg[1] + c0 // 2,
                                                 gelu_bf[b].reg[1] + (c0 + TS) // 2))

            def e1():
                stt(tb["yq"].ap, u[b].ap[:, c0:c0 + TS], dq[:, j, b:b + 1], banks[yk][:, 0:TS], ALU.mult, ALU.add,
                    R(u[b]) + [("dq", j), ("bank", yk)], R(tb["yq"]))
                act(tb["y2"].ap, tb["yq"].ap, AF.Square, R(tb["yq"]), R(tb["y2"]), scale=4.0)

            def e2():
                ts("pool", tb["ti"].ap, tb["y2"].ap, 4 * 0.044715, 4.0, ALU.mult, ALU.add, R(tb["y2"]), R(tb["ti"]))
                tt("pool", tb["ti"].ap, tb["ti"].ap, tb["yq"].ap, ALU.mult, R(tb["ti"], tb["yq"]), R(tb["ti"]))

            def e3():
                act(tb["th"].ap, tb["ti"].ap, AF.Tanh, R(tb["ti"]), R(tb["th"]), scale=0.7978845608028654)

            def e4():
                stt(Gs.ap, tb["th"].ap, 1.0, tb["yq"].ap, ALU.add, ALU.mult, R(tb["th"], tb["yq"]), R(Gs))
                act(gb.ap, Gs.ap, AF.Copy, R(Gs), R(gb), scale=2.0)

            later(0, e1)
            later(1, e2)
            later(2, e3)
            later(3, e4)

        def carry():
            qk = [("qend", k) for k in range(16)]
            cT, sT = rotc[:, j, 0, :], rotc[:, j, 1, :]
            rk = [("rotc", j)]
            tt("dve", rt[0].ap, cT, qend[:, 0, :], ALU.mult, qk + rk, R(rt[0]))
            tt("dve", rt[1].ap, sT, qend[:, 1, :], ALU.mult, qk + rk, R(rt[1]))
            tt("dve", rt[2].ap, sT, qend[:, 0, :], ALU.mult, qk + rk, R(rt[2]))
            tt("dve", rt[3].ap, cT, qend[:, 1, :], ALU.mult, qk + rk, R(rt[3]))
            tt("dve", qinit[:, j, 0, :], rt[0].ap, rt[1].ap, ALU.subtract, R(rt[0], rt[1]), [("qinit", j)])
            tt("dve", qinit[:, j, 1, :], rt[2].ap, rt[3].ap, ALU.add, R(rt[2], rt[3]), [("qinit", j)])

        NI = len(items)
        stageA(0)
        stageA(1)
        yk = None
        ib = 0
        for n in range(NI):
            sb_i, b, q = items[n]
            if n + 2 < NI:
                stageA(n + 2)
            if q == 0:
                yk = nb("yb")
            stageB(n, yk)
            tick()
            if q == 3:
                block_epilogue(sb_i, b, yk, ib)
                ib += 1
                if b == 3:
                    carry()
        flush()
        Wg = next_slab()
        for c in range(4):
            bk = proj_chunk(Wg, c * 128, "main4", gelu_bf, KT=4)
            tv = th2[c % 2]
            act(tv.ap, banks[bk][:, :], AF.Tanh, [("bank", bk), ("gbh", j)], R(tv), bias=gbh[:, j, c:c + 1], scale=0.5)
            stt(ycat[4 + c].ap, tv.ap, 1.0, G[c].ap, ALU.add, ALU.mult, R(tv, G[c]), R(ycat[4 + c]))
        out_proj(ycat, "main4")

    def ffn(i, t):
        hbl = [hb_(kt) for kt in range(8)]
        AR.reset(3072)
        hid = [AR.take(256, BF16) for _ in range(22)]
        acc = [AR.take(512) for _ in range(8)]
        sg = [AR.take(512) for _ in range(4)]
        for k in range(11):
            W = next_slab()
            for jj in range(2):
                jch = 2 * k + jj
                chs = [jch, jch + 22]
                bks = [proj_chunk(W, gv * 256 + jj * 128, "main8", hbl) for gv in range(2)]
                accs = [acc[2 * (jch % 4) + gv] for gv in range(2)]
                wk = [("fcw", i)]
                w_ = lambda ch, kk: fcw[:, i, ch, kk:kk + 1]
                for gv in range(2):
                    act(accs[gv].ap, banks[bks[gv]][:, :], AF.Identity, [("bank", bks[gv]), ("fcb", i)] + wk,
                        R(accs[gv]), bias=fcb[:, i, chs[gv]:chs[gv] + 1], scale=w_(chs[gv], 2))
                for gv in range(2):
                    a, bk, ch = accs[gv], bks[gv], chs[gv]
                    stt(a.ap[:, 1:T], banks[bk][:, 0:T - 1], w_(ch, 1), a.ap[:, 1:T], ALU.mult, ALU.add,
                        [("bank", bk)] + wk + R(a), R(a))
                for gv in range(2):
                    a, bk, ch = accs[gv], bks[gv], chs[gv]
                    stt(a.ap[:, 0:1], fhist[:, i, ch, 1:2], w_(ch, 1), a.ap[:, 0:1], ALU.mult, ALU.add,
                        [("fhist", i, ch)] + wk + R(a), R(a))
                for gv in range(2):
                    a, bk, ch = accs[gv], bks[gv], chs[gv]
                    stt(a.ap[:, 2:T], banks[bk][:, 0:T - 2], w_(ch, 0), a.ap[:, 2:T], ALU.mult, ALU.add,
                        [("bank", bk)] + wk + R(a), R(a))
                for gv in range(2):
                    a, bk, ch = accs[gv], bks[gv], chs[gv]
                    stt(a.ap[:, 0:2], fhist[:, i, ch, 0:2], w_(ch, 0), a.ap[:, 0:2], ALU.mult, ALU.add,
                        [("fhist", i, ch)] + wk + R(a), R(a))
                def fin(chs=chs, bks=bks, accs=accs, s_=sg[jch % 4], jch=jch):
                    for gv in range(2):
                        cp("act", fhist[:, i, chs[gv], :], banks[bks[gv]][:, T - 2:T], [("bank", bks[gv])],
                           [("fhist", i, chs[gv])])
                    act(s_.ap, accs[0].ap, AF.Silu, R(accs[0]), R(s_))
                    tt("pool", hid[jch].ap, s_.ap, accs[1].ap, ALU.mult, R(s_, accs[1]), R(hid[jch]))

                tick()
                later(0, fin)
        flush()
        for m in range(8):
            Wd = next_slab()
            bk = nb("main8")
            for jch in range(22):
                mm(banks[bk][:, :], Wd.ap[:, jch, :], hid[jch].ap, jch == 0, jch == 21,
                   [Wd.reg] + R(hid[jch]), [("bank", bk)])
            tt("dve", xres[:, m, :], xres[:, m, :], banks[bk][:, :], ALU.add,
               [("xres", m), ("bank", bk)], [("xres", m)])

    def odd_mixer(i, t):
        j = i // 2
        hbl = [hb_(kt) for kt in range(8)]
        AR.reset()
        zt = [AR.take(528) for _ in range(4)]
        sA, sB = AR.take(528), AR.take(528)
        pooled = [AR.take(256, BF16) for _ in range(4)]
        Gu = [AR.take(512) for _ in range(4)]
        tmpb = [dict(pc=AR.take(512), p2=AR.take(512), ti=AR.take(512), th=AR.take(512)) for _ in range(2)]
        Gv = [AR.take(512) for _ in range(2)]
        vtok = [AR.take(256, BF16) for _ in range(4)]
        ycat = [AR.take(256, BF16) for _ in range(8)]
        sgt = [AR.take(512) for _ in range(2)]
        sml = [dict(ss=AR.take(1), ms=AR.take(1), rs=AR.take(1)) for _ in range(2)]
        t15 = AR.take(16)
        W = next_slab()
        for g, w in enumerate((2, 4, 8, 16)):
            bk = proj_chunk(W, g * 128, "main8", hbl)
            zh, zm = sub(zt[g], 0, 16), sub(zt[g], 16, 528)
            cp("pool", zh.ap, zhist[:, j, g, :], [("zhist", j, g)], R(zh))
            cp("act", zm.ap, banks[bk][:, :], [("bank", bk)], R(zm))
            z = zt[g]
            tt("pool", sA.ap[:, 1:528], z.ap[:, 1:528], z.ap[:, 0:527], ALU.add, R(z), R(sA))
            Sv = sA
            if g >= 1:
                tt("pool", sB.ap[:, 3:528], sA.ap[:, 3:528], sA.ap[:, 1:526], ALU.add, R(sA), R(sB))
                Sv = sB
            if g >= 2:
                tt("pool", sA.ap[:, 7:528], sB.ap[:, 7:528], sB.ap[:, 3:524], ALU.add, R(sB), R(sA))
                Sv = sA
            if g >= 3:
                tt("pool", sB.ap[:, 15:528], sA.ap[:, 15:528], sA.ap[:, 7:520], ALU.add, R(sA), R(sB))
                Sv = sB
            stt(pooled[g].ap, Sv.ap[:, 16:528], 1.0 / w, zm.ap, ALU.mult, ALU.subtract, R(Sv, zm), R(pooled[g]))
            if t == 0:
                tt("dve", t15.ap[:, 0:15], Sv.ap[:, 16:31], invc[:, g, 0:15], ALU.mult, R(Sv) + ["invc"], R(t15))
                tt("dve", pooled[g].ap[:, 0:15], t15.ap[:, 0:15], zt[g].ap[:, 16:31], ALU.subtract,
                   R(t15, zm, pooled[g]), R(pooled[g]))
            cp("pool", zhist[:, j, g, :], zt[g].ap[:, 512:528], R(zm), [("zhist", j, g)])
            bk2 = nb("main8")
            mm(banks[bk2][:, :], poolw[:, j, g, :], pooled[g].ap, True, True, [("poolw", j)] + R(pooled[g]),
               [("bank", bk2)])
            act(ycat[g].ap, banks[bk2][:, :], AF.Identity, [("bank", bk2), ("pools", j)], R(ycat[g]),
                scale=pools[:, j, g:g + 1])
        W = next_slab()
        for c in range(4):
            bk = proj_chunk(W, c * 128, "main8", hbl)
            gelu2(bk, tmpb[c % 2], Gu[c])
        W = next_slab()
        for tb_ in range(4):
            bk = nb("main8")
            for kt in range(8):
                mm(banks[bk][:, :], hb[:, kt, tb_ * 128:(tb_ + 1) * 128], W.ap[:, kt, :], kt == 0, kt == 7,
                   [W.reg, ("hb", kt)], [("bank", bk)])
            tb = tmpb[tb_ % 2]
            gv = Gv[tb_ % 2]
            sm_ = sml[tb_ % 2]
            gelu2(bk, tb, gv)
            act(tb["p2"].ap, gv.ap, AF.Square, R(gv), R(tb["p2"], sm_["ss"]), scale=0.5, accum=sm_["ss"].ap)
            ts("dve", sm_["ms"].ap, sm_["ss"].ap, 1.0 / 512, EPS, ALU.mult, ALU.add, R(sm_["ss"]), R(sm_["ms"]))
            tt("pool", sm_["rs"].ap, sm_["ms"].ap, cst[:, 0:1], ALU.pow, R(sm_["ms"]) + ["cst"], R(sm_["rs"]))
            stt(vtok[tb_].ap, gv.ap, sm_["rs"].ap, ghbc[:, j, :], ALU.mult, ALU.mult,
                R(gv, sm_["rs"]) + [("ghbc", j)], R(vtok[tb_]))
        for hd in range(4):
            bk = nb("main8")
            for tb_ in range(4):
                mm(banks[bk][:, tb_ * 128:(tb_ + 1) * 128], vtok[tb_].ap[:, hd * 128:(hd + 1) * 128],
                   wT[:, j, hd, :], True, True, [("wT", j)] + R(vtok[tb_]), [("bank", bk)])
            sv_ = sgt[hd % 2]
            tt("dve", sv_.ap.rearrange("p (a b) -> p a b", a=4),
               banks[bk][:, :].rearrange("p (a b) -> p a b", a=4),
               bh[:, j, hd, :].unsqueeze(1).to_broadcast([128, 4, 128]), ALU.add,
               [("bank", bk), ("bh", j)], R(sv_))
            tt("dve", ycat[4 + hd].ap, sv_.ap, Gu[hd].ap, ALU.mult, R(sv_, Gu[hd]), R(ycat[4 + hd]))
        out_proj(ycat, "main8")

    for t in range(ntiles):
        t0 = t * T
        P.dma("sp", xres[:, :, :], xTv[:, :, t0:t0 + T], "xload", writes=[("xres", kt) for kt in range(8)])
        for i in layers:
            j = i // 2
            rmsnorm(gmix[:, i, :], "gmix", [hb_(kt) for kt in range(8)])
            if i % 2 == 0:
                even_mixer(i, t)
            else:
                odd_mixer(i, t)
            rmsnorm(gffn[:, i, :], "gffn", [hb_(kt) for kt in range(8)])
            ffn(i, t)
        if final:
            AR.reset(3072)
            ost = [AR.take(512) for _ in range(8)]
            p_after = AR.p
            rmsnorm(gfin[:, :], "gfin", ost)
            AR.reset(p_after)
            ov = arena[:, ost[0].reg[1]:ost[0].reg[1] + 4096].rearrange("p (k t) -> p k t", k=8)
            P.dma("act", outTv[:, :, t0:t0 + T], ov, "ostore", reads=R(*ost), writes=[("outT", t)])
        else:
            P.dma("act", outTv[:, :, t0:t0 + T], xres[:, :, :], "ostore",
                  reads=[("xres", kt) for kt in range(8)], writes=[("outT", t)])
    P.wait("act", [("outT", t) for t in range(ntiles)])
    P.emit()
    P.close()
    names = ["xT"] + list(dr.keys())
    return nc, names, dbg_out


_CACHE = {}


def _get(layers, ntiles, final):
    key = (tuple(layers), ntiles, final)
    if key not in _CACHE:
        _CACHE[key] = build(layers, ntiles, final)
    return _CACHE[key]


def kernel(**inputs):
    com = prep_common(inputs)
    x = np.asarray(inputs["x"], dtype=np.float32)
    nb_ = x.shape[0]
    xTs = [np.ascontiguousarray(x[b].T) for b in range(nb_)]
    nc, names, _ = _get((0, 1, 2, 3), SEQ // T, True)
    in_maps = []
    for b in range(nb_):
        m = {"xT": xTs[b]}
        for n in names:
            if n != "xT":
                m[n] = com[n]
        in_maps.append(m)
    res = run_bass_kernel_spmd(nc, in_maps, core_ids=list(range(nb_)))
    out = np.stack([np.ascontiguousarray(res.results[b]["outT"].T) for b in range(nb_)], axis=0)
    return out.astype(np.float32)
```

```python
import contextlib
import numpy as np
import concourse.bass as bass
import concourse.mybir as mybir
from concourse.bass_utils import run_bass_kernel_spmd

F32 = mybir.dt.float32
BF16 = mybir.dt.bfloat16
ALU = mybir.AluOpType
AF = mybir.ActivationFunctionType

D = 1024
SEQ = 4096
DEPTH = 4
DFF = 2816
T = 512
TS = 256
NSLOT = 4
EPS = 1e-6
ENGS = ("pe", "act", "dve", "pool", "sp")


class Op:
    __slots__ = ("eng", "fn", "deps", "signal", "sigval", "sem", "is_dma", "group")

    def __init__(self, eng, fn):
        self.eng = eng
        self.fn = fn
        self.deps = []
        self.signal = False
        self.sigval = 0
        self.sem = None
        self.is_dma = False
        self.group = None


class Prog:
    def __init__(self, nc):
        self.nc = nc
        self.ops = {e: [] for e in ENGS}
        self.state = {}
        self.stack = contextlib.ExitStack()
        self.dma_groups = {}
        self.n = 0

    def sb(self, shape, dtype=F32, name=None):
        self.n += 1
        return self.stack.enter_context(self.nc.sbuf_tensor(name or f"sb{self.n}", list(shape), dtype))

    def ps(self, shape, dtype=F32, name=None):
        self.n += 1
        return self.stack.enter_context(self.nc.psum_tensor(name or f"ps{self.n}", list(shape), dtype))

    @staticmethod
    def _norm(k):
        if isinstance(k, tuple) and len(k) == 3 and isinstance(k[0], str) and k[0].startswith("@"):
            return k[0], int(k[1]), int(k[2])
        return k, 0, 1

    def _segs(self, ns, lo, hi):
        L = self.state.setdefault(ns, [])
        out = []
        newL = []
        cur = lo
        for sg in L:
            a, b, w, r = sg
            if b <= lo or a >= hi:
                newL.append(sg)
                continue
            if a < lo:
                newL.append([a, lo, w, list(r)])
                a = lo
            if b > hi:
                newL.append([hi, b, w, list(r)])
                b = hi
            mid = [a, b, w, r]
            newL.append(mid)
            out.append(mid)
        out.sort(key=lambda x: x[0])
        filled = []
        for sg in out:
            if sg[0] > cur:
                g = [cur, sg[0], None, []]
                newL.append(g)
                filled.append(g)
            filled.append(sg)
            cur = sg[1]
        if cur < hi:
            g = [cur, hi, None, []]
            newL.append(g)
            filled.append(g)
        self.state[ns] = newL
        return filled

    def _track(self, o, reads, writes, skip_same_eng=False):
        deps = []
        rn = [self._norm(k) for k in reads]
        wn = [self._norm(k) for k in writes]
        for ns, lo, hi in rn:
            for sg in self._segs(ns, lo, hi):
                if sg[2] is not None:
                    deps.append(sg[2])
        for ns, lo, hi in wn:
            for sg in self._segs(ns, lo, hi):
                if sg[2] is not None:
                    deps.append(sg[2])
                last = {}
                for r in sg[3]:
                    if r.is_dma:
                        deps.append(r)
                    else:
                        last[r.eng] = r
                deps.extend(last.values())
        for ns, lo, hi in rn:
            for sg in self._segs(ns, lo, hi):
                sg[3].append(o)
        for ns, lo, hi in wn:
            segs = self._segs(ns, lo, hi)
            L = self.state[ns]
            for sg in segs:
                L.remove(sg)
            L.append([lo, hi, o, []])
        seen = set()
        for d in deps:
            if d is o or id(d) in seen:
                continue
            if skip_same_eng and (not d.is_dma) and d.eng == o.eng:
                continue
            seen.add(id(d))
            o.deps.append(d)

    def op(self, eng, fn, reads=(), writes=()):
        o = Op(eng, fn)
        self._track(o, reads, writes, skip_same_eng=(eng == "pe"))
        self.ops[eng].append(o)
        return o

    def dma(self, eng, out, in_, group, reads=(), writes=(), **kw):
        o = Op(eng, lambda e: e.dma_start(out=out, in_=in_, **kw))
        o.is_dma = True
        o.group = group
        self._track(o, reads, writes)
        lst = self.dma_groups.setdefault(group, [])
        if lst and not group.startswith("all:") and lst[-1] not in o.deps:
            o.deps.append(lst[-1])
        self.ops[eng].append(o)
        lst.append(o)
        return o

    def wait(self, eng, reads):
        o = Op(eng, lambda e: None)
        self._track(o, reads, ())
        self.ops[eng].append(o)
        return o

    def emit(self):
        nc = self.nc
        for e in ENGS:
            for o in self.ops[e]:
                for d in o.deps:
                    d.signal = True
        sems = {}
        for e in ENGS:
            sems[e] = self.stack.enter_context(nc.semaphore(f"s_{e}"))
            c = 0
            for o in self.ops[e]:
                if o.is_dma:
                    continue
                if o.signal:
                    c += 1
                    o.sigval = c
                    o.sem = sems[e]
        for g, lst in self.dma_groups.items():
            s = self.stack.enter_context(nc.semaphore(f"d_{len(sems)}"))
            sems["dma:" + g] = s
            if g.startswith("all:"):
                for o in lst:
                    o.sem = s
                    o.sigval = 16 * len(lst)
            else:
                for i, o in enumerate(lst):
                    o.sem = s
                    o.sigval = 16 * (i + 1)
        engmap = {"pe": "tensor", "act": "scalar", "dve": "vector", "pool": "gpsimd", "sp": "sync"}
        with nc.Block() as block:
            for e in ENGS:
                ops = self.ops[e]
                if not ops:
                    continue

                def body(eng, ops=ops):
                    waited = {}
                    for o in ops:
                        need = {}
                        for d in o.deps:
                            k = id(d.sem)
                            if d.sigval > need.get(k, (None, 0))[1]:
                                need[k] = (d.sem, d.sigval)
                        for k, (s, v) in need.items():
                            if waited.get(k, 0) >= v:
                                continue
                            eng.wait_ge(s, v)
                            waited[k] = v
                        inst = o.fn(eng)
                        if inst is None:
                            continue
                        if o.is_dma:
                            inst.then_inc(o.sem, 16)
                        elif o.signal:
                            inst.then_inc(o.sem, 1)

                getattr(block, engmap[e])(body)

    def close(self):
        self.stack.close()


def _pair(a):
    a = np.asarray(a, dtype=np.float32)
    rest = a.shape[2:]
    a = a.reshape((16, 2, 64) + rest)
    perm = (1, 2, 0) + tuple(range(3, 3 + len(rest)))
    return np.ascontiguousarray(a.transpose(perm).reshape((128, 16) + rest))


def _cols(v, n):
    return np.ascontiguousarray(np.asarray(v, dtype=np.float32).reshape(n, 128).T)


def prep_common(inp):
    f = lambda a: np.ascontiguousarray(np.asarray(a, dtype=np.float32))
    com = {}
    com["gmix"] = f(np.asarray(inp["norm_mix_g"]).reshape(4, 8, 128).transpose(2, 0, 1))
    com["gffn"] = f(np.asarray(inp["norm_ffn_g"]).reshape(4, 8, 128).transpose(2, 0, 1))
    com["gfin"] = _cols(inp["norm_final_g"], 8)
    for j in range(2):
        com[f"ewin{j}"] = f(inp["even_w_in"][j])
        com[f"econv{j}"] = f(np.asarray(inp["even_conv_w"][j]).reshape(3, 4, 128).transpose(2, 1, 0))
        ls = np.broadcast_to(np.asarray(inp["ssm_log_step"][j])[:, None], (32, 64))
        com[f"sls{j}"] = _pair(ls)
        com[f"sare{j}"] = _pair(inp["ssm_a_re"][j])
        com[f"saim{j}"] = _pair(inp["ssm_a_im"][j])
        com[f"sbre{j}"] = _pair(inp["ssm_b_re"][j])
        com[f"sbim{j}"] = _pair(inp["ssm_b_im"][j])
        com[f"scre{j}"] = _pair(np.asarray(inp["ssm_c_re"][j]).transpose(0, 2, 1))
        com[f"scim{j}"] = _pair(np.asarray(inp["ssm_c_im"][j]).transpose(0, 2, 1))
        com[f"sd{j}"] = _cols(inp["ssm_d"][j], 4)
        com[f"glw{j}"] = f(inp["ssm_glu_w"][j])
        com[f"glb{j}"] = _cols(inp["ssm_glu_b"][j], 4)
        com[f"ewout{j}"] = f(inp["even_w_out"][j])
        com[f"owin{j}"] = f(inp["odd_w_in"][j])
        com[f"poolw{j}"] = f(np.asarray(inp["pool_w"][j]).transpose(1, 0, 2))
        com[f"pools{j}"] = _cols(inp["pool_scale"][j], 4)
        com[f"sgng{j}"] = f(np.broadcast_to(np.asarray(inp["sgu_norm_g"][j])[None, :], (128, 512)))
        com[f"sgw{j}"] = f(np.asarray(inp["sgu_w"][j]).transpose(2, 0, 1))
        com[f"sgb{j}"] = f(np.broadcast_to(np.asarray(inp["sgu_b"][j])[None], (128, 4, 128)))
        com[f"owout{j}"] = f(inp["odd_w_out"][j])
    for i in range(4):
        com[f"fup{i}"] = f(inp["ffn_w_up"][i])
        com[f"fcw{i}"] = f(np.asarray(inp["ffn_conv_w"][i]).reshape(3, 44, 128).transpose(2, 1, 0))
        com[f"fcb{i}"] = _cols(inp["ffn_conv_b"][i], 44)
        com[f"fdn{i}"] = f(inp["ffn_w_down"][i])
    s = np.arange(128)
    com["trilh"] = f(0.5 * (s[:, None] <= s[None, :]))
    invc = np.zeros((128, 4, 16), np.float32)
    for g, w in enumerate((2, 4, 8, 16)):
        invc[:, g, :] = 1.0 / np.minimum(np.arange(1, 17), w)
    com["invc"] = invc
    com["ident"] = f(np.eye(128))
    return com


SHAPES = {
    "gmix": [128, 4, 8], "gffn": [128, 4, 8], "gfin": [128, 8],
    "trilh": [128, 128], "invc": [128, 4, 16], "ident": [128, 128],
}
for _j in range(2):
    SHAPES.update({
        f"ewin{_j}": [1024, 2048], f"econv{_j}": [128, 4, 3], f"sls{_j}": [128, 16],
        f"sare{_j}": [128, 16], f"saim{_j}": [128, 16], f"sbre{_j}": [128, 16, 16],
        f"sbim{_j}": [128, 16, 16], f"scre{_j}": [128, 16, 16], f"scim{_j}": [128, 16, 16],
        f"sd{_j}": [128, 4], f"glw{_j}": [512, 512], f"glb{_j}": [128, 4], f"ewout{_j}": [1024, 1024],
        f"owin{_j}": [1024, 1536], f"poolw{_j}": [128, 4, 128], f"pools{_j}": [128, 4],
        f"sgng{_j}": [128, 512], f"sgw{_j}": [128, 4, 128], f"sgb{_j}": [128, 4, 128],
        f"owout{_j}": [1024, 1024],
    })
for _i in range(4):
    SHAPES.update({f"fup{_i}": [1024, 5632], f"fcw{_i}": [128, 44, 3], f"fcb{_i}": [128, 44],
                   f"fdn{_i}": [2816, 1024]})


class V:
    __slots__ = ("ap", "reg")

    def __init__(self, ap, reg):
        self.ap = ap
        self.reg = reg


def build(layers=(0, 1, 2, 3), ntiles=8, final=True, dbg=()):
    nc = bass.Bass("TRN2", target_bir_lowering=False)
    P = Prog(nc)
    dr = {}
    dbg_out = {}

    def din(name):
        if name not in dr:
            dr[name] = nc.dram_tensor(name, SHAPES[name], F32, kind="ExternalInput").ap()
        return dr[name]

    xT = nc.dram_tensor("xT", [D, SEQ], F32, kind="ExternalInput").ap()
    outT = nc.dram_tensor("outT", [D, SEQ], F32, kind="ExternalOutput").ap()
    xTv = xT.rearrange("(kt p) t -> p kt t", p=128)
    outTv = outT.rearrange("(kt p) t -> p kt t", p=128)
    nlay = len(layers)
    scr = nc.dram_tensor("scr", [nlay * 26, 128, 4096], BF16, kind="Internal").ap()
    tabd = nc.dram_tensor("tabd", [2, 128, 2 * 16 * TS], F32, kind="Internal").ap()

    xres = P.sb([128, 8, T], F32, "xres")
    hb = P.sb([128, 8, T], BF16, "hb")
    ring = P.sb([128, NSLOT, 4096], BF16, "ring")
    stage = P.sb([128, 2, 2048], F32, "stage")
    tab = P.sb([128, 2, 16, TS], F32, "tab")
    AW = 17920
    arena = P.sb([128, AW], F32, "arena")
    ones = P.sb([128, 128], BF16, "ones")
    onesf = P.sb([128, 128], F32, "onesf")
    ident = P.sb([128, 128], F32, "ident_sb")
    cst = P.sb([128, 4], F32, "cst")
    gmix = P.sb([128, 4, 8], F32, "gmix_sb")
    gffn = P.sb([128, 4, 8], F32, "gffn_sb")
    gfin = P.sb([128, 8], F32, "gfin_sb")
    fcw = P.sb([128, 4, 44, 3], F32, "fcw_sb")
    fcb = P.sb([128, 4, 44], F32, "fcb_sb")
    fhist = P.sb([128, 4, 44, 2], F32, "fhist")
    cxhist = P.sb([128, 2, 4, 2], F32, "cxhist")
    zhist = P.sb([128, 2, 4, 16], F32, "zhist")
    qinit = P.sb([128, 2, 2, 16], F32, "qinit")
    qend = P.sb([128, 2, 16], F32, "qend")
    econv = P.sb([128, 2, 4, 3], F32, "econv_sb")
    Bl = P.sb([128, 2, 4, 2, 128], BF16, "Bl")
    Cl = P.sb([128, 2, 16, 3, 32], BF16, "Cl")
    rdec = P.sb([128, 2, 16], F32, "rdec")
    rotc = P.sb([128, 2, 2, 16], F32, "rotc")
    dq = P.sb([128, 2, 4], F32, "dq")
    gbh = P.sb([128, 2, 4], F32, "gbh")
    pools = P.sb([128, 2, 4], F32, "pools_sb")
    poolw = P.sb([128, 2, 4, 128], BF16, "poolw_sb")
    wT = P.sb([128, 2, 4, 128], BF16, "wT")
    bh = P.sb([128, 2, 4, 128], F32, "bh")
    ghbc = P.sb([128, 2, 512], F32, "ghbc")
    invc = P.sb([128, 4, 16], F32, "invc_sb")
    banks = [P.ps([128, 512], F32, f"bank{i}") for i in range(8)]

    def bankv(i, lo=0, hi=512):
        return V(banks[i][:, lo:hi], ("bank", i))

    class Arena:
        def __init__(self):
            self.p = 0

        def reset(self, p=0):
            self.p = p

        def take(self, words, dtype=F32, shape=None):
            lo = self.p
            self.p += (words + 7) // 8 * 8
            assert self.p <= AW, (self.p, AW)
            ap = arena[:, lo:lo + words]
            if dtype == BF16:
                ap = ap.bitcast(BF16)
            return V(ap, ("@ar", lo, lo + words))

    AR = Arena()

    def R(*vs):
        out = []
        for v in vs:
            if v is None:
                continue
            out.append(v.reg if isinstance(v, V) else v)
        return out

    def act(out, in_, func, reads, writes, bias=None, scale=None, accum=None):
        kw = {}
        if bias is not None:
            kw["bias"] = bias
        if scale is not None:
            kw["scale"] = scale
        if accum is not None:
            kw["accum_out"] = accum
        return P.op("act", lambda e: e.activation(out, in_, func, **kw), reads, writes)

    def tt(eng, out, a, b, op, reads, writes):
        return P.op(eng, lambda e: e.tensor_tensor(out, a, b, op), reads, writes)

    def ts(eng, out, a, s1, s2, op0, op1, reads, writes):
        if op1 is None:
            return P.op(eng, lambda e: e.tensor_scalar(out, a, s1, None, op0), reads, writes)
        return P.op(eng, lambda e: e.tensor_scalar(out, a, s1, s2, op0, op1), reads, writes)

    def stt(out, a, s, b, op0, op1, reads, writes):
        return P.op("dve", lambda e: e.scalar_tensor_tensor(out, a, s, b, op0, op1), reads, writes)

    def cp(eng, out, in_, reads, writes):
        if eng == "act":
            return P.op("act", lambda e: e.activation(out, in_, AF.Copy), reads, writes)
        return P.op(eng, lambda e: e.tensor_copy(out, in_), reads, writes)

    def mm(out, lhsT, rhs, start, stop, reads, writes, tp=None):
        if tp is None:
            return P.op("pe", lambda e: e.matmul(out, lhsT, rhs, start=start, stop=stop), reads, writes)
        return P.op("pe", lambda e: e.matmul(out, lhsT, rhs, start=start, stop=stop, tile_position=tp),
                    reads, writes)

    def dump(name, ap, shape, reads):
        if name not in dbg:
            return
        t = nc.dram_tensor("dbg_" + name, list(shape), ap.dtype, kind="ExternalOutput").ap()
        dbg_out[name] = t
        P.dma("act", t, ap, "all:dbg", reads=reads, writes=[("dbgout", name)])

    smc = {"n": 0}

    def small_dma(dst_ap, src_ap, writes):
        g = "sm%d" % (smc["n"] % 4)
        smc["n"] += 1
        P.dma("sp", dst_ap, src_ap, g, writes=writes)

    def load_small(dst_ap, name, key):
        small_dma(dst_ap, din(name), [key])

    P.op("dve", lambda e: e.memset(ones[:], 1.0), writes=["ones"])
    P.op("dve", lambda e: e.memset(onesf[:], 1.0), writes=["onesf"])
    P.op("dve", lambda e: e.memset(cst[:, 0:1], -0.5), writes=["cst"])
    P.op("dve", lambda e: e.memset(cst[:, 1:2], EPS), reads=["cst"], writes=["cst"])
    P.op("dve", lambda e: e.memset(cst[:, 2:3], 4.0), reads=["cst"], writes=["cst"])
    P.op("dve", lambda e: e.memset(cst[:, 3:4], 1.0), reads=["cst"], writes=["cst"])
    P.op("dve", lambda e: e.memset(fhist[:], 0.0), writes=["fhist_all"])
    P.op("dve", lambda e: e.memset(cxhist[:], 0.0), writes=["cxhist_all"])
    P.op("dve", lambda e: e.memset(zhist[:], 0.0), writes=["zhist_all"])
    P.op("dve", lambda e: e.memset(qinit[:], 0.0), writes=["qinit_all"])
    load_small(gmix[:], "gmix", "gmix")
    load_small(gffn[:], "gffn", "gffn")
    load_small(gfin[:], "gfin", "gfin")
    load_small(invc[:], "invc", "invc")
    for i in layers:
        load_small(fcw[:, i], f"fcw{i}", ("fcw", i))
        load_small(fcb[:, i], f"fcb{i}", ("fcb", i))

    def wview(name, K):
        return din(name).rearrange("(kt p) n -> p kt n", p=128)

    def slabs_for(i):
        j = i // 2
        L = []
        if i % 2 == 0:
            w = wview(f"ewin{j}", 1024)
            L.append((8, 512, [(w, 0, 0, 256), (w, 1024, 256, 256)]))
            L.append((8, 512, [(w, 256, 0, 256), (w, 1280, 256, 256)]))
            L.append((8, 512, [(w, 1536, 0, 512)]))
            L.append((8, 512, [(w, 512, 0, 512)]))
            L.append((4, 512, [(wview(f"glw{j}", 512), 0, 0, 512)]))
            wo = wview(f"ewout{j}", 1024)
        else:
            w = wview(f"owin{j}", 1024)
            L.append((8, 512, [(w, 0, 0, 512)]))
            L.append((8, 512, [(w, 512, 0, 512)]))
            L.append((8, 512, [(w, 1024, 0, 512)]))
            wo = wview(f"owout{j}", 1024)
        L.append((8, 512, [(wo, 0, 0, 512)]))
        L.append((8, 512, [(wo, 512, 0, 512)]))
        wu = wview(f"fup{i}", 1024)
        for k in range(11):
            L.append((8, 512, [(wu, 256 * k, 0, 256), (wu, 2816 + 256 * k, 256, 256)]))
        wd = wview(f"fdn{i}", 2816)
        for m in range(8):
            L.append((22, 128, [(wd, 128 * m, 0, 128)]))
        return L

    lay_slabs = {i: slabs_for(i) for i in layers}
    seq = []
    for t in range(ntiles):
        for li, i in enumerate(layers):
            for s in range(len(lay_slabs[i])):
                seq.append((t, li, i, s))
    stream = {"next": 0, "cur": 0, "cast": 0, "stg": 0}

    def slot_reg(slot):
        return ("@ring%d" % slot, 0, 4096)

    def make_load(n):
        t, li, i, s = seq[n]
        KT, W, pieces = lay_slabs[i][s]
        slot = n % NSLOT
        sid = li * 26 + s
        nel = KT * W
        if t == 0:
            h0 = (KT + 1) // 2
            for (k0, k1) in ((0, h0), (h0, KT)):
                if k1 <= k0:
                    continue
                sg = stream["stg"] % 2
                stream["stg"] += 1
                nk = k1 - k0
                sview = stage[:, sg, 0:nk * W].rearrange("p (k w) -> p k w", k=nk)
                for pi, (w, c0, d0, wd_) in enumerate(pieces):
                    P.dma("sp", sview[:, :, d0:d0 + wd_], w[:, k0:k1, c0:c0 + wd_], "stg%d" % sg,
                          writes=[("stage", sg, pi)])
                eng = "act"
                stream["cast"] += 1
                dst = ring[:, slot, k0 * W:k1 * W]
                src = stage[:, sg, 0:nk * W]
                cp(eng, dst, src, [("stage", sg, pi) for pi in range(len(pieces))],
                   [("@ring%d" % slot, k0 * W, k1 * W)])
            P.dma("act", scr[sid][:, 0:nel], ring[:, slot, 0:nel], "scrst%d" % slot,
                  reads=[("@ring%d" % slot, 0, nel)], writes=[("scr", sid)])
        else:
            P.dma("sp", ring[:, slot, 0:nel], scr[sid][:, 0:nel], "ring%d" % slot,
                  reads=[("scr", sid)], writes=[("@ring%d" % slot, 0, nel)])

    def next_slab():
        n = stream["cur"]
        stream["cur"] += 1
        while stream["next"] < min(len(seq), n + NSLOT):
            make_load(stream["next"])
            stream["next"] += 1
        t, li, i, s = seq[n]
        KT, W, _ = lay_slabs[i][s]
        slot = n % NSLOT
        view = ring[:, slot, 0:KT * W].rearrange("p (k w) -> p k w", k=KT)
        return V(view, slot_reg(slot))

    rot = {"main": 0, "bu": 0, "yb": 0}
    pools_ = {"main4": [0, 1, 2, 3], "main8": [0, 1, 2, 3, 4, 5, 6, 7], "bu": [4, 5, 0, 1], "yb": [6, 7]}

    def nb(pool):
        key = "main" if pool.startswith("main") else pool
        lst = pools_[pool]
        b = lst[rot[key] % len(lst)]
        rot[key] += 1
        return b

    load_small(ident[:], "ident", "ident")
    even_js = [i // 2 for i in layers if i % 2 == 0]
    odd_js = [i // 2 for i in layers if i % 2 == 1]

    def ssm_setup(j, jj):
        AR.reset()
        sm = lambda: AR.take(16)
        ls, are, aim = sm(), sm(), sm()
        for v, nm in ((ls, "sls"), (are, "sare"), (aim, "saim")):
            small_dma(v.ap, din(f"{nm}{j}"), R(v))
        big = {}
        for nm in ("sbre", "sbim", "scre", "scim"):
            big[nm] = AR.take(256)
            small_dma(big[nm].ap.rearrange("p (k h) -> p k h", k=16), din(f"{nm}{j}"), R(big[nm]))
        small_dma(econv[:, j], din(f"econv{j}"), [("econv", j)])
        sdl, glbl = AR.take(4), AR.take(4)
        small_dma(sdl.ap, din(f"sd{j}"), R(sdl))
        small_dma(glbl.ap, din(f"glb{j}"), R(glbl))
        ts("dve", dq[:, j, :], sdl.ap, 0.25, None, ALU.mult, None, R(sdl), [("dq", j)])
        ts("dve", gbh[:, j, :], glbl.ap, 0.5, None, ALU.mult, None, R(glbl), [("gbh", j)])

        dt_, xr, th = sm(), sm(), sm()
        act(dt_.ap, ls.ap, AF.Exp, R(ls), R(dt_))
        tt("dve", xr.ap, are.ap, dt_.ap, ALU.mult, R(are, dt_), R(xr))
        tt("dve", th.ap, aim.ap, dt_.ap, ALU.mult, R(aim, dt_), R(th))
        rv = V(rdec[:, j, :], ("rdec", j))
        act(rv.ap, xr.ap, AF.Exp, R(xr), R(rv))
        al, a2, ps_, pc_ = sm(), sm(), sm(), sm()
        ts("dve", al.ap, th.ap, 1.0 / 64, None, ALU.mult, None, R(th), R(al))
        tt("dve", a2.ap, al.ap, al.ap, ALU.mult, R(al), R(a2))

        def horner(p, coefs):
            ts("dve", p.ap, a2.ap, coefs[0], coefs[1], ALU.mult, ALU.add, R(a2), R(p))
            for c in coefs[2:]:
                tt("dve", p.ap, p.ap, a2.ap, ALU.mult, R(p, a2), R(p))
                ts("dve", p.ap, p.ap, c, None, ALU.add, None, R(p), R(p))

        horner(ps_, [1.0 / 362880, -1.0 / 5040, 1.0 / 120, -1.0 / 6, 1.0])
        tt("dve", ps_.ap, ps_.ap, al.ap, ALU.mult, R(ps_, al), R(ps_))
        horner(pc_, [-1.0 / 3628800, 1.0 / 40320, -1.0 / 720, 1.0 / 24, -0.5, 1.0])
        t1, t2 = sm(), sm()
        for _ in range(6):
            tt("dve", t1.ap, pc_.ap, pc_.ap, ALU.mult, R(pc_), R(t1))
            tt("dve", t2.ap, ps_.ap, ps_.ap, ALU.mult, R(ps_), R(t2))
            tt("dve", ps_.ap, ps_.ap, pc_.ap, ALU.mult, R(ps_, pc_), R(ps_))
            ts("dve", ps_.ap, ps_.ap, 2.0, None, ALU.mult, None, R(ps_), R(ps_))
            tt("dve", pc_.ap, t1.ap, t2.ap, ALU.subtract, R(t1, t2), R(pc_))
        c1, s1 = pc_, ps_
        nre, nim, den, fre, fim = sm(), sm(), sm(), sm(), sm()
        tt("dve", nre.ap, rv.ap, c1.ap, ALU.mult, R(rv, c1), R(nre))
        ts("dve", nre.ap, nre.ap, -1.0, None, ALU.add, None, R(nre), R(nre))
        tt("dve", nim.ap, rv.ap, s1.ap, ALU.mult, R(rv, s1), R(nim))
        tt("dve", den.ap, are.ap, are.ap, ALU.mult, R(are), R(den))
        tt("dve", t1.ap, aim.ap, aim.ap, ALU.mult, R(aim), R(t1))
        tt("dve", den.ap, den.ap, t1.ap, ALU.add, R(den, t1), R(den))
        P.op("dve", lambda e: e.reciprocal(den.ap, den.ap), R(den), R(den))
        tt("dve", fre.ap, nre.ap, are.ap, ALU.mult, R(nre, are), R(fre))
        tt("dve", t1.ap, nim.ap, aim.ap, ALU.mult, R(nim, aim), R(t1))
        tt("dve", fre.ap, fre.ap, t1.ap, ALU.add, R(fre, t1), R(fre))
        tt("dve", fre.ap, fre.ap, den.ap, ALU.mult, R(fre, den), R(fre))
        tt("dve", fim.ap, nim.ap, are.ap, ALU.mult, R(nim, are), R(fim))
        tt("dve", t1.ap, nre.ap, aim.ap, ALU.mult, R(nre, aim), R(t1))
        tt("dve", fim.ap, fim.ap, t1.ap, ALU.subtract, R(fim, t1), R(fim))
        tt("dve", fim.ap, fim.ap, den.ap, ALU.mult, R(fim, den), R(fim))
        v3 = lambda v: v.ap.rearrange("p (k h) -> p k h", k=16)
        bc = lambda v: v.ap.unsqueeze(2).to_broadcast([128, 16, 16])
        Bre, Bim, tA = AR.take(256), AR.take(256), AR.take(256)
        tt("dve", v3(Bre), v3(big["sbre"]), bc(fre), ALU.mult, R(big["sbre"], fre), R(Bre))
        tt("dve", v3(tA), v3(big["sbim"]), bc(fim), ALU.mult, R(big["sbim"], fim), R(tA))
        tt("dve", v3(Bre), v3(Bre), v3(tA), ALU.subtract, R(Bre, tA), R(Bre))
        tt("dve", v3(Bim), v3(big["sbim"]), bc(fre), ALU.mult, R(big["sbim"], fre), R(Bim))
        tt("dve", v3(tA), v3(big["sbre"]), bc(fim), ALU.mult, R(big["sbre"], fim), R(tA))
        tt("dve", v3(Bim), v3(Bim), v3(tA), ALU.add, R(Bim, tA), R(Bim))
        for comp, Bv in ((0, Bre), (1, Bim)):
            M = AR.take(512)
            P.op("dve", lambda e, M=M: e.memset(M.ap, 0.0), (), R(M))
            M4 = M.ap.rearrange("p (k g h) -> p k g h", k=16, g=2)
            B3 = v3(Bv)
            cp("dve", M4[0:64, :, 0, :], B3[0:64], R(Bv, M), R(M))
            cp("dve", M4[64:128, :, 1, :], B3[64:128], R(Bv, M), R(M))
            for b in range(4):
                bk = nb("main8")
                P.op("pe", lambda e, bk=bk, M=M, b=b: e.transpose(banks[bk][:, 0:128],
                                                                  M.ap[:, b * 128:(b + 1) * 128], ident[:]),
                     R(M) + ["ident"], [("bank", bk)])
                cp("act", Bl[:, j, b, comp, :], banks[bk][:, 0:128], [("bank", bk)], [("Bl", j)])
        P.op("dve", lambda e: e.memset(Cl[:, j], 0.0), (), [("Cl", j)])
        for m, (nm, sc) in enumerate((("scre", 0.25), ("scre", -0.25), ("scim", -0.25))):
            src = v3(big[nm])
            for g2 in range(2):
                ts("dve", Cl[64 * g2:64 * g2 + 64, j, :, m, 16 * g2:16 * g2 + 16], src[64 * g2:64 * g2 + 64],
                   sc, None, ALU.mult, None, R(big[nm]) + [("Cl", j)], [("Cl", j)])
        tabv = ("tab",)
        cp("dve", tab[:, 0, :, 0:1], c1.ap.unsqueeze(2), R(c1) + ["tab"], ["tab"])
        cp("dve", tab[:, 1, :, 0:1], s1.ap.unsqueeze(2), R(s1) + ["tab"], ["tab"])
        w1, w2 = AR.take(2048), AR.take(2048)
        n = 1
        while n < TS:
            cb = tab[:, 0, :, n - 1:n].to_broadcast([128, 16, n])
            sb_ = tab[:, 1, :, n - 1:n].to_broadcast([128, 16, n])
            a1 = w1.ap[:, 0:16 * n].rearrange("p (k t) -> p k t", k=16)
            a2_ = w2.ap[:, 0:16 * n].rearrange("p (k t) -> p k t", k=16)
            tt("dve", a1, tab[:, 0, :, 0:n], cb, ALU.mult, ["tab"], R(w1))
            tt("dve", a2_, tab[:, 1, :, 0:n], sb_, ALU.mult, ["tab"], R(w2))
            tt("dve", tab[:, 0, :, n:2 * n], a1, a2_, ALU.subtract, R(w1, w2) + ["tab"], ["tab"])
            tt("dve", a1, tab[:, 1, :, 0:n], cb, ALU.mult, ["tab"], R(w1))
            tt("dve", a2_, tab[:, 0, :, 0:n], sb_, ALU.mult, ["tab"], R(w2))
            tt("dve", tab[:, 1, :, n:2 * n], a1, a2_, ALU.add, R(w1, w2) + ["tab"], ["tab"])
            n *= 2
        cp("dve", rotc[:, j, 0, :].unsqueeze(2), tab[:, 0, :, TS - 1:TS], ["tab"], [("rotc", j)])
        cp("dve", rotc[:, j, 1, :].unsqueeze(2), tab[:, 1, :, TS - 1:TS], ["tab"], [("rotc", j)])
        P.dma("act", tabd[jj], tab[:].rearrange("p c k t -> p (c k t)"), "tabst", reads=["tab"],
              writes=[("tabd", jj)])

    def odd_setup(j):
        AR.reset()
        small_dma(pools[:, j, :], din(f"pools{j}"), [("pools", j)])
        pw = AR.take(512)
        small_dma(pw.ap.rearrange("p (g d) -> p g d", g=4), din(f"poolw{j}"), R(pw))
        cp("dve", poolw[:, j].rearrange("p g d -> p (g d)"), pw.ap, R(pw), [("poolw", j)])
        sw, tr = AR.take(512), AR.take(128)
        small_dma(sw.ap.rearrange("p (g d) -> p g d", g=4), din(f"sgw{j}"), R(sw))
        small_dma(tr.ap, din("trilh"), R(tr))
        tt("dve", wT[:, j], sw.ap.rearrange("p (g d) -> p g d", g=4),
           tr.ap.unsqueeze(1).to_broadcast([128, 4, 128]), ALU.mult, R(sw, tr), [("wT", j)])
        sb2 = AR.take(512)
        small_dma(sb2.ap.rearrange("p (g d) -> p g d", g=4), din(f"sgb{j}"), R(sb2))
        ts("dve", bh[:, j].rearrange("p g d -> p (g d)"), sb2.ap, 0.5, None, ALU.mult, None, R(sb2),
           [("bh", j)])
        gg = AR.take(512)
        small_dma(gg.ap, din(f"sgng{j}"), R(gg))
        ts("dve", ghbc[:, j, :], gg.ap, 0.5, None, ALU.mult, None, R(gg), [("ghbc", j)])

    tab_state = {"j": None}

    dq_ = []

    def later(delay, fn):
        dq_.append([delay, fn])

    def tick():
        due = [x for x in dq_ if x[0] <= 0]
        for x in due:
            dq_.remove(x)
        for x in dq_:
            x[0] -= 1
        for x in due:
            x[1]()

    def flush():
        while dq_:
            tick()

    def sub(v, a, b):
        return V(v.ap[:, a:b], ("@ar", v.reg[1] + a, v.reg[1] + b))

    def xr_(kt):
        return V(xres[:, kt, :], ("xres", kt))

    def hb_(kt):
        return V(hb[:, kt, :], ("hb", kt))

    def rmsnorm(gap, gkey, outs, bf=True):
        AR.reset()
        sq = [AR.take(256, BF16) for _ in range(8)]
        ms4, rs4 = AR.take(4), AR.take(4)
        dg4 = AR.take(512)
        bk = nb("main8")
        for kt in range(8):
            act(sq[kt].ap, xres[:, kt, :], AF.Square, [("xres", kt)], R(sq[kt]))
        for tb_ in range(4):
            for kt in range(8):
                mm(banks[bk][:, tb_:tb_ + 1], sq[kt].ap[:, tb_ * 128:(tb_ + 1) * 128], ones[:, 0:1],
                   kt == 0, kt == 7, ["ones"] + R(sq[kt]), [("bank", bk)])
        ts("dve", ms4.ap, banks[bk][:, 0:4], 1.0 / D, EPS, ALU.mult, ALU.add, [("bank", bk)], R(ms4))
        tt("pool", rs4.ap, ms4.ap, cst[:, 0:1].to_broadcast([128, 4]), ALU.pow, R(ms4) + ["cst"], R(rs4))
        bk2 = nb("main8")
        tt("dve", dg4.ap.rearrange("p (a b) -> p a b", a=4), ident[:].unsqueeze(1).to_broadcast([128, 4, 128]),
           rs4.ap.unsqueeze(2).to_broadcast([128, 4, 128]), ALU.mult, R(rs4) + ["ident"], R(dg4))
        for tb_ in range(4):
            mm(banks[bk2][:, tb_ * 128:(tb_ + 1) * 128], onesf[:], dg4.ap[:, tb_ * 128:(tb_ + 1) * 128], True, True,
               ["onesf"] + R(dg4), [("bank", bk2)])
        for kt in range(8):
            stt(outs[kt].ap, xres[:, kt, :], gap[:, kt:kt + 1], banks[bk2][:, :], ALU.mult, ALU.mult,
                [("xres", kt), gkey, ("bank", bk2)], R(outs[kt]))

    def proj_chunk(Wv, col, pool, rhs_list, KT=8):
        bk = nb(pool)
        for kt in range(KT):
            mm(banks[bk][:, :], Wv.ap[:, kt, col:col + 128], rhs_list[kt].ap, kt == 0, kt == KT - 1,
               [Wv.reg] + R(rhs_list[kt]), [("bank", bk)])
        return bk

    def out_proj(ycat, pool):
        for half in range(2):
            Wo = next_slab()
            for m_ in range(4):
                m = half * 4 + m_
                bk = proj_chunk(Wo, m_ * 128, pool, ycat)
                tt("dve", xres[:, m, :], xres[:, m, :], banks[bk][:, :], ALU.add,
                   [("xres", m), ("bank", bk)], [("xres", m)])

    def gelu2(bk, tb, Gout):
        cp("act", tb["pc"].ap, banks[bk][:, :], [("bank", bk)], R(tb["pc"]))
        act(tb["p2"].ap, banks[bk][:, :], AF.Square, [("bank", bk)], R(tb["p2"]))
        act(tb["ti"].ap, tb["p2"].ap, AF.Identity, R(tb["p2"]) + ["cst"], R(tb["ti"]), bias=cst[:, 3:4], scale=0.044715)
        tt("dve", tb["ti"].ap, tb["ti"].ap, tb["pc"].ap, ALU.mult, R(tb["ti"], tb["pc"]), R(tb["ti"]))
        act(tb["th"].ap, tb["ti"].ap, AF.Tanh, R(tb["ti"]), R(tb["th"]), scale=0.7978845608028654)
        stt(Gout.ap, tb["th"].ap, 1.0, tb["pc"].ap, ALU.add, ALU.mult, R(tb["th"], tb["pc"]), R(Gout))

    def even_mixer(i, t):
        j = i // 2
        hbl = [hb_(kt) for kt in range(8)]
        AR.reset()
        xa = [AR.take(512) for _ in range(4)]
        cxb = [AR.take(514) for _ in range(4)]
        ycat = [AR.take(256, BF16) for _ in range(8)]
        u = [AR.take(512) for _ in range(4)]
        ub = [AR.take(256, BF16) for _ in range(4)]
        ssmb = [dict(bu=AR.take(512), m34=AR.take(512), q=AR.take(512),
                     X=AR.take(256, BF16), Y=AR.take(256, BF16)) for _ in range(3)]
        tmpb = [dict(yq=AR.take(256), y2=AR.take(256), ti=AR.take(256), th=AR.take(256)) for _ in range(2)]
        rt = [AR.take(16) for _ in range(4)]
        G = xa
        base = cxb[0].reg[1]
        gelu_bf = [V(arena[:, base + 256 * b:base + 256 * (b + 1)].bitcast(BF16),
                     ("@ar", base + 256 * b, base + 256 * (b + 1))) for b in range(4)]
        th2 = [V(arena[:, base + 1024 + 512 * x:base + 1024 + 512 * (x + 1)],
                 ("@ar", base + 1024 + 512 * x, base + 1024 + 512 * (x + 1))) for x in range(2)]
        if tab_state["j"] != j:
            jj = even_js.index(j)
            P.dma("sp", tab[:].rearrange("p c k t -> p (c k t)"), tabd[jj], "tabld",
                  reads=[("tabd", jj)], writes=["tab"])
            tab_state["j"] = j
        for sl in range(2):
            W = next_slab()
            for ii in range(2):
                c = 2 * sl + ii
                bx = proj_chunk(W, ii * 128, "main4", hbl)
                cp("act", xa[c].ap, banks[bx][:, :], [("bank", bx)], R(xa[c]))
                bc_ = proj_chunk(W, 256 + ii * 128, "main4", hbl)
                cxh, cxm = sub(cxb[c], 0, 2), sub(cxb[c], 2, 514)
                cp("pool", cxh.ap, cxhist[:, j, c, :], [("cxhist", j, c)], R(cxh))
                tt("dve", cxm.ap, banks[bc_][:, :], xa[c].ap, ALU.mult, [("bank", bc_)] + R(xa[c]), R(cxm))
                ek = ("econv", j)
                act(xa[c].ap, cxm.ap, AF.Identity, R(cxm) + [ek], R(xa[c]), scale=econv[:, j, c, 2:3])
                stt(xa[c].ap, cxb[c].ap[:, 1:513], econv[:, j, c, 1:2], xa[c].ap, ALU.mult, ALU.add,
                    R(cxb[c], xa[c]) + [ek], R(xa[c]))
                stt(xa[c].ap, cxb[c].ap[:, 0:512], econv[:, j, c, 0:1], xa[c].ap, ALU.mult, ALU.add,
                    R(cxb[c], xa[c]) + [ek], R(xa[c]))
                cp("pool", cxhist[:, j, c, :], cxb[c].ap[:, 512:514], R(cxm), [("cxhist", j, c)])
        W = next_slab()
        for b in range(4):
            bk = proj_chunk(W, b * 128, "main4", hbl)
            cp("act", u[b].ap, banks[bk][:, :], [("bank", bk)], R(u[b]))
            cp("pool", ub[b].ap, u[b].ap, R(u[b]), R(ub[b]))
        W = next_slab()
        for c in range(4):
            bk = proj_chunk(W, c * 128, "main4", hbl)
            tt("dve", ycat[c].ap, banks[bk][:, :], xa[c].ap, ALU.mult, [("bank", bk)] + R(xa[c]), R(ycat[c]))
        items = [(sb_i, b, q) for sb_i in range(T // TS) for b in range(4) for q in range(4)]
        v3 = lambda v: v.ap.rearrange("p (c t) -> p c t", c=2)
        stA = {}

        def stageA(n):
            sb_i, b, q = items[n]
            c0 = sb_i * TS
            S = ssmb[n % 3]
            bb = nb("bu")
            for comp in range(2):
                mm(banks[bb][:, comp * TS:(comp + 1) * TS], Bl[32 * q:32 * q + 32, j, b, comp, :],
                   ub[b].ap[32 * q:32 * q + 32, c0:c0 + TS], True, True,
                   [("Bl", j)] + R(ub[b]), [("bank", bb)], tp=(32 * q, 0))
            stA[n] = bb

        def stageB(n, yk):
            sb_i, b, q = items[n]
            k = 4 * b + q
            S = ssmb[n % 3]
            cbc = tab[:, 0, k, :].unsqueeze(1).to_broadcast([128, 2, TS])
            sbc = tab[:, 1, k, :].unsqueeze(1).to_broadcast([128, 2, TS])
            bb = stA.pop(n)
            bk3 = banks[bb][:, :].rearrange("p (c t) -> p c t", c=2)
            tt("dve", v3(S["m34"]), bk3, sbc, ALU.mult, [("bank", bb), "tab"], R(S["m34"]))
            tt("dve", v3(S["bu"]), bk3, cbc, ALU.mult, [("bank", bb), "tab"], R(S["bu"]))
            mre, mim = sub(S["bu"], 0, TS), sub(S["bu"], TS, 2 * TS)
            tt("dve", mre.ap, mre.ap, S["m34"].ap[:, TS:2 * TS], ALU.add, R(mre, S["m34"]), R(mre))
            tt("dve", mim.ap, mim.ap, S["m34"].ap[:, 0:TS], ALU.subtract, R(mim, S["m34"]), R(mim))
            rb = rdec[:, j, k:k + 1].to_broadcast([128, TS])
            for comp, mv in ((0, mre), (1, mim)):
                qv = sub(S["q"], comp * TS, (comp + 1) * TS)
                P.op("dve", lambda e, qv=qv, rb=rb, mv=mv, comp=comp, k=k: e.tensor_tensor_scan(
                    qv.ap, rb, mv.ap, qinit[:, j, comp, k:k + 1], ALU.mult, ALU.add),
                    R(mv) + [("rdec", j), ("qinit", j)], R(qv))
            Xv = S["X"].ap.rearrange("p (c t) -> p c t", c=2)
            Yv = S["Y"].ap.rearrange("p (c t) -> p c t", c=2)
            tt("dve", Xv, v3(S["q"]), cbc, ALU.mult, R(S["q"]) + ["tab"], R(S["X"]))
            tt("dve", Yv, v3(S["q"]), sbc, ALU.mult, R(S["q"]) + ["tab"], R(S["Y"]))
            cp("act", qend[:, :, k:k + 1], v3(S["q"])[:, :, TS - 1:TS], R(S["q"]), [("qend", k)])
            yo = banks[yk][32 * q:32 * q + 32, 0:TS]
            ops_ = ((0, S["X"], 0), (1, S["Y"], 1), (2, S["Y"], 0), (2, S["X"], 1))
            for n_, (m_, src, half) in enumerate(ops_):
                mm(yo, Cl[:, j, k, m_, :], src.ap[:, half * TS:(half + 1) * TS], n_ == 0, n_ == 3,
                   [("Cl", j)] + R(src), [("bank", yk)], tp=(0, 32 * q))

        def block_epilogue(sb_i, b, yk, ib):
            c0 = sb_i * TS
            tb = tmpb[ib % 2]
            Gs = sub(G[b], c0, c0 + TS)
            gb = V(gelu_bf[b].ap[:, c0:c0 + TS], ("@ar", gelu_bf[b].reg[1] + c0 // 2,
                                                 gelu_bf[b].reg[1] + (c0 + TS) // 2))

            def e1():
                stt(tb["yq"].ap, u[b].ap[:, c0:c0 + TS], dq[:, j, b:b + 1], banks[yk][:, 0:TS], ALU.mult, ALU.add,
                    R(u[b]) + [("dq", j), ("bank", yk)], R(tb["yq"]))
                act(tb["y2"].ap, tb["yq"].ap, AF.Square, R(tb["yq"]), R(tb["y2"]), scale=4.0)

            def e2():
                act(tb["ti"].ap, tb["y2"].ap, AF.Identity, R(tb["y2"]) + ["cst"], R(tb["ti"]), bias=cst[:, 2:3],
                    scale=4 * 0.044715)
                tt("dve", tb["ti"].ap, tb["ti"].ap, tb["yq"].ap, ALU.mult, R(tb["ti"], tb["yq"]), R(tb["ti"]))

            def e3():
                act(tb["th"].ap, tb["ti"].ap, AF.Tanh, R(tb["ti"]), R(tb["th"]), scale=0.7978845608028654)

            def e4():
                stt(Gs.ap, tb["th"].ap, 1.0, tb["yq"].ap, ALU.add, ALU.mult, R(tb["th"], tb["yq"]), R(Gs))
                act(gb.ap, Gs.ap, AF.Copy, R(Gs), R(gb), scale=2.0)

            later(0, e1)
            later(1, e2)
            later(2, e3)
            later(3, e4)

        def carry():
            qk = [("qend", k) for k in range(16)]
            cT, sT = rotc[:, j, 0, :], rotc[:, j, 1, :]
            rk = [("rotc", j)]
            tt("dve", rt[0].ap, cT, qend[:, 0, :], ALU.mult, qk + rk, R(rt[0]))
            tt("dve", rt[1].ap, sT, qend[:, 1, :], ALU.mult, qk + rk, R(rt[1]))
            tt("dve", rt[2].ap, sT, qend[:, 0, :], ALU.mult, qk + rk, R(rt[2]))
            tt("dve", rt[3].ap, cT, qend[:, 1, :], ALU.mult, qk + rk, R(rt[3]))
            tt("dve", qinit[:, j, 0, :], rt[0].ap, rt[1].ap, ALU.subtract, R(rt[0], rt[1]), [("qinit", j)])
            tt("dve", qinit[:, j, 1, :], rt[2].ap, rt[3].ap, ALU.add, R(rt[2], rt[3]), [("qinit", j)])

        NI = len(items)
        stageA(0)
        stageA(1)
        yk = None
        ib = 0
        for n in range(NI):
            sb_i, b, q = items[n]
            if n + 2 < NI:
                stageA(n + 2)
            if q == 0:
                yk = nb("yb")
            stageB(n, yk)
            tick()
            if q == 3:
                block_epilogue(sb_i, b, yk, ib)
                ib += 1
                if b == 3:
                    carry()
        flush()
        Wg = next_slab()
        for c in range(4):
            bk = proj_chunk(Wg, c * 128, "main4", gelu_bf, KT=4)
            tv = th2[c % 2]
            act(tv.ap, banks[bk][:, :], AF.Tanh, [("bank", bk), ("gbh", j)], R(tv), bias=gbh[:, j, c:c + 1], scale=0.5)
            stt(ycat[4 + c].ap, tv.ap, 1.0, G[c].ap, ALU.add, ALU.mult, R(tv, G[c]), R(ycat[4 + c]))
        out_proj(ycat, "main4")

    def ffn(i, t):
        hbl = [hb_(kt) for kt in range(8)]
        AR.reset(3072)
        hid = [AR.take(256, BF16) for _ in range(22)]
        acc = [AR.take(512) for _ in range(8)]
        sg = [AR.take(512) for _ in range(4)]
        for k in range(11):
            W = next_slab()
            for jj in range(2):
                jch = 2 * k + jj
                chs = [jch, jch + 22]
                bks = [proj_chunk(W, gv * 256 + jj * 128, "main8", hbl) for gv in range(2)]
                accs = [acc[2 * (jch % 4) + gv] for gv in range(2)]
                wk = [("fcw", i)]
                w_ = lambda ch, kk: fcw[:, i, ch, kk:kk + 1]
                for gv in range(2):
                    act(accs[gv].ap, banks[bks[gv]][:, :], AF.Identity, [("bank", bks[gv]), ("fcb", i)] + wk,
                        R(accs[gv]), bias=fcb[:, i, chs[gv]:chs[gv] + 1], scale=w_(chs[gv], 2))
                for gv in range(2):
                    a, bk, ch = accs[gv], bks[gv], chs[gv]
                    stt(a.ap[:, 1:T], banks[bk][:, 0:T - 1], w_(ch, 1), a.ap[:, 1:T], ALU.mult, ALU.add,
                        [("bank", bk)] + wk + R(a), R(a))
                for gv in range(2):
                    a, bk, ch = accs[gv], bks[gv], chs[gv]
                    stt(a.ap[:, 0:1], fhist[:, i, ch, 1:2], w_(ch, 1), a.ap[:, 0:1], ALU.mult, ALU.add,
                        [("fhist", i, ch)] + wk + R(a), R(a))
                for gv in range(2):
                    a, bk, ch = accs[gv], bks[gv], chs[gv]
                    stt(a.ap[:, 2:T], banks[bk][:, 0:T - 2], w_(ch, 0), a.ap[:, 2:T], ALU.mult, ALU.add,
                        [("bank", bk)] + wk + R(a), R(a))
                for gv in range(2):
                    a, bk, ch = accs[gv], bks[gv], chs[gv]
                    stt(a.ap[:, 0:2], fhist[:, i, ch, 0:2], w_(ch, 0), a.ap[:, 0:2], ALU.mult, ALU.add,
                        [("fhist", i, ch)] + wk + R(a), R(a))
                def fin(chs=chs, bks=bks, accs=accs, s_=sg[jch % 4], jch=jch):
                    for gv in range(2):
                        cp("act", fhist[:, i, chs[gv], :], banks[bks[gv]][:, T - 2:T], [("bank", bks[gv])],
                           [("fhist", i, chs[gv])])
                    act(s_.ap, accs[0].ap, AF.Silu, R(accs[0]), R(s_))
                    tt("pool", hid[jch].ap, s_.ap, accs[1].ap, ALU.mult, R(s_, accs[1]), R(hid[jch]))

                tick()
                later(0, fin)
        flush()
        for m in range(8):
            Wd = next_slab()
            bk = nb("main8")
            for jch in range(22):
                mm(banks[bk][:, :], Wd.ap[:, jch, :], hid[jch].ap, jch == 0, jch == 21,
                   [Wd.reg] + R(hid[jch]), [("bank", bk)])
            tt("dve", xres[:, m, :], xres[:, m, :], banks[bk][:, :], ALU.add,
               [("xres", m), ("bank", bk)], [("xres", m)])

    def odd_mixer(i, t):
        j = i // 2
        hbl = [hb_(kt) for kt in range(8)]
        AR.reset()
        zt = [AR.take(528) for _ in range(4)]
        sA, sB = AR.take(528), AR.take(528)
        pooled = [AR.take(256, BF16) for _ in range(4)]
        Gu = [AR.take(512) for _ in range(4)]
        tmpb = [dict(pc=AR.take(512), p2=AR.take(512), ti=AR.take(512), th=AR.take(512)) for _ in range(2)]
        Gv = [AR.take(512) for _ in range(2)]
        vtok = [AR.take(256, BF16) for _ in range(4)]
        ycat = [AR.take(256, BF16) for _ in range(8)]
        sgt = [AR.take(512) for _ in range(2)]
        sml = [dict(ss=AR.take(1), ms=AR.take(1), rs=AR.take(1)) for _ in range(2)]
        t15 = AR.take(16)
        W = next_slab()
        for g, w in enumerate((2, 4, 8, 16)):
            bk = proj_chunk(W, g * 128, "main8", hbl)
            zh, zm = sub(zt[g], 0, 16), sub(zt[g], 16, 528)
            cp("pool", zh.ap, zhist[:, j, g, :], [("zhist", j, g)], R(zh))
            cp("act", zm.ap, banks[bk][:, :], [("bank", bk)], R(zm))
            z = zt[g]
            tt("pool", sA.ap[:, 1:528], z.ap[:, 1:528], z.ap[:, 0:527], ALU.add, R(z), R(sA))
            Sv = sA
            if g >= 1:
                tt("pool", sB.ap[:, 3:528], sA.ap[:, 3:528], sA.ap[:, 1:526], ALU.add, R(sA), R(sB))
                Sv = sB
            if g >= 2:
                tt("pool", sA.ap[:, 7:528], sB.ap[:, 7:528], sB.ap[:, 3:524], ALU.add, R(sB), R(sA))
                Sv = sA
            if g >= 3:
                tt("pool", sB.ap[:, 15:528], sA.ap[:, 15:528], sA.ap[:, 7:520], ALU.add, R(sA), R(sB))
                Sv = sB
            stt(pooled[g].ap, Sv.ap[:, 16:528], 1.0 / w, zm.ap, ALU.mult, ALU.subtract, R(Sv, zm), R(pooled[g]))
            if t == 0:
                tt("dve", t15.ap[:, 0:15], Sv.ap[:, 16:31], invc[:, g, 0:15], ALU.mult, R(Sv) + ["invc"], R(t15))
                tt("dve", pooled[g].ap[:, 0:15], t15.ap[:, 0:15], zt[g].ap[:, 16:31], ALU.subtract,
                   R(t15, zm, pooled[g]), R(pooled[g]))
            cp("pool", zhist[:, j, g, :], zt[g].ap[:, 512:528], R(zm), [("zhist", j, g)])
            bk2 = nb("main8")
            mm(banks[bk2][:, :], poolw[:, j, g, :], pooled[g].ap, True, True, [("poolw", j)] + R(pooled[g]),
               [("bank", bk2)])
            act(ycat[g].ap, banks[bk2][:, :], AF.Identity, [("bank", bk2), ("pools", j)], R(ycat[g]),
                scale=pools[:, j, g:g + 1])
        W = next_slab()
        for c in range(4):
            bk = proj_chunk(W, c * 128, "main8", hbl)
            gelu2(bk, tmpb[c % 2], Gu[c])
        W = next_slab()
        for tb_ in range(4):
            bk = nb("main8")
            for kt in range(8):
                mm(banks[bk][:, :], hb[:, kt, tb_ * 128:(tb_ + 1) * 128], W.ap[:, kt, :], kt == 0, kt == 7,
                   [W.reg, ("hb", kt)], [("bank", bk)])
            tb = tmpb[tb_ % 2]
            gv = Gv[tb_ % 2]
            sm_ = sml[tb_ % 2]
            gelu2(bk, tb, gv)
            act(tb["p2"].ap, gv.ap, AF.Square, R(gv), R(tb["p2"], sm_["ss"]), scale=0.5, accum=sm_["ss"].ap)
            ts("dve", sm_["ms"].ap, sm_["ss"].ap, 1.0 / 512, EPS, ALU.mult, ALU.add, R(sm_["ss"]), R(sm_["ms"]))
            tt("pool", sm_["rs"].ap, sm_["ms"].ap, cst[:, 0:1], ALU.pow, R(sm_["ms"]) + ["cst"], R(sm_["rs"]))
            stt(vtok[tb_].ap, gv.ap, sm_["rs"].ap, ghbc[:, j, :], ALU.mult, ALU.mult,
                R(gv, sm_["rs"]) + [("ghbc", j)], R(vtok[tb_]))
        for hd in range(4):
            bk = nb("main8")
            for tb_ in range(4):
                mm(banks[bk][:, tb_ * 128:(tb_ + 1) * 128], vtok[tb_].ap[:, hd * 128:(hd + 1) * 128],
                   wT[:, j, hd, :], True, True, [("wT", j)] + R(vtok[tb_]), [("bank", bk)])
            sv_ = sgt[hd % 2]
            tt("dve", sv_.ap.rearrange("p (a b) -> p a b", a=4),
               banks[bk][:, :].rearrange("p (a b) -> p a b", a=4),
               bh[:, j, hd, :].unsqueeze(1).to_broadcast([128, 4, 128]), ALU.add,
               [("bank", bk), ("bh", j)], R(sv_))
            tt("dve", ycat[4 + hd].ap, sv_.ap, Gu[hd].ap, ALU.mult, R(sv_, Gu[hd]), R(ycat[4 + hd]))
        out_proj(ycat, "main8")

    for t in range(ntiles):
        t0 = t * T
        P.dma("sp", xres[:, :, :], xTv[:, :, t0:t0 + T], "xload", writes=[("xres", kt) for kt in range(8)])
        for i in layers:
            j = i // 2
            if t == 0:
                if i % 2 == 0:
                    ssm_setup(j, even_js.index(j))
                    tab_state["j"] = j
                else:
                    odd_setup(j)
            rmsnorm(gmix[:, i, :], "gmix", [hb_(kt) for kt in range(8)])
            if i % 2 == 0:
                even_mixer(i, t)
            else:
                odd_mixer(i, t)
            rmsnorm(gffn[:, i, :], "gffn", [hb_(kt) for kt in range(8)])
            ffn(i, t)
        if final:
            AR.reset(3072)
            ost = [AR.take(512) for _ in range(8)]
            p_after = AR.p
            rmsnorm(gfin[:, :], "gfin", ost)
            AR.reset(p_after)
            ov = arena[:, ost[0].reg[1]:ost[0].reg[1] + 4096].rearrange("p (k t) -> p k t", k=8)
            P.dma("act", outTv[:, :, t0:t0 + T], ov, "ostore", reads=R(*ost), writes=[("outT", t)])
        else:
            P.dma("act", outTv[:, :, t0:t0 + T], xres[:, :, :], "ostore",
                  reads=[("xres", kt) for kt in range(8)], writes=[("outT", t)])
    P.wait("act", [("outT", t) for t in range(ntiles)])
    P.emit()
    P.close()
    names = ["xT"] + list(dr.keys())
    return nc, names, dbg_out


_CACHE = {}


def _get(layers, ntiles, final):
    key = (tuple(layers), ntiles, final)
    if key not in _CACHE:
        _CACHE[key] = build(layers, ntiles, final)
    return _CACHE[key]


def kernel(**inputs):
    com = prep_common(inputs)
    x = np.asarray(inputs["x"], dtype=np.float32)
    nb_ = x.shape[0]
    xTs = [np.ascontiguousarray(x[b].T) for b in range(nb_)]
    nc, names, _ = _get((0, 1, 2, 3), SEQ // T, True)
    in_maps = []
    for b in range(nb_):
        m = {"xT": xTs[b]}
        for n in names:
            if n != "xT":
                m[n] = com[n]
        in_maps.append(m)
    res = run_bass_kernel_spmd(nc, in_maps, core_ids=list(range(nb_)))
    out = np.stack([np.ascontiguousarray(res.results[b]["outT"].T) for b in range(nb_)], axis=0)
    return out.astype(np.float32)
```

```python
import contextlib
import numpy as np
import concourse.bass as bass
import concourse.mybir as mybir
from concourse.bass_utils import run_bass_kernel_spmd

F32 = mybir.dt.float32
BF16 = mybir.dt.bfloat16
ALU = mybir.AluOpType
AF = mybir.ActivationFunctionType

D = 1024
SEQ = 4096
DEPTH = 4
DFF = 2816
T = 512
TS = 256
NSLOT = 4
EPS = 1e-6
ENGS = ("pe", "act", "dve", "pool", "sp")


class Op:
    __slots__ = ("eng", "fn", "deps", "signal", "sigval", "sem", "is_dma", "group")

    def __init__(self, eng, fn):
        self.eng = eng
        self.fn = fn
        self.deps = []
        self.signal = False
        self.sigval = 0
        self.sem = None
        self.is_dma = False
        self.group = None


class Prog:
    def __init__(self, nc):
        self.nc = nc
        self.ops = {e: [] for e in ENGS}
        self.state = {}
        self.stack = contextlib.ExitStack()
        self.dma_groups = {}
        self.n = 0

    def sb(self, shape, dtype=F32, name=None):
        self.n += 1
        return self.stack.enter_context(self.nc.sbuf_tensor(name or f"sb{self.n}", list(shape), dtype))

    def ps(self, shape, dtype=F32, name=None):
        self.n += 1
        return self.stack.enter_context(self.nc.psum_tensor(name or f"ps{self.n}", list(shape), dtype))

    @staticmethod
    def _norm(k):
        if isinstance(k, tuple) and len(k) == 3 and isinstance(k[0], str) and k[0].startswith("@"):
            return k[0], int(k[1]), int(k[2])
        return k, 0, 1

    def _segs(self, ns, lo, hi):
        L = self.state.setdefault(ns, [])
        out = []
        newL = []
        cur = lo
        for sg in L:
            a, b, w, r = sg
            if b <= lo or a >= hi:
                newL.append(sg)
                continue
            if a < lo:
                newL.append([a, lo, w, list(r)])
                a = lo
            if b > hi:
                newL.append([hi, b, w, list(r)])
                b = hi
            mid = [a, b, w, r]
            newL.append(mid)
            out.append(mid)
        out.sort(key=lambda x: x[0])
        filled = []
        for sg in out:
            if sg[0] > cur:
                g = [cur, sg[0], None, []]
                newL.append(g)
                filled.append(g)
            filled.append(sg)
            cur = sg[1]
        if cur < hi:
            g = [cur, hi, None, []]
            newL.append(g)
            filled.append(g)
        self.state[ns] = newL
        return filled

    def _track(self, o, reads, writes, skip_same_eng=False):
        deps = []
        rn = [self._norm(k) for k in reads]
        wn = [self._norm(k) for k in writes]
        for ns, lo, hi in rn:
            for sg in self._segs(ns, lo, hi):
                if sg[2] is not None:
                    deps.append(sg[2])
        for ns, lo, hi in wn:
            for sg in self._segs(ns, lo, hi):
                if sg[2] is not None:
                    deps.append(sg[2])
                last = {}
                for r in sg[3]:
                    if r.is_dma:
                        deps.append(r)
                    else:
                        last[r.eng] = r
                deps.extend(last.values())
        for ns, lo, hi in rn:
            for sg in self._segs(ns, lo, hi):
                sg[3].append(o)
        for ns, lo, hi in wn:
            segs = self._segs(ns, lo, hi)
            L = self.state[ns]
            for sg in segs:
                L.remove(sg)
            L.append([lo, hi, o, []])
        seen = set()
        for d in deps:
            if d is o or id(d) in seen:
                continue
            if skip_same_eng and (not d.is_dma) and d.eng == o.eng:
                continue
            seen.add(id(d))
            o.deps.append(d)

    def op(self, eng, fn, reads=(), writes=()):
        o = Op(eng, fn)
        self._track(o, reads, writes, skip_same_eng=(eng == "pe"))
        self.ops[eng].append(o)
        return o

    def dma(self, eng, out, in_, group, reads=(), writes=(), **kw):
        o = Op(eng, lambda e: e.dma_start(out=out, in_=in_, **kw))
        o.is_dma = True
        o.group = group
        self._track(o, reads, writes)
        lst = self.dma_groups.setdefault(group, [])
        if lst and not group.startswith("all:") and lst[-1] not in o.deps:
            o.deps.append(lst[-1])
        self.ops[eng].append(o)
        lst.append(o)
        return o

    def wait(self, eng, reads):
        o = Op(eng, lambda e: None)
        self._track(o, reads, ())
        self.ops[eng].append(o)
        return o

    def emit(self):
        nc = self.nc
        for e in ENGS:
            for o in self.ops[e]:
                for d in o.deps:
                    d.signal = True
        sems = {}
        for e in ENGS:
            sems[e] = self.stack.enter_context(nc.semaphore(f"s_{e}"))
            c = 0
            for o in self.ops[e]:
                if o.is_dma:
                    continue
                if o.signal:
                    c += 1
                    o.sigval = c
                    o.sem = sems[e]
        for g, lst in self.dma_groups.items():
            s = self.stack.enter_context(nc.semaphore(f"d_{len(sems)}"))
            sems["dma:" + g] = s
            if g.startswith("all:"):
                for o in lst:
                    o.sem = s
                    o.sigval = 16 * len(lst)
            else:
                for i, o in enumerate(lst):
                    o.sem = s
                    o.sigval = 16 * (i + 1)
        engmap = {"pe": "tensor", "act": "scalar", "dve": "vector", "pool": "gpsimd", "sp": "sync"}
        with nc.Block() as block:
            for e in ENGS:
                ops = self.ops[e]
                if not ops:
                    continue

                def body(eng, ops=ops):
                    waited = {}
                    for o in ops:
                        need = {}
                        for d in o.deps:
                            k = id(d.sem)
                            if d.sigval > need.get(k, (None, 0))[1]:
                                need[k] = (d.sem, d.sigval)
                        for k, (s, v) in need.items():
                            if waited.get(k, 0) >= v:
                                continue
                            eng.wait_ge(s, v)
                            waited[k] = v
                        inst = o.fn(eng)
                        if inst is None:
                            continue
                        if o.is_dma:
                            inst.then_inc(o.sem, 16)
                        elif o.signal:
                            inst.then_inc(o.sem, 1)

                getattr(block, engmap[e])(body)

    def close(self):
        self.stack.close()


def _pair(a):
    a = np.asarray(a, dtype=np.float32)
    rest = a.shape[2:]
    a = a.reshape((16, 2, 64) + rest)
    perm = (1, 2, 0) + tuple(range(3, 3 + len(rest)))
    return np.ascontiguousarray(a.transpose(perm).reshape((128, 16) + rest))


def _cols(v, n):
    return np.ascontiguousarray(np.asarray(v, dtype=np.float32).reshape(n, 128).T)


def prep_common(inp):
    f = lambda a: np.ascontiguousarray(np.asarray(a, dtype=np.float32))
    com = {}
    com["gmix"] = f(np.asarray(inp["norm_mix_g"]).reshape(4, 8, 128).transpose(2, 0, 1))
    com["gffn"] = f(np.asarray(inp["norm_ffn_g"]).reshape(4, 8, 128).transpose(2, 0, 1))
    com["gfin"] = _cols(inp["norm_final_g"], 8)
    for j in range(2):
        com[f"ewin{j}"] = f(inp["even_w_in"][j])
        com[f"econv{j}"] = f(np.asarray(inp["even_conv_w"][j]).reshape(3, 4, 128).transpose(2, 1, 0))
        ls = np.broadcast_to(np.asarray(inp["ssm_log_step"][j])[:, None], (32, 64))
        com[f"sls{j}"] = _pair(ls)
        com[f"sare{j}"] = _pair(inp["ssm_a_re"][j])
        com[f"saim{j}"] = _pair(inp["ssm_a_im"][j])
        com[f"sbre{j}"] = _pair(inp["ssm_b_re"][j])
        com[f"sbim{j}"] = _pair(inp["ssm_b_im"][j])
        com[f"scre{j}"] = _pair(np.asarray(inp["ssm_c_re"][j]).transpose(0, 2, 1))
        com[f"scim{j}"] = _pair(np.asarray(inp["ssm_c_im"][j]).transpose(0, 2, 1))
        com[f"sd{j}"] = _cols(inp["ssm_d"][j], 4)
        com[f"glw{j}"] = f(inp["ssm_glu_w"][j])
        com[f"glb{j}"] = _cols(inp["ssm_glu_b"][j], 4)
        com[f"ewout{j}"] = f(inp["even_w_out"][j])
        com[f"owin{j}"] = f(inp["odd_w_in"][j])
        com[f"poolw{j}"] = f(np.asarray(inp["pool_w"][j]).transpose(1, 0, 2))
        com[f"pools{j}"] = _cols(inp["pool_scale"][j], 4)
        com[f"sgng{j}"] = f(np.broadcast_to(np.asarray(inp["sgu_norm_g"][j])[None, :], (128, 512)))
        com[f"sgw{j}"] = f(np.asarray(inp["sgu_w"][j]).transpose(2, 0, 1))
        com[f"sgb{j}"] = f(np.broadcast_to(np.asarray(inp["sgu_b"][j])[None], (128, 4, 128)))
        com[f"owout{j}"] = f(inp["odd_w_out"][j])
    for i in range(4):
        com[f"fup{i}"] = f(inp["ffn_w_up"][i])
        com[f"fcw{i}"] = f(np.asarray(inp["ffn_conv_w"][i]).reshape(3, 44, 128).transpose(2, 1, 0))
        com[f"fcb{i}"] = _cols(inp["ffn_conv_b"][i], 44)
        com[f"fdn{i}"] = f(inp["ffn_w_down"][i])
    s = np.arange(128)
    com["trilh"] = f(0.5 * (s[:, None] <= s[None, :]))
    invc = np.zeros((128, 4, 16), np.float32)
    for g, w in enumerate((2, 4, 8, 16)):
        invc[:, g, :] = 1.0 / np.minimum(np.arange(1, 17), w)
    com["invc"] = invc
    com["ident"] = f(np.eye(128))
    return com


SHAPES = {
    "gmix": [128, 4, 8], "gffn": [128, 4, 8], "gfin": [128, 8],
    "trilh": [128, 128], "invc": [128, 4, 16], "ident": [128, 128],
}
for _j in range(2):
    SHAPES.update({
        f"ewin{_j}": [1024, 2048], f"econv{_j}": [128, 4, 3], f"sls{_j}": [128, 16],
        f"sare{_j}": [128, 16], f"saim{_j}": [128, 16], f"sbre{_j}": [128, 16, 16],
        f"sbim{_j}": [128, 16, 16], f"scre{_j}": [128, 16, 16], f"scim{_j}": [128, 16, 16],
        f"sd{_j}": [128, 4], f"glw{_j}": [512, 512], f"glb{_j}": [128, 4], f"ewout{_j}": [1024, 1024],
        f"owin{_j}": [1024, 1536], f"poolw{_j}": [128, 4, 128], f"pools{_j}": [128, 4],
        f"sgng{_j}": [128, 512], f"sgw{_j}": [128, 4, 128], f"sgb{_j}": [128, 4, 128],
        f"owout{_j}": [1024, 1024],
    })
for _i in range(4):
    SHAPES.update({f"fup{_i}": [1024, 5632], f"fcw{_i}": [128, 44, 3], f"fcb{_i}": [128, 44],
                   f"fdn{_i}": [2816, 1024]})


class V:
    __slots__ = ("ap", "reg")

    def __init__(self, ap, reg):
        self.ap = ap
        self.reg = reg


def build(layers=(0, 1, 2, 3), ntiles=8, final=True, dbg=()):
    nc = bass.Bass("TRN2", target_bir_lowering=False)
    P = Prog(nc)
    dr = {}
    dbg_out = {}

    def din(name):
        if name not in dr:
            dr[name] = nc.dram_tensor(name, SHAPES[name], F32, kind="ExternalInput").ap()
        return dr[name]

    xT = nc.dram_tensor("xT", [D, SEQ], F32, kind="ExternalInput").ap()
    outT = nc.dram_tensor("outT", [D, SEQ], F32, kind="ExternalOutput").ap()
    xTv = xT.rearrange("(kt p) t -> p kt t", p=128)
    outTv = outT.rearrange("(kt p) t -> p kt t", p=128)
    nlay = len(layers)
    scr = nc.dram_tensor("scr", [nlay * 26, 128, 4096], BF16, kind="Internal").ap()
    tabd = nc.dram_tensor("tabd", [2, 128, 2 * 16 * TS], F32, kind="Internal").ap()

    xres = P.sb([128, 8, T], F32, "xres")
    hb = P.sb([128, 8, T], BF16, "hb")
    ring = P.sb([128, NSLOT, 4096], BF16, "ring")
    stage = P.sb([128, 2, 2048], F32, "stage")
    tab = P.sb([128, 2, 16, TS], F32, "tab")
    AW = 15616
    arena = P.sb([128, AW], F32, "arena")
    ones = P.sb([128, 128], BF16, "ones")
    onesf = P.sb([128, 128], F32, "onesf")
    ident = P.sb([128, 128], F32, "ident_sb")
    cst = P.sb([128, 4], F32, "cst")
    gmix = P.sb([128, 4, 8], F32, "gmix_sb")
    gffn = P.sb([128, 4, 8], F32, "gffn_sb")
    gfin = P.sb([128, 8], F32, "gfin_sb")
    fcw = P.sb([128, 4, 44, 3], F32, "fcw_sb")
    fcb = P.sb([128, 4, 44], F32, "fcb_sb")
    fhist = P.sb([128, 4, 44, 2], F32, "fhist")
    cxhist = P.sb([128, 2, 4, 2], F32, "cxhist")
    zhist = P.sb([128, 2, 4, 16], F32, "zhist")
    qinit = P.sb([128, 2, 2, 16], F32, "qinit")
    qend = P.sb([128, 2, 16], F32, "qend")
    econv = P.sb([128, 2, 4, 3], F32, "econv_sb")
    Bl = P.sb([128, 2, 2, 4, 2, 128], BF16, "Bl")
    Cl = P.sb([128, 2, 16, 6, 32], BF16, "Cl")
    K0l = P.sb([128, 2, 16, 32], BF16, "K0l")
    r2 = P.sb([128, 2, 16], F32, "r2")
    maskq = P.sb([128, 4], F32, "maskq")
    rdec = P.sb([128, 2, 16], F32, "rdec")
    rotc = P.sb([128, 2, 2, 16], F32, "rotc")
    dq = P.sb([128, 2, 4], F32, "dq")
    gbh = P.sb([128, 2, 4], F32, "gbh")
    pools = P.sb([128, 2, 4], F32, "pools_sb")
    poolw = P.sb([128, 2, 4, 128], BF16, "poolw_sb")
    wT = P.sb([128, 2, 4, 128], BF16, "wT")
    bh = P.sb([128, 2, 4, 128], F32, "bh")
    ghbc = P.sb([128, 2, 512], F32, "ghbc")
    invc = P.sb([128, 4, 16], F32, "invc_sb")
    banks = [P.ps([128, 512], F32, f"bank{i}") for i in range(8)]

    def bankv(i, lo=0, hi=512):
        return V(banks[i][:, lo:hi], ("bank", i))

    class Arena:
        def __init__(self):
            self.p = 0

        def reset(self, p=0):
            self.p = p

        def take(self, words, dtype=F32, shape=None):
            lo = self.p
            self.p += (words + 7) // 8 * 8
            assert self.p <= AW, (self.p, AW)
            ap = arena[:, lo:lo + words]
            if dtype == BF16:
                ap = ap.bitcast(BF16)
            return V(ap, ("@ar", lo, lo + words))

    AR = Arena()

    def R(*vs):
        out = []
        for v in vs:
            if v is None:
                continue
            out.append(v.reg if isinstance(v, V) else v)
        return out

    def act(out, in_, func, reads, writes, bias=None, scale=None, accum=None):
        kw = {}
        if bias is not None:
            kw["bias"] = bias
        if scale is not None:
            kw["scale"] = scale
        if accum is not None:
            kw["accum_out"] = accum
        return P.op("act", lambda e: e.activation(out, in_, func, **kw), reads, writes)

    def tt(eng, out, a, b, op, reads, writes):
        return P.op(eng, lambda e: e.tensor_tensor(out, a, b, op), reads, writes)

    def ts(eng, out, a, s1, s2, op0, op1, reads, writes):
        if op1 is None:
            return P.op(eng, lambda e: e.tensor_scalar(out, a, s1, None, op0), reads, writes)
        return P.op(eng, lambda e: e.tensor_scalar(out, a, s1, s2, op0, op1), reads, writes)

    def stt(out, a, s, b, op0, op1, reads, writes):
        return P.op("dve", lambda e: e.scalar_tensor_tensor(out, a, s, b, op0, op1), reads, writes)

    def cp(eng, out, in_, reads, writes):
        if eng == "act":
            return P.op("act", lambda e: e.activation(out, in_, AF.Copy), reads, writes)
        return P.op(eng, lambda e: e.tensor_copy(out, in_), reads, writes)

    def mm(out, lhsT, rhs, start, stop, reads, writes, tp=None):
        if tp is None:
            return P.op("pe", lambda e: e.matmul(out, lhsT, rhs, start=start, stop=stop), reads, writes)
        return P.op("pe", lambda e: e.matmul(out, lhsT, rhs, start=start, stop=stop, tile_position=tp),
                    reads, writes)

    def dump(name, ap, shape, reads):
        if name not in dbg:
            return
        t = nc.dram_tensor("dbg_" + name, list(shape), ap.dtype, kind="ExternalOutput").ap()
        dbg_out[name] = t
        P.dma("act", t, ap, "all:dbg", reads=reads, writes=[("dbgout", name)])

    smc = {"n": 0}

    def small_dma(dst_ap, src_ap, writes):
        g = "sm%d" % (smc["n"] % 4)
        smc["n"] += 1
        P.dma("sp", dst_ap, src_ap, g, writes=writes)

    def load_small(dst_ap, name, key):
        small_dma(dst_ap, din(name), [key])

    P.op("dve", lambda e: e.memset(ones[:], 1.0), writes=["ones"])
    P.op("dve", lambda e: e.memset(onesf[:], 1.0), writes=["onesf"])
    P.op("dve", lambda e: e.memset(cst[:, 0:1], -0.5), writes=["cst"])
    P.op("dve", lambda e: e.memset(cst[:, 1:2], EPS), reads=["cst"], writes=["cst"])
    P.op("dve", lambda e: e.memset(cst[:, 2:3], 4.0), reads=["cst"], writes=["cst"])
    P.op("dve", lambda e: e.memset(cst[:, 3:4], 1.0), reads=["cst"], writes=["cst"])
    P.op("dve", lambda e: e.memset(fhist[:], 0.0), writes=["fhist_all"])
    P.op("dve", lambda e: e.memset(cxhist[:], 0.0), writes=["cxhist_all"])
    P.op("dve", lambda e: e.memset(zhist[:], 0.0), writes=["zhist_all"])
    P.op("dve", lambda e: e.memset(qinit[:], 0.0), writes=["qinit_all"])
    load_small(gmix[:], "gmix", "gmix")
    load_small(gffn[:], "gffn", "gffn")
    load_small(gfin[:], "gfin", "gfin")
    load_small(invc[:], "invc", "invc")
    for i in layers:
        load_small(fcw[:, i], f"fcw{i}", ("fcw", i))
        load_small(fcb[:, i], f"fcb{i}", ("fcb", i))

    def wview(name, K):
        return din(name).rearrange("(kt p) n -> p kt n", p=128)

    def slabs_for(i):
        j = i // 2
        L = []
        if i % 2 == 0:
            w = wview(f"ewin{j}", 1024)
            L.append((8, 512, [(w, 0, 0, 256), (w, 1024, 256, 256)]))
            L.append((8, 512, [(w, 256, 0, 256), (w, 1280, 256, 256)]))
            L.append((8, 512, [(w, 1536, 0, 512)]))
            L.append((8, 512, [(w, 512, 0, 512)]))
            L.append((4, 512, [(wview(f"glw{j}", 512), 0, 0, 512)]))
            wo = wview(f"ewout{j}", 1024)
        else:
            w = wview(f"owin{j}", 1024)
            L.append((8, 512, [(w, 0, 0, 512)]))
            L.append((8, 512, [(w, 512, 0, 512)]))
            L.append((8, 512, [(w, 1024, 0, 512)]))
            wo = wview(f"owout{j}", 1024)
        L.append((8, 512, [(wo, 0, 0, 512)]))
        L.append((8, 512, [(wo, 512, 0, 512)]))
        wu = wview(f"fup{i}", 1024)
        for k in range(11):
            L.append((8, 512, [(wu, 256 * k, 0, 256), (wu, 2816 + 256 * k, 256, 256)]))
        wd = wview(f"fdn{i}", 2816)
        for m in range(8):
            L.append((22, 128, [(wd, 128 * m, 0, 128)]))
        return L

    lay_slabs = {i: slabs_for(i) for i in layers}
    seq = []
    for t in range(ntiles):
        for li, i in enumerate(layers):
            for s in range(len(lay_slabs[i])):
                seq.append((t, li, i, s))
    stream = {"next": 0, "cur": 0, "cast": 0, "stg": 0}

    def slot_reg(slot):
        return ("@ring%d" % slot, 0, 4096)

    def make_load(n):
        t, li, i, s = seq[n]
        KT, W, pieces = lay_slabs[i][s]
        slot = n % NSLOT
        sid = li * 26 + s
        nel = KT * W
        if t == 0:
            h0 = (KT + 1) // 2
            for (k0, k1) in ((0, h0), (h0, KT)):
                if k1 <= k0:
                    continue
                sg = stream["stg"] % 2
                stream["stg"] += 1
                nk = k1 - k0
                sview = stage[:, sg, 0:nk * W].rearrange("p (k w) -> p k w", k=nk)
                for pi, (w, c0, d0, wd_) in enumerate(pieces):
                    P.dma("sp", sview[:, :, d0:d0 + wd_], w[:, k0:k1, c0:c0 + wd_], "stg%d" % sg,
                          writes=[("stage", sg, pi)])
                eng = "act"
                stream["cast"] += 1
                dst = ring[:, slot, k0 * W:k1 * W]
                src = stage[:, sg, 0:nk * W]
                cp(eng, dst, src, [("stage", sg, pi) for pi in range(len(pieces))],
                   [("@ring%d" % slot, k0 * W, k1 * W)])
            P.dma("act", scr[sid][:, 0:nel], ring[:, slot, 0:nel], "scrst%d" % slot,
                  reads=[("@ring%d" % slot, 0, nel)], writes=[("scr", sid)])
        else:
            P.dma("sp", ring[:, slot, 0:nel], scr[sid][:, 0:nel], "ring%d" % slot,
                  reads=[("scr", sid)], writes=[("@ring%d" % slot, 0, nel)])

    def next_slab():
        n = stream["cur"]
        stream["cur"] += 1
        while stream["next"] < min(len(seq), n + NSLOT):
            make_load(stream["next"])
            stream["next"] += 1
        t, li, i, s = seq[n]
        KT, W, _ = lay_slabs[i][s]
        slot = n % NSLOT
        view = ring[:, slot, 0:KT * W].rearrange("p (k w) -> p k w", k=KT)
        return V(view, slot_reg(slot))

    rot = {"main": 0, "bu": 0, "yb": 0}
    pools_ = {"main4": [0, 1, 2, 3], "main8": [0, 1, 2, 3, 4, 5, 6, 7], "bu": [4, 5, 0, 1], "yb": [6, 7]}

    def nb(pool):
        key = "main" if pool.startswith("main") else pool
        lst = pools_[pool]
        b = lst[rot[key] % len(lst)]
        rot[key] += 1
        return b

    load_small(ident[:], "ident", "ident")
    even_js = [i // 2 for i in layers if i % 2 == 0]
    odd_js = [i // 2 for i in layers if i % 2 == 1]

    def ssm_setup(j, jj):
        AR.reset()
        sm = lambda: AR.take(16)
        ls, are, aim = sm(), sm(), sm()
        for v, nm in ((ls, "sls"), (are, "sare"), (aim, "saim")):
            small_dma(v.ap, din(f"{nm}{j}"), R(v))
        big = {}
        for nm in ("sbre", "sbim", "scre", "scim"):
            big[nm] = AR.take(256)
            small_dma(big[nm].ap.rearrange("p (k h) -> p k h", k=16), din(f"{nm}{j}"), R(big[nm]))
        small_dma(econv[:, j], din(f"econv{j}"), [("econv", j)])
        sdl, glbl = AR.take(4), AR.take(4)
        small_dma(sdl.ap, din(f"sd{j}"), R(sdl))
        small_dma(glbl.ap, din(f"glb{j}"), R(glbl))
        ts("dve", dq[:, j, :], sdl.ap, 0.25, None, ALU.mult, None, R(sdl), [("dq", j)])
        ts("dve", gbh[:, j, :], glbl.ap, 0.5, None, ALU.mult, None, R(glbl), [("gbh", j)])

        dt_, xr, th = sm(), sm(), sm()
        act(dt_.ap, ls.ap, AF.Exp, R(ls), R(dt_))
        tt("dve", xr.ap, are.ap, dt_.ap, ALU.mult, R(are, dt_), R(xr))
        tt("dve", th.ap, aim.ap, dt_.ap, ALU.mult, R(aim, dt_), R(th))
        rv = V(rdec[:, j, :], ("rdec", j))
        act(rv.ap, xr.ap, AF.Exp, R(xr), R(rv))
        al, a2, ps_, pc_ = sm(), sm(), sm(), sm()
        ts("dve", al.ap, th.ap, 1.0 / 64, None, ALU.mult, None, R(th), R(al))
        tt("dve", a2.ap, al.ap, al.ap, ALU.mult, R(al), R(a2))

        def horner(p, coefs):
            ts("dve", p.ap, a2.ap, coefs[0], coefs[1], ALU.mult, ALU.add, R(a2), R(p))
            for c in coefs[2:]:
                tt("dve", p.ap, p.ap, a2.ap, ALU.mult, R(p, a2), R(p))
                ts("dve", p.ap, p.ap, c, None, ALU.add, None, R(p), R(p))

        horner(ps_, [1.0 / 362880, -1.0 / 5040, 1.0 / 120, -1.0 / 6, 1.0])
        tt("dve", ps_.ap, ps_.ap, al.ap, ALU.mult, R(ps_, al), R(ps_))
        horner(pc_, [-1.0 / 3628800, 1.0 / 40320, -1.0 / 720, 1.0 / 24, -0.5, 1.0])
        t1, t2 = sm(), sm()
        for _ in range(6):
            tt("dve", t1.ap, pc_.ap, pc_.ap, ALU.mult, R(pc_), R(t1))
            tt("dve", t2.ap, ps_.ap, ps_.ap, ALU.mult, R(ps_), R(t2))
            tt("dve", ps_.ap, ps_.ap, pc_.ap, ALU.mult, R(ps_, pc_), R(ps_))
            ts("dve", ps_.ap, ps_.ap, 2.0, None, ALU.mult, None, R(ps_), R(ps_))
            tt("dve", pc_.ap, t1.ap, t2.ap, ALU.subtract, R(t1, t2), R(pc_))
        c1, s1 = pc_, ps_
        nre, nim, den, fre, fim = sm(), sm(), sm(), sm(), sm()
        tt("dve", nre.ap, rv.ap, c1.ap, ALU.mult, R(rv, c1), R(nre))
        ts("dve", nre.ap, nre.ap, -1.0, None, ALU.add, None, R(nre), R(nre))
        tt("dve", nim.ap, rv.ap, s1.ap, ALU.mult, R(rv, s1), R(nim))
        tt("dve", den.ap, are.ap, are.ap, ALU.mult, R(are), R(den))
        tt("dve", t1.ap, aim.ap, aim.ap, ALU.mult, R(aim), R(t1))
        tt("dve", den.ap, den.ap, t1.ap, ALU.add, R(den, t1), R(den))
        P.op("dve", lambda e: e.reciprocal(den.ap, den.ap), R(den), R(den))
        tt("dve", fre.ap, nre.ap, are.ap, ALU.mult, R(nre, are), R(fre))
        tt("dve", t1.ap, nim.ap, aim.ap, ALU.mult, R(nim, aim), R(t1))
        tt("dve", fre.ap, fre.ap, t1.ap, ALU.add, R(fre, t1), R(fre))
        tt("dve", fre.ap, fre.ap, den.ap, ALU.mult, R(fre, den), R(fre))
        tt("dve", fim.ap, nim.ap, are.ap, ALU.mult, R(nim, are), R(fim))
        tt("dve", t1.ap, nre.ap, aim.ap, ALU.mult, R(nre, aim), R(t1))
        tt("dve", fim.ap, fim.ap, t1.ap, ALU.subtract, R(fim, t1), R(fim))
        tt("dve", fim.ap, fim.ap, den.ap, ALU.mult, R(fim, den), R(fim))
        v3 = lambda v: v.ap.rearrange("p (k h) -> p k h", k=16)
        bc = lambda v: v.ap.unsqueeze(2).to_broadcast([128, 16, 16])
        Bre, Bim, tA = AR.take(256), AR.take(256), AR.take(256)
        tt("dve", v3(Bre), v3(big["sbre"]), bc(fre), ALU.mult, R(big["sbre"], fre), R(Bre))
        tt("dve", v3(tA), v3(big["sbim"]), bc(fim), ALU.mult, R(big["sbim"], fim), R(tA))
        tt("dve", v3(Bre), v3(Bre), v3(tA), ALU.subtract, R(Bre, tA), R(Bre))
        tt("dve", v3(Bim), v3(big["sbim"]), bc(fre), ALU.mult, R(big["sbim"], fre), R(Bim))
        tt("dve", v3(tA), v3(big["sbre"]), bc(fim), ALU.mult, R(big["sbre"], fim), R(tA))
        tt("dve", v3(Bim), v3(Bim), v3(tA), ALU.add, R(Bim, tA), R(Bim))
        lr, li = sm(), nim
        tt("dve", lr.ap, rv.ap, c1.ap, ALU.mult, R(rv, c1), R(lr))
        tt("dve", r2[:, j, :], rv.ap, rv.ap, ALU.mult, R(rv), [("r2", j)])
        LBre, LBim = AR.take(256), AR.take(256)
        tt("dve", v3(LBre), v3(Bre), bc(lr), ALU.mult, R(Bre, lr), R(LBre))
        tt("dve", v3(tA), v3(Bim), bc(li), ALU.mult, R(Bim, li), R(tA))
        tt("dve", v3(LBre), v3(LBre), v3(tA), ALU.subtract, R(LBre, tA), R(LBre))
        tt("dve", v3(LBim), v3(Bim), bc(lr), ALU.mult, R(Bim, lr), R(LBim))
        tt("dve", v3(tA), v3(Bre), bc(li), ALU.mult, R(Bre, li), R(tA))
        tt("dve", v3(LBim), v3(LBim), v3(tA), ALU.add, R(LBim, tA), R(LBim))
        Mb = {}
        for st_, comp, Bv in ((0, 0, Bre), (0, 1, Bim), (1, 0, LBre), (1, 1, LBim)):
            M = AR.take(512)
            P.op("dve", lambda e, M=M: e.memset(M.ap, 0.0), (), R(M))
            M4 = M.ap.rearrange("p (k g h) -> p k g h", k=16, g=2)
            B3 = v3(Bv)
            cp("dve", M4[0:64, :, 0, :], B3[0:64], R(Bv, M), R(M))
            cp("dve", M4[64:128, :, 1, :], B3[64:128], R(Bv, M), R(M))
            if st_ == 0:
                Mb[comp] = AR.take(256, BF16)
                cp("dve", Mb[comp].ap, M.ap, R(M), R(Mb[comp]))
            for b in range(4):
                bk = nb("main8")
                P.op("pe", lambda e, bk=bk, M=M, b=b: e.transpose(banks[bk][:, 0:128],
                                                                  M.ap[:, b * 128:(b + 1) * 128], ident[:]),
                     R(M) + ["ident"], [("bank", bk)])
                cp("act", Bl[:, j, st_, b, comp, :], banks[bk][:, 0:128], [("bank", bk)], [("Bl", j)])
        CLre, CLim = AR.take(256), AR.take(256)
        tt("dve", v3(CLre), v3(big["scre"]), bc(lr), ALU.mult, R(big["scre"], lr), R(CLre))
        tt("dve", v3(tA), v3(big["scim"]), bc(li), ALU.mult, R(big["scim"], li), R(tA))
        tt("dve", v3(CLre), v3(CLre), v3(tA), ALU.subtract, R(CLre, tA), R(CLre))
        tt("dve", v3(CLim), v3(big["scre"]), bc(li), ALU.mult, R(big["scre"], li), R(CLim))
        tt("dve", v3(tA), v3(big["scim"]), bc(lr), ALU.mult, R(big["scim"], lr), R(tA))
        tt("dve", v3(CLim), v3(CLim), v3(tA), ALU.add, R(CLim, tA), R(CLim))
        P.op("dve", lambda e: e.memset(Cl[:, j], 0.0), (), [("Cl", j)])
        csrc = ((big["scre"], 0.25), (big["scre"], -0.25), (big["scim"], -0.25),
                (CLre, 0.25), (CLre, -0.25), (CLim, -0.25))
        for m, (sv_, sc) in enumerate(csrc):
            src = v3(sv_)
            for g2 in range(2):
                ts("dve", Cl[64 * g2:64 * g2 + 64, j, :, m, 16 * g2:16 * g2 + 16], src[64 * g2:64 * g2 + 64],
                   sc, None, ALU.mult, None, R(sv_) + [("Cl", j)], [("Cl", j)])
        for q in range(4):
            P.op("dve", lambda e, q=q: e.reduce_sum(maskq[:, q:q + 1], ident[:, 32 * q:32 * q + 32],
                                                    axis=mybir.AxisListType.X), ["ident"], [("maskq", q)])
        bkK = nb("main8")
        for k in range(16):
            b, q = divmod(k, 4)
            ko = banks[bkK][32 * q:32 * q + 32, 32 * b:32 * b + 32]
            mm(ko, Mb[0].ap[:, 32 * k:32 * k + 32], Cl[:, j, k, 0, :], True, False,
               R(Mb[0]) + [("Cl", j)], [("bank", bkK)], tp=(0, 32 * q))
            mm(ko, Mb[1].ap[:, 32 * k:32 * k + 32], Cl[:, j, k, 2, :], False, True,
               R(Mb[1]) + [("Cl", j)], [("bank", bkK)], tp=(0, 32 * q))
        for k in range(16):
            b, q = divmod(k, 4)
            ts("dve", K0l[:, j, k, :], banks[bkK][:, 32 * b:32 * b + 32], maskq[:, q:q + 1], None, ALU.mult, None,
               [("bank", bkK), ("maskq", q)], [("K0l", j)])
        tabv = ("tab",)
        cp("dve", tab[:, 0, :, 0:1], c1.ap.unsqueeze(2), R(c1) + ["tab"], ["tab"])
        cp("dve", tab[:, 1, :, 0:1], s1.ap.unsqueeze(2), R(s1) + ["tab"], ["tab"])
        w1, w2 = AR.take(2048), AR.take(2048)
        n = 1
        while n < TS:
            cb = tab[:, 0, :, n - 1:n].to_broadcast([128, 16, n])
            sb_ = tab[:, 1, :, n - 1:n].to_broadcast([128, 16, n])
            a1 = w1.ap[:, 0:16 * n].rearrange("p (k t) -> p k t", k=16)
            a2_ = w2.ap[:, 0:16 * n].rearrange("p (k t) -> p k t", k=16)
            tt("dve", a1, tab[:, 0, :, 0:n], cb, ALU.mult, ["tab"], R(w1))
            tt("dve", a2_, tab[:, 1, :, 0:n], sb_, ALU.mult, ["tab"], R(w2))
            tt("dve", tab[:, 0, :, n:2 * n], a1, a2_, ALU.subtract, R(w1, w2) + ["tab"], ["tab"])
            tt("dve", a1, tab[:, 1, :, 0:n], cb, ALU.mult, ["tab"], R(w1))
            tt("dve", a2_, tab[:, 0, :, 0:n], sb_, ALU.mult, ["tab"], R(w2))
            tt("dve", tab[:, 1, :, n:2 * n], a1, a2_, ALU.add, R(w1, w2) + ["tab"], ["tab"])
            n *= 2
        cp("dve", rotc[:, j, 0, :].unsqueeze(2), tab[:, 0, :, TS - 1:TS], ["tab"], [("rotc", j)])
        cp("dve", rotc[:, j, 1, :].unsqueeze(2), tab[:, 1, :, TS - 1:TS], ["tab"], [("rotc", j)])
        P.dma("act", tabd[jj], tab[:].rearrange("p c k t -> p (c k t)"), "tabst", reads=["tab"],
              writes=[("tabd", jj)])

    def odd_setup(j):
        AR.reset()
        small_dma(pools[:, j, :], din(f"pools{j}"), [("pools", j)])
        pw = AR.take(512)
        small_dma(pw.ap.rearrange("p (g d) -> p g d", g=4), din(f"poolw{j}"), R(pw))
        cp("dve", poolw[:, j].rearrange("p g d -> p (g d)"), pw.ap, R(pw), [("poolw", j)])
        sw, tr = AR.take(512), AR.take(128)
        small_dma(sw.ap.rearrange("p (g d) -> p g d", g=4), din(f"sgw{j}"), R(sw))
        small_dma(tr.ap, din("trilh"), R(tr))
        tt("dve", wT[:, j], sw.ap.rearrange("p (g d) -> p g d", g=4),
           tr.ap.unsqueeze(1).to_broadcast([128, 4, 128]), ALU.mult, R(sw, tr), [("wT", j)])
        sb2 = AR.take(512)
        small_dma(sb2.ap.rearrange("p (g d) -> p g d", g=4), din(f"sgb{j}"), R(sb2))
        ts("dve", bh[:, j].rearrange("p g d -> p (g d)"), sb2.ap, 0.5, None, ALU.mult, None, R(sb2),
           [("bh", j)])
        gg = AR.take(512)
        small_dma(gg.ap, din(f"sgng{j}"), R(gg))
        ts("dve", ghbc[:, j, :], gg.ap, 0.5, None, ALU.mult, None, R(gg), [("ghbc", j)])

    tab_state = {"j": None}

    dq_ = []

    def later(delay, fn):
        dq_.append([delay, fn])

    def tick():
        due = [x for x in dq_ if x[0] <= 0]
        for x in due:
            dq_.remove(x)
        for x in dq_:
            x[0] -= 1
        for x in due:
            x[1]()

    def flush():
        while dq_:
            tick()

    def sub(v, a, b):
        return V(v.ap[:, a:b], ("@ar", v.reg[1] + a, v.reg[1] + b))

    def xr_(kt):
        return V(xres[:, kt, :], ("xres", kt))

    def hb_(kt):
        return V(hb[:, kt, :], ("hb", kt))

    def rmsnorm(gap, gkey, outs, bf=True):
        AR.reset()
        sq = [AR.take(256, BF16) for _ in range(8)]
        ms4, rs4 = AR.take(4), AR.take(4)
        dg4 = AR.take(512)
        bk = nb("main8")
        for kt in range(8):
            act(sq[kt].ap, xres[:, kt, :], AF.Square, [("xres", kt)], R(sq[kt]))
        for tb_ in range(4):
            for kt in range(8):
                mm(banks[bk][:, tb_:tb_ + 1], sq[kt].ap[:, tb_ * 128:(tb_ + 1) * 128], ones[:, 0:1],
                   kt == 0, kt == 7, ["ones"] + R(sq[kt]), [("bank", bk)])
        ts("dve", ms4.ap, banks[bk][:, 0:4], 1.0 / D, EPS, ALU.mult, ALU.add, [("bank", bk)], R(ms4))
        tt("pool", rs4.ap, ms4.ap, cst[:, 0:1].to_broadcast([128, 4]), ALU.pow, R(ms4) + ["cst"], R(rs4))
        bk2 = nb("main8")
        tt("dve", dg4.ap.rearrange("p (a b) -> p a b", a=4), ident[:].unsqueeze(1).to_broadcast([128, 4, 128]),
           rs4.ap.unsqueeze(2).to_broadcast([128, 4, 128]), ALU.mult, R(rs4) + ["ident"], R(dg4))
        for tb_ in range(4):
            mm(banks[bk2][:, tb_ * 128:(tb_ + 1) * 128], onesf[:], dg4.ap[:, tb_ * 128:(tb_ + 1) * 128], True, True,
               ["onesf"] + R(dg4), [("bank", bk2)])
        for kt in range(8):
            stt(outs[kt].ap, xres[:, kt, :], gap[:, kt:kt + 1], banks[bk2][:, :], ALU.mult, ALU.mult,
                [("xres", kt), gkey, ("bank", bk2)], R(outs[kt]))

    def proj_chunk(Wv, col, pool, rhs_list, KT=8):
        bk = nb(pool)
        for kt in range(KT):
            mm(banks[bk][:, :], Wv.ap[:, kt, col:col + 128], rhs_list[kt].ap, kt == 0, kt == KT - 1,
               [Wv.reg] + R(rhs_list[kt]), [("bank", bk)])
        return bk

    def out_proj(ycat, pool):
        for half in range(2):
            Wo = next_slab()
            for m_ in range(4):
                m = half * 4 + m_
                bk = proj_chunk(Wo, m_ * 128, pool, ycat)
                tt("dve", xres[:, m, :], xres[:, m, :], banks[bk][:, :], ALU.add,
                   [("xres", m), ("bank", bk)], [("xres", m)])

    def gelu2(bk, tb, Gout):
        cp("act", tb["pc"].ap, banks[bk][:, :], [("bank", bk)], R(tb["pc"]))
        act(tb["p2"].ap, banks[bk][:, :], AF.Square, [("bank", bk)], R(tb["p2"]))
        act(tb["ti"].ap, tb["p2"].ap, AF.Identity, R(tb["p2"]) + ["cst"], R(tb["ti"]), bias=cst[:, 3:4], scale=0.044715)
        tt("dve", tb["ti"].ap, tb["ti"].ap, tb["pc"].ap, ALU.mult, R(tb["ti"], tb["pc"]), R(tb["ti"]))
        act(tb["th"].ap, tb["ti"].ap, AF.Tanh, R(tb["ti"]), R(tb["th"]), scale=0.7978845608028654)
        stt(Gout.ap, tb["th"].ap, 1.0, tb["pc"].ap, ALU.add, ALU.mult, R(tb["th"], tb["pc"]), R(Gout))

    def even_mixer(i, t):
        j = i // 2
        hbl = [hb_(kt) for kt in range(8)]
        AR.reset()
        xa = [AR.take(512) for _ in range(4)]
        cxb = [AR.take(514) for _ in range(4)]
        ycat = [AR.take(256, BF16) for _ in range(8)]
        u = [AR.take(512) for _ in range(4)]
        ub = [AR.take(256, BF16) for _ in range(4)]
        HT = TS // 2
        ssmb = [dict(bu=AR.take(2 * HT), m34=AR.take(2 * HT), q=AR.take(2 * HT),
                     X=AR.take(HT + 1, BF16), Y=AR.take(HT + 1, BF16)) for _ in range(3)]
        for S_ in ssmb:
            P.op("dve", lambda e, S_=S_: e.memset(S_["Y"].ap, 0.0), (), R(S_["Y"]))
        tmpb = [dict(yq=AR.take(256), y2=AR.take(256), ti=AR.take(256), th=AR.take(256)) for _ in range(2)]
        rt = [AR.take(16) for _ in range(4)]
        G = xa
        base = cxb[0].reg[1]
        gelu_bf = [V(arena[:, base + 256 * b:base + 256 * (b + 1)].bitcast(BF16),
                     ("@ar", base + 256 * b, base + 256 * (b + 1))) for b in range(4)]
        th2 = [V(arena[:, base + 1024 + 512 * x:base + 1024 + 512 * (x + 1)],
                 ("@ar", base + 1024 + 512 * x, base + 1024 + 512 * (x + 1))) for x in range(2)]
        if tab_state["j"] != j:
            jj = even_js.index(j)
            P.dma("sp", tab[:].rearrange("p c k t -> p (c k t)"), tabd[jj], "tabld",
                  reads=[("tabd", jj)], writes=["tab"])
            tab_state["j"] = j
        for sl in range(2):
            W = next_slab()
            for ii in range(2):
                c = 2 * sl + ii
                bx = proj_chunk(W, ii * 128, "main4", hbl)
                cp("act", xa[c].ap, banks[bx][:, :], [("bank", bx)], R(xa[c]))
                bc_ = proj_chunk(W, 256 + ii * 128, "main4", hbl)
                cxh, cxm = sub(cxb[c], 0, 2), sub(cxb[c], 2, 514)
                cp("pool", cxh.ap, cxhist[:, j, c, :], [("cxhist", j, c)], R(cxh))
                tt("dve", cxm.ap, banks[bc_][:, :], xa[c].ap, ALU.mult, [("bank", bc_)] + R(xa[c]), R(cxm))
                ek = ("econv", j)
                act(xa[c].ap, cxm.ap, AF.Identity, R(cxm) + [ek], R(xa[c]), scale=econv[:, j, c, 2:3])
                stt(xa[c].ap, cxb[c].ap[:, 1:513], econv[:, j, c, 1:2], xa[c].ap, ALU.mult, ALU.add,
                    R(cxb[c], xa[c]) + [ek], R(xa[c]))
                stt(xa[c].ap, cxb[c].ap[:, 0:512], econv[:, j, c, 0:1], xa[c].ap, ALU.mult, ALU.add,
                    R(cxb[c], xa[c]) + [ek], R(xa[c]))
                cp("pool", cxhist[:, j, c, :], cxb[c].ap[:, 512:514], R(cxm), [("cxhist", j, c)])
        W = next_slab()
        for b in range(4):
            bk = proj_chunk(W, b * 128, "main4", hbl)
            cp("act", u[b].ap, banks[bk][:, :], [("bank", bk)], R(u[b]))
            cp("pool", ub[b].ap, u[b].ap, R(u[b]), R(ub[b]))
        W = next_slab()
        for c in range(4):
            bk = proj_chunk(W, c * 128, "main4", hbl)
            tt("dve", ycat[c].ap, banks[bk][:, :], xa[c].ap, ALU.mult, [("bank", bk)] + R(xa[c]), R(ycat[c]))
        items = [(sb_i, b, q) for sb_i in range(T // TS) for b in range(4) for q in range(4)]
        v3 = lambda v: v.ap.rearrange("p (c t) -> p c t", c=2)
        stA = {}

        def stageA(n):
            sb_i, b, q = items[n]
            c0 = sb_i * TS
            bb = nb("bu")
            for comp in range(2):
                o_ = banks[bb][:, comp * HT:(comp + 1) * HT]
                mm(o_, Bl[32 * q:32 * q + 32, j, 1, b, comp, :], ub[b].ap[32 * q:32 * q + 32, c0:c0 + TS:2],
                   True, False, [("Bl", j)] + R(ub[b]), [("bank", bb)], tp=(32 * q, 0))
                mm(o_, Bl[32 * q:32 * q + 32, j, 0, b, comp, :], ub[b].ap[32 * q:32 * q + 32, c0 + 1:c0 + TS:2],
                   False, True, [("Bl", j)] + R(ub[b]), [("bank", bb)], tp=(32 * q, 0))
            stA[n] = bb

        def stageB(n, yk):
            sb_i, b, q = items[n]
            c0 = sb_i * TS
            k = 4 * b + q
            S = ssmb[n % 3]
            v3 = lambda v: v.ap.rearrange("p (c t) -> p c t", c=2)
            cbc = tab[:, 0, k, 1:TS:2].unsqueeze(1).to_broadcast([128, 2, HT])
            sbc = tab[:, 1, k, 1:TS:2].unsqueeze(1).to_broadcast([128, 2, HT])
            bb = stA.pop(n)
            bk3 = banks[bb][:, 0:2 * HT].rearrange("p (c t) -> p c t", c=2)
            tt("dve", v3(S["m34"]), bk3, sbc, ALU.mult, [("bank", bb), "tab"], R(S["m34"]))
            tt("dve", v3(S["bu"]), bk3, cbc, ALU.mult, [("bank", bb), "tab"], R(S["bu"]))
            mre, mim = sub(S["bu"], 0, HT), sub(S["bu"], HT, 2 * HT)
            tt("dve", mre.ap, mre.ap, S["m34"].ap[:, HT:2 * HT], ALU.add, R(mre, S["m34"]), R(mre))
            tt("dve", mim.ap, mim.ap, S["m34"].ap[:, 0:HT], ALU.subtract, R(mim, S["m34"]), R(mim))
            rb = r2[:, j, k:k + 1].to_broadcast([128, HT])
            for comp, mv in ((0, mre), (1, mim)):
                qv = sub(S["q"], comp * HT, (comp + 1) * HT)
                P.op("dve", lambda e, qv=qv, rb=rb, mv=mv, comp=comp, k=k: e.tensor_tensor_scan(
                    qv.ap, rb, mv.ap, qinit[:, j, comp, k:k + 1], ALU.mult, ALU.add),
                    R(mv) + [("r2", j), ("qinit", j)], R(qv))
            X3 = S["X"].ap.rearrange("p (c t) -> p c t", c=2)
            Y3 = S["Y"].ap.rearrange("p (c t) -> p c t", c=2)
            cp("act", X3[:, :, 0:1], qinit[:, j, :, k:k + 1], [("qinit", j)] + R(S["X"]), R(S["X"]))
            tt("dve", X3[:, :, 1:HT + 1], v3(S["q"]), cbc, ALU.mult, R(S["q"], S["X"]) + ["tab"], R(S["X"]))
            tt("dve", Y3[:, :, 1:HT + 1], v3(S["q"]), sbc, ALU.mult, R(S["q"], S["Y"]) + ["tab"], R(S["Y"]))
            cp("act", qend[:, :, k:k + 1], v3(S["q"])[:, :, HT - 1:HT], R(S["q"]), [("qend", k)])
            W1 = HT + 1
            seg = lambda src, half, a_: src.ap[:, half * W1 + a_:half * W1 + a_ + HT]
            yo_odd = banks[yk][32 * q:32 * q + 32, 1:TS:2]
            yo_even = banks[yk][32 * q:32 * q + 32, 0:TS:2]
            ops_ = ((0, S["X"], 0), (1, S["Y"], 1), (2, S["Y"], 0), (2, S["X"], 1))
            for n_, (m_, src, half) in enumerate(ops_):
                mm(yo_odd, Cl[:, j, k, m_, :], seg(src, half, 1), n_ == 0, n_ == 3,
                   [("Cl", j)] + R(src), [("bank", yk)], tp=(0, 32 * q))
            mm(yo_even, K0l[:, j, k, :], ub[b].ap[:, c0:c0 + TS:2], True, False,
               [("K0l", j)] + R(ub[b]), [("bank", yk)], tp=(0, 32 * q))
            for n_, (m_, src, half) in enumerate(ops_):
                mm(yo_even, Cl[:, j, k, 3 + m_, :], seg(src, half, 0), False, n_ == 3,
                   [("Cl", j)] + R(src), [("bank", yk)], tp=(0, 32 * q))

        def block_epilogue(sb_i, b, yk, ib):
            c0 = sb_i * TS
            tb = tmpb[ib % 2]
            Gs = sub(G[b], c0, c0 + TS)
            gb = V(gelu_bf[b].ap[:, c0:c0 + TS], ("@ar", gelu_bf[b].reg[1] + c0 // 2,
                                                 gelu_bf[b].reg[1] + (c0 + TS) // 2))

            def e1():
                stt(tb["yq"].ap, u[b].ap[:, c0:c0 + TS], dq[:, j, b:b + 1], banks[yk][:, 0:TS], ALU.mult, ALU.add,
                    R(u[b]) + [("dq", j), ("bank", yk)], R(tb["yq"]))
                act(tb["y2"].ap, tb["yq"].ap, AF.Square, R(tb["yq"]), R(tb["y2"]), scale=4.0)

            def e2():
                act(tb["ti"].ap, tb["y2"].ap, AF.Identity, R(tb["y2"]) + ["cst"], R(tb["ti"]), bias=cst[:, 2:3],
                    scale=4 * 0.044715)
                tt("dve", tb["ti"].ap, tb["ti"].ap, tb["yq"].ap, ALU.mult, R(tb["ti"], tb["yq"]), R(tb["ti"]))

            def e3():
                act(tb["th"].ap, tb["ti"].ap, AF.Tanh, R(tb["ti"]), R(tb["th"]), scale=0.7978845608028654)

            def e4():
                stt(Gs.ap, tb["th"].ap, 1.0, tb["yq"].ap, ALU.add, ALU.mult, R(tb["th"], tb["yq"]), R(Gs))
                act(gb.ap, Gs.ap, AF.Copy, R(Gs), R(gb), scale=2.0)

            later(0, e1)
            later(1, e2)
            later(2, e3)
            later(3, e4)

        def carry():
            qk = [("qend", k) for k in range(16)]
            cT, sT = rotc[:, j, 0, :], rotc[:, j, 1, :]
            rk = [("rotc", j)]
            tt("dve", rt[0].ap, cT, qend[:, 0, :], ALU.mult, qk + rk, R(rt[0]))
            tt("dve", rt[1].ap, sT, qend[:, 1, :], ALU.mult, qk + rk, R(rt[1]))
            tt("dve", rt[2].ap, sT, qend[:, 0, :], ALU.mult, qk + rk, R(rt[2]))
            tt("dve", rt[3].ap, cT, qend[:, 1, :], ALU.mult, qk + rk, R(rt[3]))
            tt("dve", qinit[:, j, 0, :], rt[0].ap, rt[1].ap, ALU.subtract, R(rt[0], rt[1]), [("qinit", j)])
            tt("dve", qinit[:, j, 1, :], rt[2].ap, rt[3].ap, ALU.add, R(rt[2], rt[3]), [("qinit", j)])

        NI = len(items)
        stageA(0)
        stageA(1)
        yk = None
        ib = 0
        for n in range(NI):
            sb_i, b, q = items[n]
            if n + 2 < NI:
                stageA(n + 2)
            if q == 0:
                yk = nb("yb")
            stageB(n, yk)
            tick()
            if q == 3:
                block_epilogue(sb_i, b, yk, ib)
                ib += 1
                if b == 3:
                    carry()
        flush()
        Wg = next_slab()
        for c in range(4):
            bk = proj_chunk(Wg, c * 128, "main4", gelu_bf, KT=4)
            tv = th2[c % 2]
            act(tv.ap, banks[bk][:, :], AF.Tanh, [("bank", bk), ("gbh", j)], R(tv), bias=gbh[:, j, c:c + 1], scale=0.5)
            stt(ycat[4 + c].ap, tv.ap, 1.0, G[c].ap, ALU.add, ALU.mult, R(tv, G[c]), R(ycat[4 + c]))
        out_proj(ycat, "main4")

    def ffn(i, t):
        hbl = [hb_(kt) for kt in range(8)]
        AR.reset(3072)
        hid = [AR.take(256, BF16) for _ in range(22)]
        acc = [AR.take(512) for _ in range(8)]
        sg = [AR.take(512) for _ in range(4)]
        for k in range(11):
            W = next_slab()
            for jj in range(2):
                jch = 2 * k + jj
                chs = [jch, jch + 22]
                bks = [proj_chunk(W, gv * 256 + jj * 128, "main8", hbl) for gv in range(2)]
                accs = [acc[2 * (jch % 4) + gv] for gv in range(2)]
                wk = [("fcw", i)]
                w_ = lambda ch, kk: fcw[:, i, ch, kk:kk + 1]
                for gv in range(2):
                    act(accs[gv].ap, banks[bks[gv]][:, :], AF.Identity, [("bank", bks[gv]), ("fcb", i)] + wk,
                        R(accs[gv]), bias=fcb[:, i, chs[gv]:chs[gv] + 1], scale=w_(chs[gv], 2))
                for gv in range(2):
                    a, bk, ch = accs[gv], bks[gv], chs[gv]
                    stt(a.ap[:, 1:T], banks[bk][:, 0:T - 1], w_(ch, 1), a.ap[:, 1:T], ALU.mult, ALU.add,
                        [("bank", bk)] + wk + R(a), R(a))
                for gv in range(2):
                    a, bk, ch = accs[gv], bks[gv], chs[gv]
                    stt(a.ap[:, 0:1], fhist[:, i, ch, 1:2], w_(ch, 1), a.ap[:, 0:1], ALU.mult, ALU.add,
                        [("fhist", i, ch)] + wk + R(a), R(a))
                for gv in range(2):
                    a, bk, ch = accs[gv], bks[gv], chs[gv]
                    stt(a.ap[:, 2:T], banks[bk][:, 0:T - 2], w_(ch, 0), a.ap[:, 2:T], ALU.mult, ALU.add,
                        [("bank", bk)] + wk + R(a), R(a))
                for gv in range(2):
                    a, bk, ch = accs[gv], bks[gv], chs[gv]
                    stt(a.ap[:, 0:2], fhist[:, i, ch, 0:2], w_(ch, 0), a.ap[:, 0:2], ALU.mult, ALU.add,
                        [("fhist", i, ch)] + wk + R(a), R(a))
                def fin(chs=chs, bks=bks, accs=accs, s_=sg[jch % 4], jch=jch):
                    for gv in range(2):
                        cp("act", fhist[:, i, chs[gv], :], banks[bks[gv]][:, T - 2:T], [("bank", bks[gv])],
                           [("fhist", i, chs[gv])])
                    act(s_.ap, accs[0].ap, AF.Silu, R(accs[0]), R(s_))
                    tt("pool", hid[jch].ap, s_.ap, accs[1].ap, ALU.mult, R(s_, accs[1]), R(hid[jch]))

                tick()
                later(0, fin)
        flush()
        for m in range(8):
            Wd = next_slab()
            bk = nb("main8")
            for jch in range(22):
                mm(banks[bk][:, :], Wd.ap[:, jch, :], hid[jch].ap, jch == 0, jch == 21,
                   [Wd.reg] + R(hid[jch]), [("bank", bk)])
            tt("dve", xres[:, m, :], xres[:, m, :], banks[bk][:, :], ALU.add,
               [("xres", m), ("bank", bk)], [("xres", m)])

    def odd_mixer(i, t):
        j = i // 2
        hbl = [hb_(kt) for kt in range(8)]
        AR.reset()
        zt = [AR.take(528) for _ in range(4)]
        sA, sB = AR.take(528), AR.take(528)
        pooled = [AR.take(256, BF16) for _ in range(4)]
        Gu = [AR.take(512) for _ in range(4)]
        tmpb = [dict(pc=AR.take(512), p2=AR.take(512), ti=AR.take(512), th=AR.take(512)) for _ in range(2)]
        Gv = [AR.take(512) for _ in range(2)]
        vtok = [AR.take(256, BF16) for _ in range(4)]
        ycat = [AR.take(256, BF16) for _ in range(8)]
        sgt = [AR.take(512) for _ in range(2)]
        sml = [dict(ss=AR.take(1), ms=AR.take(1), rs=AR.take(1)) for _ in range(2)]
        t15 = AR.take(16)
        W = next_slab()
        for g, w in enumerate((2, 4, 8, 16)):
            bk = proj_chunk(W, g * 128, "main8", hbl)
            zh, zm = sub(zt[g], 0, 16), sub(zt[g], 16, 528)
            cp("pool", zh.ap, zhist[:, j, g, :], [("zhist", j, g)], R(zh))
            cp("act", zm.ap, banks[bk][:, :], [("bank", bk)], R(zm))
            z = zt[g]
            tt("pool", sA.ap[:, 1:528], z.ap[:, 1:528], z.ap[:, 0:527], ALU.add, R(z), R(sA))
            Sv = sA
            if g >= 1:
                tt("pool", sB.ap[:, 3:528], sA.ap[:, 3:528], sA.ap[:, 1:526], ALU.add, R(sA), R(sB))
                Sv = sB
            if g >= 2:
                tt("pool", sA.ap[:, 7:528], sB.ap[:, 7:528], sB.ap[:, 3:524], ALU.add, R(sB), R(sA))
                Sv = sA
            if g >= 3:
                tt("pool", sB.ap[:, 15:528], sA.ap[:, 15:528], sA.ap[:, 7:520], ALU.add, R(sA), R(sB))
                Sv = sB
            stt(pooled[g].ap, Sv.ap[:, 16:528], 1.0 / w, zm.ap, ALU.mult, ALU.subtract, R(Sv, zm), R(pooled[g]))
            if t == 0:
                tt("dve", t15.ap[:, 0:15], Sv.ap[:, 16:31], invc[:, g, 0:15], ALU.mult, R(Sv) + ["invc"], R(t15))
                tt("dve", pooled[g].ap[:, 0:15], t15.ap[:, 0:15], zt[g].ap[:, 16:31], ALU.subtract,
                   R(t15, zm, pooled[g]), R(pooled[g]))
            cp("pool", zhist[:, j, g, :], zt[g].ap[:, 512:528], R(zm), [("zhist", j, g)])
            bk2 = nb("main8")
            mm(banks[bk2][:, :], poolw[:, j, g, :], pooled[g].ap, True, True, [("poolw", j)] + R(pooled[g]),
               [("bank", bk2)])
            act(ycat[g].ap, banks[bk2][:, :], AF.Identity, [("bank", bk2), ("pools", j)], R(ycat[g]),
                scale=pools[:, j, g:g + 1])
        W = next_slab()
        for c in range(4):
            bk = proj_chunk(W, c * 128, "main8", hbl)
            gelu2(bk, tmpb[c % 2], Gu[c])
        W = next_slab()
        for tb_ in range(4):
            bk = nb("main8")
            for kt in range(8):
                mm(banks[bk][:, :], hb[:, kt, tb_ * 128:(tb_ + 1) * 128], W.ap[:, kt, :], kt == 0, kt == 7,
                   [W.reg, ("hb", kt)], [("bank", bk)])
            tb = tmpb[tb_ % 2]
            gv = Gv[tb_ % 2]
            sm_ = sml[tb_ % 2]
            gelu2(bk, tb, gv)
            act(tb["p2"].ap, gv.ap, AF.Square, R(gv), R(tb["p2"], sm_["ss"]), scale=0.5, accum=sm_["ss"].ap)
            ts("dve", sm_["ms"].ap, sm_["ss"].ap, 1.0 / 512, EPS, ALU.mult, ALU.add, R(sm_["ss"]), R(sm_["ms"]))
            tt("pool", sm_["rs"].ap, sm_["ms"].ap, cst[:, 0:1], ALU.pow, R(sm_["ms"]) + ["cst"], R(sm_["rs"]))
            stt(vtok[tb_].ap, gv.ap, sm_["rs"].ap, ghbc[:, j, :], ALU.mult, ALU.mult,
                R(gv, sm_["rs"]) + [("ghbc", j)], R(vtok[tb_]))
        for hd in range(4):
            bk = nb("main8")
            for tb_ in range(4):
                mm(banks[bk][:, tb_ * 128:(tb_ + 1) * 128], vtok[tb_].ap[:, hd * 128:(hd + 1) * 128],
                   wT[:, j, hd, :], True, True, [("wT", j)] + R(vtok[tb_]), [("bank", bk)])
            sv_ = sgt[hd % 2]
            tt("dve", sv_.ap.rearrange("p (a b) -> p a b", a=4),
               banks[bk][:, :].rearrange("p (a b) -> p a b", a=4),
               bh[:, j, hd, :].unsqueeze(1).to_broadcast([128, 4, 128]), ALU.add,
               [("bank", bk), ("bh", j)], R(sv_))
            tt("dve", ycat[4 + hd].ap, sv_.ap, Gu[hd].ap, ALU.mult, R(sv_, Gu[hd]), R(ycat[4 + hd]))
        out_proj(ycat, "main8")

    for t in range(ntiles):
        t0 = t * T
        P.dma("sp", xres[:, :, :], xTv[:, :, t0:t0 + T], "xload", writes=[("xres", kt) for kt in range(8)])
        for i in layers:
            j = i // 2
            if t == 0:
                if i % 2 == 0:
                    ssm_setup(j, even_js.index(j))
                    tab_state["j"] = j
                else:
                    odd_setup(j)
            rmsnorm(gmix[:, i, :], "gmix", [hb_(kt) for kt in range(8)])
            if i % 2 == 0:
                even_mixer(i, t)
            else:
                odd_mixer(i, t)
            rmsnorm(gffn[:, i, :], "gffn", [hb_(kt) for kt in range(8)])
            ffn(i, t)
        if final:
            AR.reset(3072)
            ost = [AR.take(512) for _ in range(8)]
            p_after = AR.p
            rmsnorm(gfin[:, :], "gfin", ost)
            AR.reset(p_after)
            ov = arena[:, ost[0].reg[1]:ost[0].reg[1] + 4096].rearrange("p (k t) -> p k t", k=8)
            P.dma("act", outTv[:, :, t0:t0 + T], ov, "ostore", reads=R(*ost), writes=[("outT", t)])
        else:
            P.dma("act", outTv[:, :, t0:t0 + T], xres[:, :, :], "ostore",
                  reads=[("xres", kt) for kt in range(8)], writes=[("outT", t)])
    P.wait("act", [("outT", t) for t in range(ntiles)])
    P.emit()
    P.close()
    names = ["xT"] + list(dr.keys())
    return nc, names, dbg_out


_CACHE = {}


def _get(layers, ntiles, final):
    key = (tuple(layers), ntiles, final)
    if key not in _CACHE:
        _CACHE[key] = build(layers, ntiles, final)
    return _CACHE[key]


def kernel(**inputs):
    com = prep_common(inputs)
    x = np.asarray(inputs["x"], dtype=np.float32)
    nb_ = x.shape[0]
    xTs = [np.ascontiguousarray(x[b].T) for b in range(nb_)]
    nc, names, _ = _get((0, 1, 2, 3), SEQ // T, True)
    in_maps = []
    for b in range(nb_):
        m = {"xT": xTs[b]}
        for n in names:
            if n != "xT":
                m[n] = com[n]
        in_maps.append(m)
    res = run_bass_kernel_spmd(nc, in_maps, core_ids=list(range(nb_)))
    out = np.stack([np.ascontiguousarray(res.results[b]["outT"].T) for b in range(nb_)], axis=0)
    return out.astype(np.float32)
```

```python
import contextlib
import numpy as np
import concourse.bass as bass
import concourse.mybir as mybir
from concourse.bass_utils import run_bass_kernel_spmd

F32 = mybir.dt.float32
BF16 = mybir.dt.bfloat16
ALU = mybir.AluOpType
AF = mybir.ActivationFunctionType

D = 1024
SEQ = 4096
DEPTH = 4
DFF = 2816
T = 512
TS = 256
NSLOT = 4
EPS = 1e-6
ENGS = ("pe", "act", "dve", "pool", "sp")


class Op:
    __slots__ = ("eng", "fn", "deps", "signal", "sigval", "sem", "is_dma", "group")

    def __init__(self, eng, fn):
        self.eng = eng
        self.fn = fn
        self.deps = []
        self.signal = False
        self.sigval = 0
        self.sem = None
        self.is_dma = False
        self.group = None


class Prog:
    def __init__(self, nc):
        self.nc = nc
        self.ops = {e: [] for e in ENGS}
        self.state = {}
        self.stack = contextlib.ExitStack()
        self.dma_groups = {}
        self.n = 0

    def sb(self, shape, dtype=F32, name=None):
        self.n += 1
        return self.stack.enter_context(self.nc.sbuf_tensor(name or f"sb{self.n}", list(shape), dtype))

    def ps(self, shape, dtype=F32, name=None):
        self.n += 1
        return self.stack.enter_context(self.nc.psum_tensor(name or f"ps{self.n}", list(shape), dtype))

    @staticmethod
    def _norm(k):
        if isinstance(k, tuple) and len(k) == 3 and isinstance(k[0], str) and k[0].startswith("@"):
            return k[0], int(k[1]), int(k[2])
        return k, 0, 1

    def _segs(self, ns, lo, hi):
        L = self.state.setdefault(ns, [])
        out = []
        newL = []
        cur = lo
        for sg in L:
            a, b, w, r = sg
            if b <= lo or a >= hi:
                newL.append(sg)
                continue
            if a < lo:
                newL.append([a, lo, w, list(r)])
                a = lo
            if b > hi:
                newL.append([hi, b, w, list(r)])
                b = hi
            mid = [a, b, w, r]
            newL.append(mid)
            out.append(mid)
        out.sort(key=lambda x: x[0])
        filled = []
        for sg in out:
            if sg[0] > cur:
                g = [cur, sg[0], None, []]
                newL.append(g)
                filled.append(g)
            filled.append(sg)
            cur = sg[1]
        if cur < hi:
            g = [cur, hi, None, []]
            newL.append(g)
            filled.append(g)
        self.state[ns] = newL
        return filled

    def _track(self, o, reads, writes, skip_same_eng=False):
        deps = []
        rn = [self._norm(k) for k in reads]
        wn = [self._norm(k) for k in writes]
        for ns, lo, hi in rn:
            for sg in self._segs(ns, lo, hi):
                if sg[2] is not None:
                    deps.append(sg[2])
        for ns, lo, hi in wn:
            for sg in self._segs(ns, lo, hi):
                if sg[2] is not None:
                    deps.append(sg[2])
                last = {}
                for r in sg[3]:
                    if r.is_dma:
                        deps.append(r)
                    else:
                        last[r.eng] = r
                deps.extend(last.values())
        for ns, lo, hi in rn:
            for sg in self._segs(ns, lo, hi):
                sg[3].append(o)
        for ns, lo, hi in wn:
            segs = self._segs(ns, lo, hi)
            L = self.state[ns]
            for sg in segs:
                L.remove(sg)
            L.append([lo, hi, o, []])
        seen = set()
        for d in deps:
            if d is o or id(d) in seen:
                continue
            if skip_same_eng and (not d.is_dma) and d.eng == o.eng:
                continue
            seen.add(id(d))
            o.deps.append(d)

    def op(self, eng, fn, reads=(), writes=()):
        o = Op(eng, fn)
        self._track(o, reads, writes, skip_same_eng=(eng == "pe"))
        self.ops[eng].append(o)
        return o

    def dma(self, eng, out, in_, group, reads=(), writes=(), **kw):
        o = Op(eng, lambda e: e.dma_start(out=out, in_=in_, **kw))
        o.is_dma = True
        o.group = group
        self._track(o, reads, writes)
        lst = self.dma_groups.setdefault(group, [])
        if lst and not group.startswith("all:") and lst[-1] not in o.deps:
            o.deps.append(lst[-1])
        self.ops[eng].append(o)
        lst.append(o)
        return o

    def wait(self, eng, reads):
        o = Op(eng, lambda e: None)
        self._track(o, reads, ())
        self.ops[eng].append(o)
        return o

    def emit(self):
        nc = self.nc
        for e in ENGS:
            for o in self.ops[e]:
                for d in o.deps:
                    d.signal = True
        sems = {}
        for e in ENGS:
            sems[e] = self.stack.enter_context(nc.semaphore(f"s_{e}"))
            c = 0
            for o in self.ops[e]:
                if o.is_dma:
                    continue
                if o.signal:
                    c += 1
                    o.sigval = c
                    o.sem = sems[e]
        for g, lst in self.dma_groups.items():
            s = self.stack.enter_context(nc.semaphore(f"d_{len(sems)}"))
            sems["dma:" + g] = s
            if g.startswith("all:"):
                for o in lst:
                    o.sem = s
                    o.sigval = 16 * len(lst)
            else:
                for i, o in enumerate(lst):
                    o.sem = s
                    o.sigval = 16 * (i + 1)
        engmap = {"pe": "tensor", "act": "scalar", "dve": "vector", "pool": "gpsimd", "sp": "sync"}
        with nc.Block() as block:
            for e in ENGS:
                ops = self.ops[e]
                if not ops:
                    continue

                def body(eng, ops=ops):
                    waited = {}
                    for o in ops:
                        need = {}
                        for d in o.deps:
                            k = id(d.sem)
                            if d.sigval > need.get(k, (None, 0))[1]:
                                need[k] = (d.sem, d.sigval)
                        for k, (s, v) in need.items():
                            if waited.get(k, 0) >= v:
                                continue
                            eng.wait_ge(s, v)
                            waited[k] = v
                        inst = o.fn(eng)
                        if inst is None:
                            continue
                        if o.is_dma:
                            inst.then_inc(o.sem, 16)
                        elif o.signal:
                            inst.then_inc(o.sem, 1)

                getattr(block, engmap[e])(body)

    def close(self):
        self.stack.close()


def _pair(a):
    a = np.asarray(a, dtype=np.float32)
    rest = a.shape[2:]
    a = a.reshape((16, 2, 64) + rest)
    perm = (1, 2, 0) + tuple(range(3, 3 + len(rest)))
    return np.ascontiguousarray(a.transpose(perm).reshape((128, 16) + rest))


def _cols(v, n):
    return np.ascontiguousarray(np.asarray(v, dtype=np.float32).reshape(n, 128).T)


def prep_common(inp):
    f = lambda a: np.ascontiguousarray(np.asarray(a, dtype=np.float32))
    com = {}
    com["gmix"] = f(np.asarray(inp["norm_mix_g"]).reshape(4, 8, 128).transpose(2, 0, 1))
    com["gffn"] = f(np.asarray(inp["norm_ffn_g"]).reshape(4, 8, 128).transpose(2, 0, 1))
    com["gfin"] = _cols(inp["norm_final_g"], 8)
    for j in range(2):
        com[f"ewin{j}"] = f(inp["even_w_in"][j])
        com[f"econv{j}"] = f(np.asarray(inp["even_conv_w"][j]).reshape(3, 4, 128).transpose(2, 1, 0))
        ls = np.broadcast_to(np.asarray(inp["ssm_log_step"][j])[:, None], (32, 64))
        com[f"sls{j}"] = _pair(ls)
        com[f"sare{j}"] = _pair(inp["ssm_a_re"][j])
        com[f"saim{j}"] = _pair(inp["ssm_a_im"][j])
        com[f"sbre{j}"] = _pair(inp["ssm_b_re"][j])
        com[f"sbim{j}"] = _pair(inp["ssm_b_im"][j])
        com[f"scre{j}"] = _pair(np.asarray(inp["ssm_c_re"][j]).transpose(0, 2, 1))
        com[f"scim{j}"] = _pair(np.asarray(inp["ssm_c_im"][j]).transpose(0, 2, 1))
        com[f"sd{j}"] = _cols(inp["ssm_d"][j], 4)
        com[f"glw{j}"] = f(inp["ssm_glu_w"][j])
        com[f"glb{j}"] = _cols(inp["ssm_glu_b"][j], 4)
        com[f"ewout{j}"] = f(inp["even_w_out"][j])
        com[f"owin{j}"] = f(inp["odd_w_in"][j])
        com[f"poolw{j}"] = f(np.asarray(inp["pool_w"][j]).transpose(1, 0, 2))
        com[f"pools{j}"] = _cols(inp["pool_scale"][j], 4)
        com[f"sgng{j}"] = f(np.broadcast_to(np.asarray(inp["sgu_norm_g"][j])[None, :], (128, 512)))
        com[f"sgw{j}"] = f(np.asarray(inp["sgu_w"][j]).transpose(2, 0, 1))
        com[f"sgb{j}"] = f(np.broadcast_to(np.asarray(inp["sgu_b"][j])[None], (128, 4, 128)))
        com[f"owout{j}"] = f(inp["odd_w_out"][j])
    for i in range(4):
        com[f"fup{i}"] = f(inp["ffn_w_up"][i])
        com[f"fcw{i}"] = f(np.asarray(inp["ffn_conv_w"][i]).reshape(3, 44, 128).transpose(2, 1, 0))
        com[f"fcb{i}"] = _cols(inp["ffn_conv_b"][i], 44)
        com[f"fdn{i}"] = f(inp["ffn_w_down"][i])
    s = np.arange(128)
    com["trilh"] = f(0.5 * (s[:, None] <= s[None, :]))
    invc = np.zeros((128, 4, 16), np.float32)
    for g, w in enumerate((2, 4, 8, 16)):
        invc[:, g, :] = 1.0 / np.minimum(np.arange(1, 17), w)
    com["invc"] = invc
    com["ident"] = f(np.eye(128))
    return com


SHAPES = {
    "gmix": [128, 4, 8], "gffn": [128, 4, 8], "gfin": [128, 8],
    "trilh": [128, 128], "invc": [128, 4, 16], "ident": [128, 128],
}
for _j in range(2):
    SHAPES.update({
        f"ewin{_j}": [1024, 2048], f"econv{_j}": [128, 4, 3], f"sls{_j}": [128, 16],
        f"sare{_j}": [128, 16], f"saim{_j}": [128, 16], f"sbre{_j}": [128, 16, 16],
        f"sbim{_j}": [128, 16, 16], f"scre{_j}": [128, 16, 16], f"scim{_j}": [128, 16, 16],
        f"sd{_j}": [128, 4], f"glw{_j}": [512, 512], f"glb{_j}": [128, 4], f"ewout{_j}": [1024, 1024],
        f"owin{_j}": [1024, 1536], f"poolw{_j}": [128, 4, 128], f"pools{_j}": [128, 4],
        f"sgng{_j}": [128, 512], f"sgw{_j}": [128, 4, 128], f"sgb{_j}": [128, 4, 128],
        f"owout{_j}": [1024, 1024],
    })
for _i in range(4):
    SHAPES.update({f"fup{_i}": [1024, 5632], f"fcw{_i}": [128, 44, 3], f"fcb{_i}": [128, 44],
                   f"fdn{_i}": [2816, 1024]})


class V:
    __slots__ = ("ap", "reg")

    def __init__(self, ap, reg):
        self.ap = ap
        self.reg = reg


def build(layers=(0, 1, 2, 3), ntiles=8, final=True, dbg=()):
    nc = bass.Bass("TRN2", target_bir_lowering=False)
    P = Prog(nc)
    dr = {}
    dbg_out = {}

    def din(name):
        if name not in dr:
            dr[name] = nc.dram_tensor(name, SHAPES[name], F32, kind="ExternalInput").ap()
        return dr[name]

    xT = nc.dram_tensor("xT", [D, SEQ], F32, kind="ExternalInput").ap()
    outT = nc.dram_tensor("outT", [D, SEQ], F32, kind="ExternalOutput").ap()
    xTv = xT.rearrange("(kt p) t -> p kt t", p=128)
    outTv = outT.rearrange("(kt p) t -> p kt t", p=128)
    nlay = len(layers)
    scr = nc.dram_tensor("scr", [nlay * 26, 128, 4096], BF16, kind="Internal").ap()
    tabd = nc.dram_tensor("tabd", [2, 128, 2 * 16 * TS], F32, kind="Internal").ap()

    xres = P.sb([128, 8, T], F32, "xres")
    hb = P.sb([128, 8, T], BF16, "hb")
    ring = P.sb([128, NSLOT, 4096], BF16, "ring")
    stage = P.sb([128, 2, 2048], F32, "stage")
    tab = P.sb([128, 2, 16, TS], F32, "tab")
    AW = 15616
    arena = P.sb([128, AW], F32, "arena")
    ones = P.sb([128, 128], BF16, "ones")
    onesf = P.sb([128, 128], F32, "onesf")
    ident = P.sb([128, 128], F32, "ident_sb")
    cst = P.sb([128, 4], F32, "cst")
    gmix = P.sb([128, 4, 8], F32, "gmix_sb")
    gffn = P.sb([128, 4, 8], F32, "gffn_sb")
    gfin = P.sb([128, 8], F32, "gfin_sb")
    fcw = P.sb([128, 4, 44, 3], F32, "fcw_sb")
    fcb = P.sb([128, 4, 44], F32, "fcb_sb")
    fhist = P.sb([128, 4, 44, 2], F32, "fhist")
    cxhist = P.sb([128, 2, 4, 2], F32, "cxhist")
    zhist = P.sb([128, 2, 4, 16], F32, "zhist")
    qinit = P.sb([128, 2, 2, 16], F32, "qinit")
    qend = P.sb([128, 2, 16], F32, "qend")
    econv = P.sb([128, 2, 4, 3], F32, "econv_sb")
    Bl = P.sb([128, 2, 2, 4, 2, 128], BF16, "Bl")
    Cl = P.sb([128, 2, 16, 6, 32], BF16, "Cl")
    K0l = P.sb([128, 2, 16, 32], BF16, "K0l")
    r2 = P.sb([128, 2, 16], F32, "r2")
    maskq = P.sb([128, 4], F32, "maskq")
    rdec = P.sb([128, 2, 16], F32, "rdec")
    rotc = P.sb([128, 2, 2, 16], F32, "rotc")
    dq = P.sb([128, 2, 4], F32, "dq")
    gbh = P.sb([128, 2, 4], F32, "gbh")
    pools = P.sb([128, 2, 4], F32, "pools_sb")
    poolw = P.sb([128, 2, 4, 128], BF16, "poolw_sb")
    wT = P.sb([128, 2, 4, 128], BF16, "wT")
    bh = P.sb([128, 2, 4, 128], F32, "bh")
    ghbc = P.sb([128, 2, 512], F32, "ghbc")
    invc = P.sb([128, 4, 16], F32, "invc_sb")
    banks = [P.ps([128, 512], F32, f"bank{i}") for i in range(8)]

    def bankv(i, lo=0, hi=512):
        return V(banks[i][:, lo:hi], ("bank", i))

    class Arena:
        def __init__(self):
            self.p = 0

        def reset(self, p=0):
            self.p = p

        def take(self, words, dtype=F32, shape=None):
            lo = self.p
            self.p += (words + 7) // 8 * 8
            assert self.p <= AW, (self.p, AW)
            ap = arena[:, lo:lo + words]
            if dtype == BF16:
                ap = ap.bitcast(BF16)
            return V(ap, ("@ar", lo, lo + words))

    AR = Arena()

    def R(*vs):
        out = []
        for v in vs:
            if v is None:
                continue
            out.append(v.reg if isinstance(v, V) else v)
        return out

    def act(out, in_, func, reads, writes, bias=None, scale=None, accum=None):
        kw = {}
        if bias is not None:
            kw["bias"] = bias
        if scale is not None:
            kw["scale"] = scale
        if accum is not None:
            kw["accum_out"] = accum
        return P.op("act", lambda e: e.activation(out, in_, func, **kw), reads, writes)

    def tt(eng, out, a, b, op, reads, writes):
        return P.op(eng, lambda e: e.tensor_tensor(out, a, b, op), reads, writes)

    def ts(eng, out, a, s1, s2, op0, op1, reads, writes):
        if op1 is None:
            return P.op(eng, lambda e: e.tensor_scalar(out, a, s1, None, op0), reads, writes)
        return P.op(eng, lambda e: e.tensor_scalar(out, a, s1, s2, op0, op1), reads, writes)

    def stt(out, a, s, b, op0, op1, reads, writes):
        return P.op("dve", lambda e: e.scalar_tensor_tensor(out, a, s, b, op0, op1), reads, writes)

    def cp(eng, out, in_, reads, writes):
        if eng == "act":
            return P.op("act", lambda e: e.activation(out, in_, AF.Copy), reads, writes)
        return P.op(eng, lambda e: e.tensor_copy(out, in_), reads, writes)

    def mm(out, lhsT, rhs, start, stop, reads, writes, tp=None):
        if tp is None:
            return P.op("pe", lambda e: e.matmul(out, lhsT, rhs, start=start, stop=stop), reads, writes)
        return P.op("pe", lambda e: e.matmul(out, lhsT, rhs, start=start, stop=stop, tile_position=tp),
                    reads, writes)

    def dump(name, ap, shape, reads):
        if name not in dbg:
            return
        t = nc.dram_tensor("dbg_" + name, list(shape), ap.dtype, kind="ExternalOutput").ap()
        dbg_out[name] = t
        P.dma("act", t, ap, "all:dbg", reads=reads, writes=[("dbgout", name)])

    smc = {"n": 0}

    def small_dma(dst_ap, src_ap, writes):
        g = "sm%d" % (smc["n"] % 4)
        smc["n"] += 1
        P.dma("sp", dst_ap, src_ap, g, writes=writes)

    def load_small(dst_ap, name, key):
        small_dma(dst_ap, din(name), [key])

    P.op("dve", lambda e: e.memset(ones[:], 1.0), writes=["ones"])
    P.op("dve", lambda e: e.memset(onesf[:], 1.0), writes=["onesf"])
    P.op("dve", lambda e: e.memset(cst[:, 0:1], -0.5), writes=["cst"])
    P.op("dve", lambda e: e.memset(cst[:, 1:2], EPS), reads=["cst"], writes=["cst"])
    P.op("dve", lambda e: e.memset(cst[:, 2:3], 4.0), reads=["cst"], writes=["cst"])
    P.op("dve", lambda e: e.memset(cst[:, 3:4], 1.0), reads=["cst"], writes=["cst"])
    P.op("dve", lambda e: e.memset(fhist[:], 0.0), writes=["fhist_all"])
    P.op("dve", lambda e: e.memset(cxhist[:], 0.0), writes=["cxhist_all"])
    P.op("dve", lambda e: e.memset(zhist[:], 0.0), writes=["zhist_all"])
    P.op("dve", lambda e: e.memset(qinit[:], 0.0), writes=["qinit_all"])
    load_small(gmix[:], "gmix", "gmix")
    load_small(gffn[:], "gffn", "gffn")
    load_small(gfin[:], "gfin", "gfin")
    load_small(invc[:], "invc", "invc")
    for i in layers:
        load_small(fcw[:, i], f"fcw{i}", ("fcw", i))
        load_small(fcb[:, i], f"fcb{i}", ("fcb", i))

    def wview(name, K):
        return din(name).rearrange("(kt p) n -> p kt n", p=128)

    def slabs_for(i):
        j = i // 2
        L = []
        if i % 2 == 0:
            w = wview(f"ewin{j}", 1024)
            L.append((8, 512, [(w, 0, 0, 256), (w, 1024, 256, 256)]))
            L.append((8, 512, [(w, 256, 0, 256), (w, 1280, 256, 256)]))
            L.append((8, 512, [(w, 1536, 0, 512)]))
            L.append((8, 512, [(w, 512, 0, 512)]))
            L.append((4, 512, [(wview(f"glw{j}", 512), 0, 0, 512)]))
            wo = wview(f"ewout{j}", 1024)
        else:
            w = wview(f"owin{j}", 1024)
            L.append((8, 512, [(w, 0, 0, 512)]))
            L.append((8, 512, [(w, 512, 0, 512)]))
            L.append((8, 512, [(w, 1024, 0, 512)]))
            wo = wview(f"owout{j}", 1024)
        L.append((8, 512, [(wo, 0, 0, 512)]))
        L.append((8, 512, [(wo, 512, 0, 512)]))
        wu = wview(f"fup{i}", 1024)
        for k in range(11):
            L.append((8, 512, [(wu, 256 * k, 0, 256), (wu, 2816 + 256 * k, 256, 256)]))
        wd = wview(f"fdn{i}", 2816)
        for m in range(8):
            L.append((22, 128, [(wd, 128 * m, 0, 128)]))
        return L

    lay_slabs = {i: slabs_for(i) for i in layers}
    seq = []
    for t in range(ntiles):
        for li, i in enumerate(layers):
            for s in range(len(lay_slabs[i])):
                seq.append((t, li, i, s))
    stream = {"next": 0, "cur": 0, "cast": 0, "stg": 0}

    def slot_reg(slot):
        return ("@ring%d" % slot, 0, 4096)

    def make_load(n):
        t, li, i, s = seq[n]
        KT, W, pieces = lay_slabs[i][s]
        slot = n % NSLOT
        sid = li * 26 + s
        nel = KT * W
        if t == 0:
            h0 = (KT + 1) // 2
            for (k0, k1) in ((0, h0), (h0, KT)):
                if k1 <= k0:
                    continue
                sg = stream["stg"] % 2
                stream["stg"] += 1
                nk = k1 - k0
                sview = stage[:, sg, 0:nk * W].rearrange("p (k w) -> p k w", k=nk)
                for pi, (w, c0, d0, wd_) in enumerate(pieces):
                    P.dma("sp", sview[:, :, d0:d0 + wd_], w[:, k0:k1, c0:c0 + wd_], "stg%d" % sg,
                          writes=[("stage", sg, pi)])
                eng = "act"
                stream["cast"] += 1
                dst = ring[:, slot, k0 * W:k1 * W]
                src = stage[:, sg, 0:nk * W]
                cp(eng, dst, src, [("stage", sg, pi) for pi in range(len(pieces))],
                   [("@ring%d" % slot, k0 * W, k1 * W)])
            P.dma("act", scr[sid][:, 0:nel], ring[:, slot, 0:nel], "scrst%d" % slot,
                  reads=[("@ring%d" % slot, 0, nel)], writes=[("scr", sid)])
        else:
            P.dma("sp", ring[:, slot, 0:nel], scr[sid][:, 0:nel], "ring%d" % slot,
                  reads=[("scr", sid)], writes=[("@ring%d" % slot, 0, nel)])

    def next_slab():
        n = stream["cur"]
        stream["cur"] += 1
        while stream["next"] < min(len(seq), n + NSLOT):
            make_load(stream["next"])
            stream["next"] += 1
        t, li, i, s = seq[n]
        KT, W, _ = lay_slabs[i][s]
        slot = n % NSLOT
        view = ring[:, slot, 0:KT * W].rearrange("p (k w) -> p k w", k=KT)
        return V(view, slot_reg(slot))

    rot = {"main": 0, "bu": 0, "yb": 0}
    pools_ = {"main4": [0, 1, 2, 3], "main8": [0, 1, 2, 3, 4, 5, 6, 7], "bu": [4, 5, 0, 1], "yb": [6, 7]}

    def nb(pool):
        key = "main" if pool.startswith("main") else pool
        lst = pools_[pool]
        b = lst[rot[key] % len(lst)]
        rot[key] += 1
        return b

    load_small(ident[:], "ident", "ident")
    even_js = [i // 2 for i in layers if i % 2 == 0]
    odd_js = [i // 2 for i in layers if i % 2 == 1]

    def ssm_setup(j, jj):
        AR.reset()
        sm = lambda: AR.take(16)
        ls, are, aim = sm(), sm(), sm()
        for v, nm in ((ls, "sls"), (are, "sare"), (aim, "saim")):
            small_dma(v.ap, din(f"{nm}{j}"), R(v))
        big = {}
        for nm in ("sbre", "sbim", "scre", "scim"):
            big[nm] = AR.take(256)
            small_dma(big[nm].ap.rearrange("p (k h) -> p k h", k=16), din(f"{nm}{j}"), R(big[nm]))
        small_dma(econv[:, j], din(f"econv{j}"), [("econv", j)])
        sdl, glbl = AR.take(4), AR.take(4)
        small_dma(sdl.ap, din(f"sd{j}"), R(sdl))
        small_dma(glbl.ap, din(f"glb{j}"), R(glbl))
        ts("dve", dq[:, j, :], sdl.ap, 0.25, None, ALU.mult, None, R(sdl), [("dq", j)])
        ts("dve", gbh[:, j, :], glbl.ap, 0.5, None, ALU.mult, None, R(glbl), [("gbh", j)])

        dt_, xr, th = sm(), sm(), sm()
        act(dt_.ap, ls.ap, AF.Exp, R(ls), R(dt_))
        tt("dve", xr.ap, are.ap, dt_.ap, ALU.mult, R(are, dt_), R(xr))
        tt("dve", th.ap, aim.ap, dt_.ap, ALU.mult, R(aim, dt_), R(th))
        rv = V(rdec[:, j, :], ("rdec", j))
        act(rv.ap, xr.ap, AF.Exp, R(xr), R(rv))
        al, a2, ps_, pc_ = sm(), sm(), sm(), sm()
        ts("dve", al.ap, th.ap, 1.0 / 64, None, ALU.mult, None, R(th), R(al))
        tt("dve", a2.ap, al.ap, al.ap, ALU.mult, R(al), R(a2))

        def horner(p, coefs):
            ts("dve", p.ap, a2.ap, coefs[0], coefs[1], ALU.mult, ALU.add, R(a2), R(p))
            for c in coefs[2:]:
                tt("dve", p.ap, p.ap, a2.ap, ALU.mult, R(p, a2), R(p))
                ts("dve", p.ap, p.ap, c, None, ALU.add, None, R(p), R(p))

        horner(ps_, [1.0 / 362880, -1.0 / 5040, 1.0 / 120, -1.0 / 6, 1.0])
        tt("dve", ps_.ap, ps_.ap, al.ap, ALU.mult, R(ps_, al), R(ps_))
        horner(pc_, [-1.0 / 3628800, 1.0 / 40320, -1.0 / 720, 1.0 / 24, -0.5, 1.0])
        t1, t2 = sm(), sm()
        for _ in range(6):
            tt("dve", t1.ap, pc_.ap, pc_.ap, ALU.mult, R(pc_), R(t1))
            tt("dve", t2.ap, ps_.ap, ps_.ap, ALU.mult, R(ps_), R(t2))
            tt("dve", ps_.ap, ps_.ap, pc_.ap, ALU.mult, R(ps_, pc_), R(ps_))
            ts("dve", ps_.ap, ps_.ap, 2.0, None, ALU.mult, None, R(ps_), R(ps_))
            tt("dve", pc_.ap, t1.ap, t2.ap, ALU.subtract, R(t1, t2), R(pc_))
        c1, s1 = pc_, ps_
        nre, nim, den, fre, fim = sm(), sm(), sm(), sm(), sm()
        tt("dve", nre.ap, rv.ap, c1.ap, ALU.mult, R(rv, c1), R(nre))
        ts("dve", nre.ap, nre.ap, -1.0, None, ALU.add, None, R(nre), R(nre))
        tt("dve", nim.ap, rv.ap, s1.ap, ALU.mult, R(rv, s1), R(nim))
        tt("dve", den.ap, are.ap, are.ap, ALU.mult, R(are), R(den))
        tt("dve", t1.ap, aim.ap, aim.ap, ALU.mult, R(aim), R(t1))
        tt("dve", den.ap, den.ap, t1.ap, ALU.add, R(den, t1), R(den))
        P.op("dve", lambda e: e.reciprocal(den.ap, den.ap), R(den), R(den))
        tt("dve", fre.ap, nre.ap, are.ap, ALU.mult, R(nre, are), R(fre))
        tt("dve", t1.ap, nim.ap, aim.ap, ALU.mult, R(nim, aim), R(t1))
        tt("dve", fre.ap, fre.ap, t1.ap, ALU.add, R(fre, t1), R(fre))
        tt("dve", fre.ap, fre.ap, den.ap, ALU.mult, R(fre, den), R(fre))
        tt("dve", fim.ap, nim.ap, are.ap, ALU.mult, R(nim, are), R(fim))
        tt("dve", t1.ap, nre.ap, aim.ap, ALU.mult, R(nre, aim), R(t1))
        tt("dve", fim.ap, fim.ap, t1.ap, ALU.subtract, R(fim, t1), R(fim))
        tt("dve", fim.ap, fim.ap, den.ap, ALU.mult, R(fim, den), R(fim))
        v3 = lambda v: v.ap.rearrange("p (k h) -> p k h", k=16)
        bc = lambda v: v.ap.unsqueeze(2).to_broadcast([128, 16, 16])
        Bre, Bim, tA = AR.take(256), AR.take(256), AR.take(256)
        tt("dve", v3(Bre), v3(big["sbre"]), bc(fre), ALU.mult, R(big["sbre"], fre), R(Bre))
        tt("dve", v3(tA), v3(big["sbim"]), bc(fim), ALU.mult, R(big["sbim"], fim), R(tA))
        tt("dve", v3(Bre), v3(Bre), v3(tA), ALU.subtract, R(Bre, tA), R(Bre))
        tt("dve", v3(Bim), v3(big["sbim"]), bc(fre), ALU.mult, R(big["sbim"], fre), R(Bim))
        tt("dve", v3(tA), v3(big["sbre"]), bc(fim), ALU.mult, R(big["sbre"], fim), R(tA))
        tt("dve", v3(Bim), v3(Bim), v3(tA), ALU.add, R(Bim, tA), R(Bim))
        lr, li = sm(), nim
        tt("dve", lr.ap, rv.ap, c1.ap, ALU.mult, R(rv, c1), R(lr))
        tt("dve", r2[:, j, :], rv.ap, rv.ap, ALU.mult, R(rv), [("r2", j)])
        LBre, LBim = AR.take(256), AR.take(256)
        tt("dve", v3(LBre), v3(Bre), bc(lr), ALU.mult, R(Bre, lr), R(LBre))
        tt("dve", v3(tA), v3(Bim), bc(li), ALU.mult, R(Bim, li), R(tA))
        tt("dve", v3(LBre), v3(LBre), v3(tA), ALU.subtract, R(LBre, tA), R(LBre))
        tt("dve", v3(LBim), v3(Bim), bc(lr), ALU.mult, R(Bim, lr), R(LBim))
        tt("dve", v3(tA), v3(Bre), bc(li), ALU.mult, R(Bre, li), R(tA))
        tt("dve", v3(LBim), v3(LBim), v3(tA), ALU.add, R(LBim, tA), R(LBim))
        Mb = {}
        for st_, comp, Bv in ((0, 0, Bre), (0, 1, Bim), (1, 0, LBre), (1, 1, LBim)):
            M = AR.take(512)
            P.op("dve", lambda e, M=M: e.memset(M.ap, 0.0), (), R(M))
            M4 = M.ap.rearrange("p (k g h) -> p k g h", k=16, g=2)
            B3 = v3(Bv)
            cp("dve", M4[0:64, :, 0, :], B3[0:64], R(Bv, M), R(M))
            cp("dve", M4[64:128, :, 1, :], B3[64:128], R(Bv, M), R(M))
            if st_ == 0:
                Mb[comp] = AR.take(256, BF16)
                cp("dve", Mb[comp].ap, M.ap, R(M), R(Mb[comp]))
            for b in range(4):
                bk = nb("main8")
                P.op("pe", lambda e, bk=bk, M=M, b=b: e.transpose(banks[bk][:, 0:128],
                                                                  M.ap[:, b * 128:(b + 1) * 128], ident[:]),
                     R(M) + ["ident"], [("bank", bk)])
                cp("act", Bl[:, j, st_, b, comp, :], banks[bk][:, 0:128], [("bank", bk)], [("Bl", j)])
        CLre, CLim = AR.take(256), AR.take(256)
        tt("dve", v3(CLre), v3(big["scre"]), bc(lr), ALU.mult, R(big["scre"], lr), R(CLre))
        tt("dve", v3(tA), v3(big["scim"]), bc(li), ALU.mult, R(big["scim"], li), R(tA))
        tt("dve", v3(CLre), v3(CLre), v3(tA), ALU.subtract, R(CLre, tA), R(CLre))
        tt("dve", v3(CLim), v3(big["scre"]), bc(li), ALU.mult, R(big["scre"], li), R(CLim))
        tt("dve", v3(tA), v3(big["scim"]), bc(lr), ALU.mult, R(big["scim"], lr), R(tA))
        tt("dve", v3(CLim), v3(CLim), v3(tA), ALU.add, R(CLim, tA), R(CLim))
        P.op("dve", lambda e: e.memset(Cl[:, j], 0.0), (), [("Cl", j)])
        csrc = ((big["scre"], 0.25), (big["scre"], -0.25), (big["scim"], -0.25),
                (CLre, 0.25), (CLre, -0.25), (CLim, -0.25))
        for m, (sv_, sc) in enumerate(csrc):
            src = v3(sv_)
            for g2 in range(2):
                ts("dve", Cl[64 * g2:64 * g2 + 64, j, :, m, 16 * g2:16 * g2 + 16], src[64 * g2:64 * g2 + 64],
                   sc, None, ALU.mult, None, R(sv_) + [("Cl", j)], [("Cl", j)])
        for q in range(4):
            P.op("dve", lambda e, q=q: e.reduce_sum(maskq[:, q:q + 1], ident[:, 32 * q:32 * q + 32],
                                                    axis=mybir.AxisListType.X), ["ident"], [("maskq", q)])
        bkK = nb("main8")
        for k in range(16):
            b, q = divmod(k, 4)
            ko = banks[bkK][32 * q:32 * q + 32, 32 * b:32 * b + 32]
            mm(ko, Mb[0].ap[:, 32 * k:32 * k + 32], Cl[:, j, k, 0, :], True, False,
               R(Mb[0]) + [("Cl", j)], [("bank", bkK)], tp=(0, 32 * q))
            mm(ko, Mb[1].ap[:, 32 * k:32 * k + 32], Cl[:, j, k, 2, :], False, True,
               R(Mb[1]) + [("Cl", j)], [("bank", bkK)], tp=(0, 32 * q))
        for k in range(16):
            b, q = divmod(k, 4)
            ts("dve", K0l[:, j, k, :], banks[bkK][:, 32 * b:32 * b + 32], maskq[:, q:q + 1], None, ALU.mult, None,
               [("bank", bkK), ("maskq", q)], [("K0l", j)])
        tabv = ("tab",)
        cp("dve", tab[:, 0, :, 0:1], c1.ap.unsqueeze(2), R(c1) + ["tab"], ["tab"])
        cp("dve", tab[:, 1, :, 0:1], s1.ap.unsqueeze(2), R(s1) + ["tab"], ["tab"])
        w1, w2 = AR.take(2048), AR.take(2048)
        n = 1
        while n < TS:
            cb = tab[:, 0, :, n - 1:n].to_broadcast([128, 16, n])
            sb_ = tab[:, 1, :, n - 1:n].to_broadcast([128, 16, n])
            a1 = w1.ap[:, 0:16 * n].rearrange("p (k t) -> p k t", k=16)
            a2_ = w2.ap[:, 0:16 * n].rearrange("p (k t) -> p k t", k=16)
            tt("dve", a1, tab[:, 0, :, 0:n], cb, ALU.mult, ["tab"], R(w1))
            tt("dve", a2_, tab[:, 1, :, 0:n], sb_, ALU.mult, ["tab"], R(w2))
            tt("dve", tab[:, 0, :, n:2 * n], a1, a2_, ALU.subtract, R(w1, w2) + ["tab"], ["tab"])
            tt("dve", a1, tab[:, 1, :, 0:n], cb, ALU.mult, ["tab"], R(w1))
            tt("dve", a2_, tab[:, 0, :, 0:n], sb_, ALU.mult, ["tab"], R(w2))
            tt("dve", tab[:, 1, :, n:2 * n], a1, a2_, ALU.add, R(w1, w2) + ["tab"], ["tab"])
            n *= 2
        cp("dve", rotc[:, j, 0, :].unsqueeze(2), tab[:, 0, :, TS - 1:TS], ["tab"], [("rotc", j)])
        cp("dve", rotc[:, j, 1, :].unsqueeze(2), tab[:, 1, :, TS - 1:TS], ["tab"], [("rotc", j)])
        P.dma("act", tabd[jj], tab[:].rearrange("p c k t -> p (c k t)"), "tabst", reads=["tab"],
              writes=[("tabd", jj)])

    def odd_setup(j):
        AR.reset()
        small_dma(pools[:, j, :], din(f"pools{j}"), [("pools", j)])
        pw = AR.take(512)
        small_dma(pw.ap.rearrange("p (g d) -> p g d", g=4), din(f"poolw{j}"), R(pw))
        cp("dve", poolw[:, j].rearrange("p g d -> p (g d)"), pw.ap, R(pw), [("poolw", j)])
        sw, tr = AR.take(512), AR.take(128)
        small_dma(sw.ap.rearrange("p (g d) -> p g d", g=4), din(f"sgw{j}"), R(sw))
        small_dma(tr.ap, din("trilh"), R(tr))
        tt("dve", wT[:, j], sw.ap.rearrange("p (g d) -> p g d", g=4),
           tr.ap.unsqueeze(1).to_broadcast([128, 4, 128]), ALU.mult, R(sw, tr), [("wT", j)])
        sb2 = AR.take(512)
        small_dma(sb2.ap.rearrange("p (g d) -> p g d", g=4), din(f"sgb{j}"), R(sb2))
        ts("dve", bh[:, j].rearrange("p g d -> p (g d)"), sb2.ap, 0.5, None, ALU.mult, None, R(sb2),
           [("bh", j)])
        gg = AR.take(512)
        small_dma(gg.ap, din(f"sgng{j}"), R(gg))
        ts("dve", ghbc[:, j, :], gg.ap, 0.5, None, ALU.mult, None, R(gg), [("ghbc", j)])

    tab_state = {"j": None}

    dq_ = []

    def later(delay, fn):
        dq_.append([delay, fn])

    def tick():
        due = [x for x in dq_ if x[0] <= 0]
        for x in due:
            dq_.remove(x)
        for x in dq_:
            x[0] -= 1
        for x in due:
            x[1]()

    def flush():
        while dq_:
            tick()

    def sub(v, a, b):
        return V(v.ap[:, a:b], ("@ar", v.reg[1] + a, v.reg[1] + b))

    def xr_(kt):
        return V(xres[:, kt, :], ("xres", kt))

    def hb_(kt):
        return V(hb[:, kt, :], ("hb", kt))

    def rmsnorm(gap, gkey, outs, bf=True):
        AR.reset()
        sq = [AR.take(256, BF16) for _ in range(8)]
        ms4, rs4 = AR.take(4), AR.take(4)
        dg4 = AR.take(512)
        bk = nb("main8")
        for kt in range(8):
            act(sq[kt].ap, xres[:, kt, :], AF.Square, [("xres", kt)], R(sq[kt]))
        for tb_ in range(4):
            for kt in range(8):
                mm(banks[bk][:, tb_:tb_ + 1], sq[kt].ap[:, tb_ * 128:(tb_ + 1) * 128], ones[:, 0:1],
                   kt == 0, kt == 7, ["ones"] + R(sq[kt]), [("bank", bk)])
        ts("dve", ms4.ap, banks[bk][:, 0:4], 1.0 / D, EPS, ALU.mult, ALU.add, [("bank", bk)], R(ms4))
        tt("pool", rs4.ap, ms4.ap, cst[:, 0:1].to_broadcast([128, 4]), ALU.pow, R(ms4) + ["cst"], R(rs4))
        bk2 = nb("main8")
        tt("dve", dg4.ap.rearrange("p (a b) -> p a b", a=4), ident[:].unsqueeze(1).to_broadcast([128, 4, 128]),
           rs4.ap.unsqueeze(2).to_broadcast([128, 4, 128]), ALU.mult, R(rs4) + ["ident"], R(dg4))
        for tb_ in range(4):
            mm(banks[bk2][:, tb_ * 128:(tb_ + 1) * 128], onesf[:], dg4.ap[:, tb_ * 128:(tb_ + 1) * 128], True, True,
               ["onesf"] + R(dg4), [("bank", bk2)])
        for kt in range(8):
            stt(outs[kt].ap, xres[:, kt, :], gap[:, kt:kt + 1], banks[bk2][:, :], ALU.mult, ALU.mult,
                [("xres", kt), gkey, ("bank", bk2)], R(outs[kt]))

    def proj_chunk(Wv, col, pool, rhs_list, KT=8):
        bk = nb(pool)
        for kt in range(KT):
            mm(banks[bk][:, :], Wv.ap[:, kt, col:col + 128], rhs_list[kt].ap, kt == 0, kt == KT - 1,
               [Wv.reg] + R(rhs_list[kt]), [("bank", bk)])
        return bk

    def out_proj(ycat, pool):
        for half in range(2):
            Wo = next_slab()
            for m_ in range(4):
                m = half * 4 + m_
                bk = proj_chunk(Wo, m_ * 128, pool, ycat)
                tt("dve", xres[:, m, :], xres[:, m, :], banks[bk][:, :], ALU.add,
                   [("xres", m), ("bank", bk)], [("xres", m)])

    def gelu2(bk, tb, Gout):
        cp("act", tb["pc"].ap, banks[bk][:, :], [("bank", bk)], R(tb["pc"]))
        act(tb["p2"].ap, banks[bk][:, :], AF.Square, [("bank", bk)], R(tb["p2"]))
        act(tb["ti"].ap, tb["p2"].ap, AF.Identity, R(tb["p2"]) + ["cst"], R(tb["ti"]), bias=cst[:, 3:4], scale=0.044715)
        tt("dve", tb["ti"].ap, tb["ti"].ap, tb["pc"].ap, ALU.mult, R(tb["ti"], tb["pc"]), R(tb["ti"]))
        act(tb["th"].ap, tb["ti"].ap, AF.Tanh, R(tb["ti"]), R(tb["th"]), scale=0.7978845608028654)
        stt(Gout.ap, tb["th"].ap, 1.0, tb["pc"].ap, ALU.add, ALU.mult, R(tb["th"], tb["pc"]), R(Gout))

    def even_mixer(i, t):
        j = i // 2
        hbl = [hb_(kt) for kt in range(8)]
        AR.reset()
        xa = [AR.take(512) for _ in range(4)]
        cxb = [AR.take(514) for _ in range(4)]
        ycat = [AR.take(256, BF16) for _ in range(8)]
        u = [AR.take(512) for _ in range(4)]
        ub = [AR.take(256, BF16) for _ in range(4)]
        HT = TS // 2
        ssmb = [dict(bu=AR.take(2 * HT), m34=AR.take(2 * HT), q=AR.take(2 * HT),
                     X=AR.take(HT + 1, BF16), Y=AR.take(HT + 1, BF16)) for _ in range(3)]
        for S_ in ssmb:
            P.op("dve", lambda e, S_=S_: e.memset(S_["Y"].ap, 0.0), (), R(S_["Y"]))
        tmpb = [dict(yq=AR.take(256), y2=AR.take(256), ti=AR.take(256), th=AR.take(256)) for _ in range(2)]
        rt = [AR.take(16) for _ in range(4)]
        G = xa
        base = cxb[0].reg[1]
        gelu_bf = [V(arena[:, base + 256 * b:base + 256 * (b + 1)].bitcast(BF16),
                     ("@ar", base + 256 * b, base + 256 * (b + 1))) for b in range(4)]
        th2 = [V(arena[:, base + 1024 + 512 * x:base + 1024 + 512 * (x + 1)],
                 ("@ar", base + 1024 + 512 * x, base + 1024 + 512 * (x + 1))) for x in range(2)]
        if tab_state["j"] != j:
            jj = even_js.index(j)
            P.dma("sp", tab[:].rearrange("p c k t -> p (c k t)"), tabd[jj], "tabld",
                  reads=[("tabd", jj)], writes=["tab"])
            tab_state["j"] = j
        for sl in range(2):
            W = next_slab()
            for ii in range(2):
                c = 2 * sl + ii
                bx = proj_chunk(W, ii * 128, "main4", hbl)
                cp("act", xa[c].ap, banks[bx][:, :], [("bank", bx)], R(xa[c]))
                bc_ = proj_chunk(W, 256 + ii * 128, "main4", hbl)
                cxh, cxm = sub(cxb[c], 0, 2), sub(cxb[c], 2, 514)
                cp("pool", cxh.ap, cxhist[:, j, c, :], [("cxhist", j, c)], R(cxh))
                tt("dve", cxm.ap, banks[bc_][:, :], xa[c].ap, ALU.mult, [("bank", bc_)] + R(xa[c]), R(cxm))
                ek = ("econv", j)
                act(xa[c].ap, cxm.ap, AF.Identity, R(cxm) + [ek], R(xa[c]), scale=econv[:, j, c, 2:3])
                stt(xa[c].ap, cxb[c].ap[:, 1:513], econv[:, j, c, 1:2], xa[c].ap, ALU.mult, ALU.add,
                    R(cxb[c], xa[c]) + [ek], R(xa[c]))
                stt(xa[c].ap, cxb[c].ap[:, 0:512], econv[:, j, c, 0:1], xa[c].ap, ALU.mult, ALU.add,
                    R(cxb[c], xa[c]) + [ek], R(xa[c]))
                cp("pool", cxhist[:, j, c, :], cxb[c].ap[:, 512:514], R(cxm), [("cxhist", j, c)])
        W = next_slab()
        for b in range(4):
            bk = proj_chunk(W, b * 128, "main4", hbl)
            cp("act", u[b].ap, banks[bk][:, :], [("bank", bk)], R(u[b]))
            cp("pool", ub[b].ap, u[b].ap, R(u[b]), R(ub[b]))
        W = next_slab()
        for c in range(4):
            bk = proj_chunk(W, c * 128, "main4", hbl)
            tt("dve", ycat[c].ap, banks[bk][:, :], xa[c].ap, ALU.mult, [("bank", bk)] + R(xa[c]), R(ycat[c]))
        items = [(sb_i, b, q) for sb_i in range(T // TS) for b in range(4) for q in range(4)]
        v3 = lambda v: v.ap.rearrange("p (c t) -> p c t", c=2)
        stA = {}

        def stageA(n):
            sb_i, b, q = items[n]
            c0 = sb_i * TS
            bb = nb("bu")
            for comp in range(2):
                o_ = banks[bb][:, comp * HT:(comp + 1) * HT]
                mm(o_, Bl[32 * q:32 * q + 32, j, 1, b, comp, :], ub[b].ap[32 * q:32 * q + 32, c0:c0 + TS:2],
                   True, False, [("Bl", j)] + R(ub[b]), [("bank", bb)], tp=(32 * q, 0))
                mm(o_, Bl[32 * q:32 * q + 32, j, 0, b, comp, :], ub[b].ap[32 * q:32 * q + 32, c0 + 1:c0 + TS:2],
                   False, True, [("Bl", j)] + R(ub[b]), [("bank", bb)], tp=(32 * q, 0))
            stA[n] = bb

        def stageB(n, yk):
            sb_i, b, q = items[n]
            c0 = sb_i * TS
            k = 4 * b + q
            S = ssmb[n % 3]
            v3 = lambda v: v.ap.rearrange("p (c t) -> p c t", c=2)
            cbc = tab[:, 0, k, 1:TS:2].unsqueeze(1).to_broadcast([128, 2, HT])
            sbc = tab[:, 1, k, 1:TS:2].unsqueeze(1).to_broadcast([128, 2, HT])
            bb = stA.pop(n)
            bk3 = banks[bb][:, 0:2 * HT].rearrange("p (c t) -> p c t", c=2)
            tt("dve", v3(S["m34"]), bk3, sbc, ALU.mult, [("bank", bb), "tab"], R(S["m34"]))
            tt("dve", v3(S["bu"]), bk3, cbc, ALU.mult, [("bank", bb), "tab"], R(S["bu"]))
            mre, mim = sub(S["bu"], 0, HT), sub(S["bu"], HT, 2 * HT)
            tt("dve", mre.ap, mre.ap, S["m34"].ap[:, HT:2 * HT], ALU.add, R(mre, S["m34"]), R(mre))
            tt("dve", mim.ap, mim.ap, S["m34"].ap[:, 0:HT], ALU.subtract, R(mim, S["m34"]), R(mim))
            rb = r2[:, j, k:k + 1].to_broadcast([128, HT])
            for comp, mv in ((0, mre), (1, mim)):
                qv = sub(S["q"], comp * HT, (comp + 1) * HT)
                P.op("dve", lambda e, qv=qv, rb=rb, mv=mv, comp=comp, k=k: e.tensor_tensor_scan(
                    qv.ap, rb, mv.ap, qinit[:, j, comp, k:k + 1], ALU.mult, ALU.add),
                    R(mv) + [("r2", j), ("qinit", j)], R(qv))
            X3 = S["X"].ap.rearrange("p (c t) -> p c t", c=2)
            Y3 = S["Y"].ap.rearrange("p (c t) -> p c t", c=2)
            cp("act", X3[:, :, 0:1], qinit[:, j, :, k:k + 1], [("qinit", j)] + R(S["X"]), R(S["X"]))
            tt("dve", X3[:, :, 1:HT + 1], v3(S["q"]), cbc, ALU.mult, R(S["q"], S["X"]) + ["tab"], R(S["X"]))
            tt("dve", Y3[:, :, 1:HT + 1], v3(S["q"]), sbc, ALU.mult, R(S["q"], S["Y"]) + ["tab"], R(S["Y"]))
            cp("act", qend[:, :, k:k + 1], v3(S["q"])[:, :, HT - 1:HT], R(S["q"]), [("qend", k)])
            W1 = HT + 1
            seg = lambda src, half, a_: src.ap[:, half * W1 + a_:half * W1 + a_ + HT]
            yo_odd = banks[yk][32 * q:32 * q + 32, 1:TS:2]
            yo_even = banks[yk][32 * q:32 * q + 32, 0:TS:2]
            ops_ = ((0, S["X"], 0), (1, S["Y"], 1), (2, S["Y"], 0), (2, S["X"], 1))
            for n_, (m_, src, half) in enumerate(ops_):
                mm(yo_odd, Cl[:, j, k, m_, :], seg(src, half, 1), n_ == 0, n_ == 3,
                   [("Cl", j)] + R(src), [("bank", yk)], tp=(0, 32 * q))
            mm(yo_even, K0l[:, j, k, :], ub[b].ap[:, c0:c0 + TS:2], True, False,
               [("K0l", j)] + R(ub[b]), [("bank", yk)], tp=(0, 32 * q))
            for n_, (m_, src, half) in enumerate(ops_):
                mm(yo_even, Cl[:, j, k, 3 + m_, :], seg(src, half, 0), False, n_ == 3,
                   [("Cl", j)] + R(src), [("bank", yk)], tp=(0, 32 * q))

        def block_epilogue(sb_i, b, yk, ib):
            c0 = sb_i * TS
            tb = tmpb[ib % 2]
            Gs = sub(G[b], c0, c0 + TS)
            gb = V(gelu_bf[b].ap[:, c0:c0 + TS], ("@ar", gelu_bf[b].reg[1] + c0 // 2,
                                                 gelu_bf[b].reg[1] + (c0 + TS) // 2))

            def e1():
                stt(tb["yq"].ap, u[b].ap[:, c0:c0 + TS], dq[:, j, b:b + 1], banks[yk][:, 0:TS], ALU.mult, ALU.add,
                    R(u[b]) + [("dq", j), ("bank", yk)], R(tb["yq"]))
                act(tb["y2"].ap, tb["yq"].ap, AF.Square, R(tb["yq"]), R(tb["y2"]), scale=4.0)

            def e2():
                act(tb["ti"].ap, tb["y2"].ap, AF.Identity, R(tb["y2"]) + ["cst"], R(tb["ti"]), bias=cst[:, 2:3],
                    scale=4 * 0.044715)
                tt("dve", tb["ti"].ap, tb["ti"].ap, tb["yq"].ap, ALU.mult, R(tb["ti"], tb["yq"]), R(tb["ti"]))

            def e3():
                act(tb["th"].ap, tb["ti"].ap, AF.Tanh, R(tb["ti"]), R(tb["th"]), scale=0.7978845608028654)

            def e4():
                stt(Gs.ap, tb["th"].ap, 1.0, tb["yq"].ap, ALU.add, ALU.mult, R(tb["th"], tb["yq"]), R(Gs))
                act(gb.ap, Gs.ap, AF.Copy, R(Gs), R(gb), scale=2.0)

            later(0, e1)
            later(1, e2)
            later(2, e3)
            later(3, e4)

        def carry():
            qk = [("qend", k) for k in range(16)]
            cT, sT = rotc[:, j, 0, :], rotc[:, j, 1, :]
            rk = [("rotc", j)]
            tt("dve", rt[0].ap, cT, qend[:, 0, :], ALU.mult, qk + rk, R(rt[0]))
            tt("dve", rt[1].ap, sT, qend[:, 1, :], ALU.mult, qk + rk, R(rt[1]))
            tt("dve", rt[2].ap, sT, qend[:, 0, :], ALU.mult, qk + rk, R(rt[2]))
            tt("dve", rt[3].ap, cT, qend[:, 1, :], ALU.mult, qk + rk, R(rt[3]))
            tt("dve", qinit[:, j, 0, :], rt[0].ap, rt[1].ap, ALU.subtract, R(rt[0], rt[1]), [("qinit", j)])
            tt("dve", qinit[:, j, 1, :], rt[2].ap, rt[3].ap, ALU.add, R(rt[2], rt[3]), [("qinit", j)])

        NI = len(items)
        stageA(0)
        stageA(1)
        yk = None
        ib = 0
        for n in range(NI):
            sb_i, b, q = items[n]
            if n + 2 < NI:
                stageA(n + 2)
            if q == 0:
                yk = nb("yb")
            stageB(n, yk)
            tick()
            if q == 3:
                block_epilogue(sb_i, b, yk, ib)
                ib += 1
                if b == 3:
                    carry()
        flush()
        Wg = next_slab()
        for c in range(4):
            bk = proj_chunk(Wg, c * 128, "main4", gelu_bf, KT=4)
            tv = th2[c % 2]
            act(tv.ap, banks[bk][:, :], AF.Tanh, [("bank", bk), ("gbh", j)], R(tv), bias=gbh[:, j, c:c + 1], scale=0.5)
            stt(ycat[4 + c].ap, tv.ap, 1.0, G[c].ap, ALU.add, ALU.mult, R(tv, G[c]), R(ycat[4 + c]))
        out_proj(ycat, "main4")

    def ffn(i, t):
        hbl = [hb_(kt) for kt in range(8)]
        AR.reset(3072)
        hid = [AR.take(256, BF16) for _ in range(22)]
        acc = [AR.take(512) for _ in range(8)]
        sg = [AR.take(512) for _ in range(4)]
        for k in range(11):
            W = next_slab()
            for jj in range(2):
                jch = 2 * k + jj
                chs = [jch, jch + 22]
                bks = [proj_chunk(W, gv * 256 + jj * 128, "main8", hbl) for gv in range(2)]
                accs = [acc[2 * (jch % 4) + gv] for gv in range(2)]
                wk = [("fcw", i)]
                w_ = lambda ch, kk: fcw[:, i, ch, kk:kk + 1]
                for gv in range(2):
                    act(accs[gv].ap, banks[bks[gv]][:, :], AF.Identity, [("bank", bks[gv]), ("fcb", i)] + wk,
                        R(accs[gv]), bias=fcb[:, i, chs[gv]:chs[gv] + 1], scale=w_(chs[gv], 2))
                for gv in range(2):
                    a, bk, ch = accs[gv], bks[gv], chs[gv]
                    stt(a.ap[:, 1:T], banks[bk][:, 0:T - 1], w_(ch, 1), a.ap[:, 1:T], ALU.mult, ALU.add,
                        [("bank", bk)] + wk + R(a), R(a))
                for gv in range(2):
                    a, bk, ch = accs[gv], bks[gv], chs[gv]
                    stt(a.ap[:, 0:1], fhist[:, i, ch, 1:2], w_(ch, 1), a.ap[:, 0:1], ALU.mult, ALU.add,
                        [("fhist", i, ch)] + wk + R(a), R(a))
                for gv in range(2):
                    a, bk, ch = accs[gv], bks[gv], chs[gv]
                    stt(a.ap[:, 2:T], banks[bk][:, 0:T - 2], w_(ch, 0), a.ap[:, 2:T], ALU.mult, ALU.add,
                        [("bank", bk)] + wk + R(a), R(a))
                for gv in range(2):
                    a, bk, ch = accs[gv], bks[gv], chs[gv]
                    stt(a.ap[:, 0:2], fhist[:, i, ch, 0:2], w_(ch, 0), a.ap[:, 0:2], ALU.mult, ALU.add,
                        [("fhist", i, ch)] + wk + R(a), R(a))
                def fin(chs=chs, bks=bks, accs=accs, s_=sg[jch % 4], jch=jch):
                    for gv in range(2):
                        cp("act", fhist[:, i, chs[gv], :], banks[bks[gv]][:, T - 2:T], [("bank", bks[gv])],
                           [("fhist", i, chs[gv])])
                    act(s_.ap, accs[0].ap, AF.Silu, R(accs[0]), R(s_))
                    tt("pool", hid[jch].ap, s_.ap, accs[1].ap, ALU.mult, R(s_, accs[1]), R(hid[jch]))

                tick()
                later(0, fin)
        flush()
        for m in range(8):
            Wd = next_slab()
            bk = nb("main8")
            for jch in range(22):
                mm(banks[bk][:, :], Wd.ap[:, jch, :], hid[jch].ap, jch == 0, jch == 21,
                   [Wd.reg] + R(hid[jch]), [("bank", bk)])
            tt("dve", xres[:, m, :], xres[:, m, :], banks[bk][:, :], ALU.add,
               [("xres", m), ("bank", bk)], [("xres", m)])

    def odd_mixer(i, t):
        j = i // 2
        hbl = [hb_(kt) for kt in range(8)]
        AR.reset()
        zt = [AR.take(528) for _ in range(4)]
        sA, sB = AR.take(528), AR.take(528)
        pooled = [AR.take(256, BF16) for _ in range(4)]
        Gu = [AR.take(512) for _ in range(4)]
        tmpb = [dict(pc=AR.take(512), p2=AR.take(512), ti=AR.take(512), th=AR.take(512)) for _ in range(2)]
        Gv = [AR.take(512) for _ in range(2)]
        vtok = [AR.take(256, BF16) for _ in range(4)]
        ycat = [AR.take(256, BF16) for _ in range(8)]
        sgt = [AR.take(512) for _ in range(2)]
        sml = [dict(ss=AR.take(1), ms=AR.take(1), rs=AR.take(1)) for _ in range(2)]
        t15 = AR.take(16)
        W = next_slab()
        for g, w in enumerate((2, 4, 8, 16)):
            bk = proj_chunk(W, g * 128, "main8", hbl)
            zh, zm = sub(zt[g], 0, 16), sub(zt[g], 16, 528)
            cp("pool", zh.ap, zhist[:, j, g, :], [("zhist", j, g)], R(zh))
            cp("act", zm.ap, banks[bk][:, :], [("bank", bk)], R(zm))
            z = zt[g]
            tt("dve", sA.ap[:, 1:528], z.ap[:, 1:528], z.ap[:, 0:527], ALU.add, R(z), R(sA))
            Sv = sA
            if g >= 1:
                tt("dve", sB.ap[:, 3:528], sA.ap[:, 3:528], sA.ap[:, 1:526], ALU.add, R(sA), R(sB))
                Sv = sB
            if g >= 2:
                tt("dve", sA.ap[:, 7:528], sB.ap[:, 7:528], sB.ap[:, 3:524], ALU.add, R(sB), R(sA))
                Sv = sA
            if g >= 3:
                tt("dve", sB.ap[:, 15:528], sA.ap[:, 15:528], sA.ap[:, 7:520], ALU.add, R(sA), R(sB))
                Sv = sB
            stt(pooled[g].ap, Sv.ap[:, 16:528], 1.0 / w, zm.ap, ALU.mult, ALU.subtract, R(Sv, zm), R(pooled[g]))
            if t == 0:
                tt("dve", t15.ap[:, 0:15], Sv.ap[:, 16:31], invc[:, g, 0:15], ALU.mult, R(Sv) + ["invc"], R(t15))
                tt("dve", pooled[g].ap[:, 0:15], t15.ap[:, 0:15], zt[g].ap[:, 16:31], ALU.subtract,
                   R(t15, zm, pooled[g]), R(pooled[g]))
            cp("pool", zhist[:, j, g, :], zt[g].ap[:, 512:528], R(zm), [("zhist", j, g)])
            bk2 = nb("main8")
            mm(banks[bk2][:, :], poolw[:, j, g, :], pooled[g].ap, True, True, [("poolw", j)] + R(pooled[g]),
               [("bank", bk2)])
            act(ycat[g].ap, banks[bk2][:, :], AF.Identity, [("bank", bk2), ("pools", j)], R(ycat[g]),
                scale=pools[:, j, g:g + 1])
        W = next_slab()
        for c in range(4):
            bk = proj_chunk(W, c * 128, "main8", hbl)
            gelu2(bk, tmpb[c % 2], Gu[c])
        W = next_slab()
        for tb_ in range(4):
            bk = nb("main8")
            for kt in range(8):
                mm(banks[bk][:, :], hb[:, kt, tb_ * 128:(tb_ + 1) * 128], W.ap[:, kt, :], kt == 0, kt == 7,
                   [W.reg, ("hb", kt)], [("bank", bk)])
            tb = tmpb[tb_ % 2]
            gv = Gv[tb_ % 2]
            sm_ = sml[tb_ % 2]
            gelu2(bk, tb, gv)
            act(tb["p2"].ap, gv.ap, AF.Square, R(gv), R(tb["p2"], sm_["ss"]), scale=0.5, accum=sm_["ss"].ap)
            ts("dve", sm_["ms"].ap, sm_["ss"].ap, 1.0 / 512, EPS, ALU.mult, ALU.add, R(sm_["ss"]), R(sm_["ms"]))
            tt("pool", sm_["rs"].ap, sm_["ms"].ap, cst[:, 0:1], ALU.pow, R(sm_["ms"]) + ["cst"], R(sm_["rs"]))
            stt(vtok[tb_].ap, gv.ap, sm_["rs"].ap, ghbc[:, j, :], ALU.mult, ALU.mult,
                R(gv, sm_["rs"]) + [("ghbc", j)], R(vtok[tb_]))
        for hd in range(4):
            bk = nb("main8")
            for tb_ in range(4):
                mm(banks[bk][:, tb_ * 128:(tb_ + 1) * 128], vtok[tb_].ap[:, hd * 128:(hd + 1) * 128],
                   wT[:, j, hd, :], True, True, [("wT", j)] + R(vtok[tb_]), [("bank", bk)])
            sv_ = sgt[hd % 2]
            tt("dve", sv_.ap.rearrange("p (a b) -> p a b", a=4),
               banks[bk][:, :].rearrange("p (a b) -> p a b", a=4),
               bh[:, j, hd, :].unsqueeze(1).to_broadcast([128, 4, 128]), ALU.add,
               [("bank", bk), ("bh", j)], R(sv_))
            tt("dve", ycat[4 + hd].ap, sv_.ap, Gu[hd].ap, ALU.mult, R(sv_, Gu[hd]), R(ycat[4 + hd]))
        out_proj(ycat, "main8")

    for t in range(ntiles):
        t0 = t * T
        P.dma("sp", xres[:, :, :], xTv[:, :, t0:t0 + T], "xload", writes=[("xres", kt) for kt in range(8)])
        for i in layers:
            j = i // 2
            if t == 0:
                if i % 2 == 0:
                    ssm_setup(j, even_js.index(j))
                    tab_state["j"] = j
                else:
                    odd_setup(j)
            rmsnorm(gmix[:, i, :], "gmix", [hb_(kt) for kt in range(8)])
            if i % 2 == 0:
                even_mixer(i, t)
            else:
                odd_mixer(i, t)
            rmsnorm(gffn[:, i, :], "gffn", [hb_(kt) for kt in range(8)])
            ffn(i, t)
        if final:
            AR.reset(3072)
            ost = [AR.take(512) for _ in range(8)]
            p_after = AR.p
            rmsnorm(gfin[:, :], "gfin", ost)
            AR.reset(p_after)
            ov = arena[:, ost[0].reg[1]:ost[0].reg[1] + 4096].rearrange("p (k t) -> p k t", k=8)
            P.dma("act", outTv[:, :, t0:t0 + T], ov, "ostore", reads=R(*ost), writes=[("outT", t)])
        else:
            P.dma("act", outTv[:, :, t0:t0 + T], xres[:, :, :], "ostore",
                  reads=[("xres", kt) for kt in range(8)], writes=[("outT", t)])
    P.wait("act", [("outT", t) for t in range(ntiles)])
    P.emit()
    P.close()
    names = ["xT"] + list(dr.keys())
    return nc, names, dbg_out


_CACHE = {}


def _get(layers, ntiles, final):
    key = (tuple(layers), ntiles, final)
    if key not in _CACHE:
        _CACHE[key] = build(layers, ntiles, final)
    return _CACHE[key]


def kernel(**inputs):
    com = prep_common(inputs)
    x = np.asarray(inputs["x"], dtype=np.float32)
    nb_ = x.shape[0]
    xTs = [np.ascontiguousarray(x[b].T) for b in range(nb_)]
    nc, names, _ = _get((0, 1, 2, 3), SEQ // T, True)
    in_maps = []
    for b in range(nb_):
        m = {"xT": xTs[b]}
        for n in names:
            if n != "xT":
                m[n] = com[n]
        in_maps.append(m)
    res = run_bass_kernel_spmd(nc, in_maps, core_ids=list(range(nb_)))
    out = np.stack([np.ascontiguousarray(res.results[b]["outT"].T) for b in range(nb_)], axis=0)
    return out.astype(np.float32)
```
